# Optimizing a Trainium2 kernel written in Bass

```python
import math
import jax, jax.numpy as jnp
from jax import lax
import numpy as np

D_MODEL = 1024
BATCH = 8
SEQ = 2048
DEPTH = 2
DEC_BATCH = 128
DEC_SEQ = 1
PAST_LEN = 16384
PAGE_SIZE = 128

HEAD_A = 64
N_HEADS_A = 8
D_A = N_HEADS_A * HEAD_A
D_DECAY_LORA = 64
D_ICLR_LORA = 64
D_VRES_LORA = 32
GN_EPS = 64e-5
CHUNK = 128
N_GROUPS_B = 4
D_B = D_MODEL // 2
GROUP_B = D_B // N_GROUPS_B
D_PLE = 256
EPS = 1e-6
N_SHIFT = 3 * D_A + D_DECAY_LORA + D_ICLR_LORA
D_IN = N_SHIFT + D_A + 2 * D_B + D_B + 2 * D_MODEL

kernel_name = "rwkv7_gmlp_gated_hybrid_step"


def _rms_norm(x, g):
    xf = x.astype(jnp.float32)
    y = xf * lax.rsqrt(jnp.mean(xf * xf, axis=-1, keepdims=True) + EPS)
    return (y * g.astype(jnp.float32)).astype(x.dtype)


def _layer_norm(x, g, b):
    xf = x.astype(jnp.float32)
    mu = jnp.mean(xf, axis=-1, keepdims=True)
    var = jnp.mean((xf - mu) ** 2, axis=-1, keepdims=True)
    y = (xf - mu) * lax.rsqrt(var + 1e-5)
    return (y * g.astype(jnp.float32) + b.astype(jnp.float32)).astype(x.dtype)


def _wkv_step(S, inp):
    r, w, k, v, aa, bb = inp
    sa = jnp.einsum('bhij,bhj->bhi', S, aa)
    S = S * w[:, :, None, :] + sa[..., :, None] * bb[..., None, :] + v[..., :, None] * k[..., None, :]
    y = jnp.einsum('bhij,bhj->bhi', S, r)
    return S, y


def _rwkv7(z, shift_prev, xn, v_first, wkv0, mu, w0, w_up, a0, a_up, k_k, k_a, r_k, gn_g, gn_b, vres):
    B, T, _ = z.shape
    z_prev = jnp.concatenate([shift_prev[:, None, :].astype(z.dtype), z[:, :-1]], axis=1)
    zs = z + (z_prev - z) * mu
    r = zs[..., :D_A]
    k = zs[..., D_A:2 * D_A]
    v = zs[..., 2 * D_A:3 * D_A]
    wd = zs[..., 3 * D_A:3 * D_A + D_DECAY_LORA]
    ad = zs[..., 3 * D_A + D_DECAY_LORA:]
    w = -jax.nn.softplus(-(w0 + jnp.tanh(wd) @ w_up)) - 0.5
    decay = jnp.exp(-jnp.exp(w.astype(jnp.float32)))
    a = jax.nn.sigmoid(a0 + ad @ a_up)
    if vres is not None:
        vd, vu, vb = vres
        v = v + (v_first - v) * jax.nn.sigmoid(vb + (xn @ vd) @ vu)
    heads = lambda t: t.reshape(B, T, N_HEADS_A, HEAD_A).astype(jnp.float32)
    kk = heads(k * k_k)
    kk = kk / jnp.maximum(jnp.sqrt(jnp.sum(kk * kk, axis=-1, keepdims=True)), 1e-12)
    k = k * (1.0 + (a - 1.0) * k_a)
    rh, kh, vh, ah = heads(r), heads(k), heads(v), heads(a)
    wh = decay.reshape(B, T, N_HEADS_A, HEAD_A)
    aa = -kk
    bb = kk * ah
    tm = lambda t: jnp.moveaxis(t, 1, 0)
    S, y = lax.scan(_wkv_step, wkv0.astype(jnp.float32),
                    (tm(rh), tm(wh), tm(kh), tm(vh), tm(aa), tm(bb)))
    y = jnp.moveaxis(y, 0, 1)
    m = jnp.mean(y, axis=-1, keepdims=True)
    var = jnp.mean((y - m) ** 2, axis=-1, keepdims=True)
    y = ((y - m) * lax.rsqrt(var + GN_EPS)).reshape(B, T, D_A)
    y = y * gn_g.astype(jnp.float32) + gn_b.astype(jnp.float32)
    bonus = jnp.sum(rh * kh * r_k.astype(jnp.float32), axis=-1, keepdims=True) * vh
    y = y + bonus.reshape(B, T, D_A)
    return y.astype(z.dtype), S, z[:, -1], v


def _chunk_spatial(v_n, w_s, b_s):
    B, T, _ = v_n.shape
    n_chunks = -(-T // CHUNK)
    pad = n_chunks * CHUNK - T
    vp = jnp.pad(v_n, ((0, 0), (0, pad), (0, 0))).reshape(B, n_chunks, CHUNK, N_GROUPS_B, GROUP_B)
    mask = jnp.tril(jnp.ones((CHUNK, CHUNK), dtype=bool))
    wm = jnp.where(mask[None], w_s, jnp.zeros_like(w_s))
    mixed = jnp.einsum('gts,bnsgc->bntgc', wm, vp) + jnp.swapaxes(b_s, 0, 1)[None, None, :, :, None]
    return mixed.reshape(B, n_chunks * CHUNK, D_B)[:, :T]


def _forward(x, p, wkv0, shift0, W):
    h = x
    v_first = None
    wkvs, shifts, chunk_vs = [], [], []
    for l in range(DEPTH):
        xn = _rms_norm(h, W['norm_g'][l])
        zall = jnp.einsum('btd,de->bte', xn, W['w_in'][l])
        o = N_SHIFT
        z_shift = zall[..., :o]
        g_a = zall[..., o:o + D_A]; o += D_A
        u = jax.nn.gelu(zall[..., o:o + D_B]); o += D_B
        vb_ = jax.nn.gelu(zall[..., o:o + D_B]); o += D_B
        g_b = zall[..., o:o + D_B]; o += D_B
        m_a = zall[..., o:o + D_MODEL]; o += D_MODEL
        m_b = zall[..., o:o + D_MODEL]
        vres = None if l == 0 else (W['vres_down'][l - 1], W['vres_up'][l - 1], W['vres_b'][l - 1])
        y_a, S, last, v = _rwkv7(z_shift, shift0[l], xn, v_first, wkv0[l], W['shift_mu'][l],
                                 W['w0'][l], W['w_up'][l], W['a0'][l], W['a_up'][l],
                                 W['k_k'][l], W['k_a'][l], W['r_k'][l], W['gn_g'][l], W['gn_b'][l], vres)
        if l == 0:
            v_first = v
        wkvs.append(S.astype(wkv0.dtype))
        shifts.append(last.astype(shift0.dtype))
        v_n = _layer_norm(vb_, W['ln_v_g'][l], W['ln_v_b'][l])
        chunk_vs.append(v_n)
        y_b = u * _chunk_spatial(v_n, W['w_spatial'][l], W['b_spatial'][l])
        br_a = (y_a * jax.nn.silu(g_a)) @ W['w_br_a'][l]
        br_b = (y_b * jax.nn.silu(g_b)) @ W['w_br_b'][l]
        merged = jax.nn.sigmoid(m_a) * br_a + jax.nn.sigmoid(m_b) * br_b
        h = h + merged @ W['w_out'][l]
        ple = p[l] @ W['w_ple'][l]
        h = h + jax.nn.sigmoid(h @ W['w_ple_gate'][l] + W['b_ple_gate'][l]) * ple
    y = _rms_norm(h, W['final_g'])
    return y, jnp.stack(wkvs), jnp.stack(shifts), jnp.stack(chunk_vs)


def setup_inputs(seed: int = 0) -> dict:
    key = jax.random.key(seed)
    ks = iter(jax.random.split(key, 64))
    f32 = jnp.float32
    nrm = lambda shape, s: jax.random.normal(next(ks), shape, f32) * s
    L = DEPTH
    return {
        'x_prompt': nrm((BATCH, SEQ, D_MODEL), 1.0),
        'x_sample': nrm((DEC_BATCH, DEC_SEQ, D_MODEL), 1.0),
        'state_rwkv_wkv': nrm((L, DEC_BATCH, N_HEADS_A, HEAD_A, HEAD_A), 0.5),
        'state_rwkv_shift': nrm((L, DEC_BATCH, N_SHIFT), 1.0),
        'p_prompt': nrm((L, BATCH, SEQ, D_PLE), 1.0),
        'p_sample': nrm((L, DEC_BATCH, DEC_SEQ, D_PLE), 1.0),
        'norm_g': 1.0 + nrm((L, D_MODEL), 0.05),
        'w_in': nrm((L, D_MODEL, D_IN), D_MODEL ** -0.5),
        'shift_mu': jax.random.uniform(next(ks), (L, N_SHIFT), f32),
        'w0': nrm((L, D_A), 0.5),
        'w_up': nrm((L, D_DECAY_LORA, D_A), 0.5 * D_DECAY_LORA ** -0.5),
        'a0': nrm((L, D_A), 0.5),
        'a_up': nrm((L, D_ICLR_LORA, D_A), 0.5 * D_ICLR_LORA ** -0.5),
        'vres_down': nrm((L - 1, D_MODEL, D_VRES_LORA), D_MODEL ** -0.5),
        'vres_up': nrm((L - 1, D_VRES_LORA, D_A), 0.5 * D_VRES_LORA ** -0.5),
        'vres_b': nrm((L - 1, D_A), 0.5),
        'k_k': 0.85 + nrm((L, D_A), 0.05),
        'k_a': 1.0 + nrm((L, D_A), 0.05),
        'r_k': nrm((L, N_HEADS_A, HEAD_A), 0.1),
        'gn_g': 1.0 + nrm((L, D_A), 0.05),
        'gn_b': nrm((L, D_A), 0.01),
        'ln_v_g': 1.0 + nrm((L, D_B), 0.05),
        'ln_v_b': nrm((L, D_B), 0.01),
        'w_spatial': nrm((L, N_GROUPS_B, CHUNK, CHUNK), CHUNK ** -0.5),
        'b_spatial': 1.0 + nrm((L, N_GROUPS_B, CHUNK), 0.1),
        'w_br_a': nrm((L, D_A, D_MODEL), D_A ** -0.5),
        'w_br_b': nrm((L, D_B, D_MODEL), D_B ** -0.5),
        'w_out': nrm((L, D_MODEL, D_MODEL), 0.5 * D_MODEL ** -0.5),
        'w_ple': nrm((L, D_PLE, D_MODEL), 0.5 * D_PLE ** -0.5),
        'w_ple_gate': nrm((L, D_MODEL, D_MODEL), D_MODEL ** -0.5),
        'b_ple_gate': nrm((L, D_MODEL), 0.01),
        'final_g': 1.0 + nrm((D_MODEL,), 0.05),
    }


def reference(x_prompt, x_sample, state_rwkv_wkv, state_rwkv_shift, p_prompt, p_sample,
              norm_g, w_in, shift_mu, w0, w_up, a0, a_up, vres_down, vres_up, vres_b,
              k_k, k_a, r_k, gn_g, gn_b, ln_v_g, ln_v_b, w_spatial, b_spatial,
              w_br_a, w_br_b, w_out, w_ple, w_ple_gate, b_ple_gate, final_g):
    W = dict(norm_g=norm_g, w_in=w_in, shift_mu=shift_mu, w0=w0, w_up=w_up, a0=a0, a_up=a_up,
             vres_down=vres_down, vres_up=vres_up, vres_b=vres_b, k_k=k_k, k_a=k_a, r_k=r_k,
             gn_g=gn_g, gn_b=gn_b, ln_v_g=ln_v_g, ln_v_b=ln_v_b, w_spatial=w_spatial,
             b_spatial=b_spatial, w_br_a=w_br_a, w_br_b=w_br_b, w_out=w_out, w_ple=w_ple,
             w_ple_gate=w_ple_gate, b_ple_gate=b_ple_gate, final_g=final_g)
    wkv0_p = jnp.zeros((DEPTH, x_prompt.shape[0], N_HEADS_A, HEAD_A, HEAD_A), state_rwkv_wkv.dtype)
    shift0_p = jnp.zeros((DEPTH, x_prompt.shape[0], N_SHIFT), state_rwkv_shift.dtype)
    y_prompt, wkv_prompt, shift_prompt, _ = _forward(x_prompt, p_prompt, wkv0_p, shift0_p, W)
    y_sample, wkv_sample, shift_sample, chunk_v_sample = _forward(
        x_sample, p_sample, state_rwkv_wkv, state_rwkv_shift, W)
    return (y_prompt, y_sample, wkv_prompt, shift_prompt, wkv_sample, shift_sample, chunk_v_sample)
```

```python
import sys
import os
import numpy as np
from contextlib import ExitStack
import concourse.bass as bass
import concourse.mybir as mybir
from concourse.bass_utils import run_bass_kernel_spmd

F32 = mybir.dt.float32
BF16 = mybir.dt.bfloat16
AF = mybir.ActivationFunctionType
ALU = mybir.AluOpType
AX = mybir.AxisListType

D = 1024
DA = 512
NSHIFT = 1664
DIN = 5760
DPLE = 256
L = 2
EPS = 1e-6
GN_EPS = 64e-5
C = 64
LOGW_C = -0.3032653298563167


class Buf:
    def __init__(s, name):
        s.name = name
        s.w = None
        s.r = {}


class T:
    def __init__(s, ap, buf):
        s.ap = ap
        s.buf = buf

    def __getitem__(s, k):
        return T(s.ap[k], s.buf)

    def v(s, f):
        return T(f(s.ap), s.buf)


class Op:
    pass


class Prog:
    def __init__(s, nc):
        s.nc = nc
        s.ops = []
        s.es = ExitStack()
        s.eng = {'pe': nc.tensor, 'act': nc.scalar, 'dve': nc.vector, 'pool': nc.gpsimd, 'sp': nc.sync}
        s.nbuf = 0
        s.rr = {}
        s.side = None
        s.side_kind = None
        s.auto = None
        s._in_auto = False
        s.auto_acc = 0.0
        s.auto_ratio = 0.0
        s.s_done = set()

    def sbuf(s, name, shape, dt, nsplit=None):
        t = s.es.enter_context(s.nc.sbuf_tensor(name, shape, dt))
        return T(t[:], Buf(name))

    def psum(s, name, shape, dt):
        t = s.es.enter_context(s.nc.psum_tensor(name, shape, dt))
        return T(t[:], Buf(name))

    def sub(s, t, name=None):
        s.nbuf += 1
        return T(t.ap, Buf(name or ("b%d" % s.nbuf)))

    def add(s, eng, fn, reads, writes, dma=None, force_main=False):
        op = Op()
        op.eng = eng
        op.fn = fn
        op.dma = dma
        op.sig = False
        op.cnt = 0
        op.reads = reads
        op.writes = writes
        fr = sys._getframe(1)
        wh = []
        while fr is not None and len(wh) < 4:
            wh.append(fr.f_lineno)
            fr = fr.f_back
        op.where = wh
        if s.side is not None and not force_main:
            s.side.append(op)
        else:
            s._append_main(op)
        return op

    def begin_side(s, kind='B'):
        s.side = []
        s.side_kind = kind

    def end_side(s):
        lst = s.side
        s.side = None
        s.side_kind = None
        return lst

    def marker(s, m):
        assert s.side is not None
        s.side.append(m)

    def _append_main(s, op):
        s.ops.append(op)
        if s.auto is not None and not s._in_auto:
            s._in_auto = True
            s.auto_acc += s.auto_ratio
            while s.auto_acc >= 1.0 and s.auto:
                s.auto_acc -= 1.0
                if not s.pull_s(1):
                    break
            s._in_auto = False

    def pull_s(s, k, until_done=None):
        lst = s.auto
        moved = 0
        while lst and (moved < k or (until_done and not until_done <= s.s_done)):
            it = lst[0]
            if isinstance(it, tuple):
                if it[0] == 'get':
                    if s.main_cur() <= it[1]:
                        assert not until_done, "S stream would wait for a slice the main stream has not acquired"
                        return False
                    lst.pop(0)
                else:
                    lst.pop(0)
                    s.s_done |= set(it[1])
                continue
            lst.pop(0)
            if isinstance(it, Op):
                s.ops.append(it)
                moved += 1
            else:
                it()
        return True

    def action(s, fn):
        if s.side is not None:
            s.side.append(fn)
        else:
            fn()

    def pull(s, lst, k):
        while k > 0 and lst:
            it = lst.pop(0)
            if isinstance(it, Op):
                s._append_main(it)
                k -= 1
            else:
                it()

    def analyse(s):
        last_dma = {}
        for idx, op in enumerate(s.ops):
            op.idx = idx
            deps = {}

            def dep(o):
                if o is None:
                    return
                key = o.dma if o.dma else o.eng
                if key not in deps or deps[key].idx < o.idx:
                    deps[key] = o

            def bl(ts_):
                o = []
                for t in ts_:
                    if isinstance(t.buf, (list, tuple)):
                        o.extend(t.buf)
                    else:
                        o.append(t.buf)
                return o
            rb = bl(op.reads)
            wb = bl(op.writes)
            if op.dma:
                dep(last_dma.get(op.dma))
                last_dma[op.dma] = op
            for b in rb:
                dep(b.w)
            for b in wb:
                dep(b.w)
                for o in b.r.values():
                    dep(o)
            mykey = op.dma if op.dma else op.eng
            for b in rb:
                b.r[mykey] = op
            for b in wb:
                b.w = op
                b.r = {}
            op.deps = list(deps.values())

    def emit(s):
        nc = s.nc
        import os
        if os.environ.get('K_STOP'):
            s.ops = s.ops[:int(os.environ['K_STOP'])]
        s.analyse()
        pos = {}
        for op in s.ops:
            op.pos = pos.get(op.eng, 0)
            pos[op.eng] = op.pos + 1
        for op in s.ops:
            op.waits = []
            for d in op.deps:
                if d.dma is None and d.eng == op.eng:
                    if op.eng == 'pe' and op.dma is None:
                        continue
                op.waits.append(d)
                d.sig = True
        cnt = {}
        keys = []
        for op in s.ops:
            k = op.dma if op.dma else op.eng
            if k not in cnt:
                cnt[k] = 0
                keys.append(k)
            if op.dma:
                cnt[k] += 16
                op.cnt = cnt[k]
            elif op.sig:
                cnt[k] += 1
                op.cnt = cnt[k]
        sems = {k: s.es.enter_context(nc.semaphore("s_" + k)) for k in keys}
        waited = {e: {} for e in s.eng}
        for op in s.ops:
            E = s.eng[op.eng]
            for d in op.waits:
                k = d.dma if d.dma else d.eng
                if waited[op.eng].get(k, 0) >= d.cnt:
                    continue
                E.wait_ge(sems[k], d.cnt)
                waited[op.eng][k] = d.cnt
            try:
                ins = op.fn(E)
                if os.environ.get('K_ANNOT'):
                    ins.annotate("W%s" % (op.where,))
            except BaseException:
                print("FAILED op at lines", op.where, op.eng)
                raise
            if op.dma:
                ins.then_inc(sems[op.dma], 16)
            elif op.sig:
                ins.then_inc(sems[op.eng], 1)
        for k in keys:
            if cnt[k] > 0:
                nc.sync.wait_ge(sems[k], cnt[k])

    def dma(s, out, in_, lane, eng='sp', slow=False, nlanes=1, force_main=False):
        kw = {'allow_slow_non_contiguous': True} if slow else {}
        nlanes = {'ld': 4, 'st': 4, 'wl': 2}.get(lane, nlanes)
        if nlanes > 1:
            i = s.rr.get(lane, 0)
            s.rr[lane] = i + 1
            lane = "%s%d" % (lane, i % nlanes)
        return s.add(eng, lambda E: E.dma_start(out=out.ap, in_=in_.ap, **kw), [in_], [out], dma=lane, force_main=force_main)

    def mm(s, out, lhsT, rhs, start=True, stop=True, tp=None, extra_r=()):
        kw = {}
        if tp is not None:
            kw['tile_position'] = tp
        return s.add('pe', lambda E: E.matmul(out.ap, lhsT=lhsT.ap, rhs=rhs.ap, start=start, stop=stop, **kw),
                     [lhsT, rhs] + list(extra_r), [out])

    def tr(s, out, in_, ident):
        return s.add('pe', lambda E: E.transpose(out.ap, in_.ap, ident.ap), [in_, ident], [out])

    def act(s, out, in_, func, bias=None, scale=1.0):
        reads = [in_]
        kw = {}
        if isinstance(bias, T):
            reads.append(bias)
            kw['bias'] = bias.ap
        elif bias is not None:
            kw['bias'] = bias
        if isinstance(scale, T):
            reads.append(scale)
            kw['scale'] = scale.ap
        else:
            kw['scale'] = scale
        return s.add('act', lambda E: E.activation(out=out.ap, in_=in_.ap, func=func, **kw), reads, [out])

    def tt(s, eng, out, in0, in1, op):
        return s.add(eng, lambda E: E.tensor_tensor(out=out.ap, in0=in0.ap, in1=in1.ap, op=op), [in0, in1], [out])

    def ts(s, eng, out, in0, s1, s2, op0, op1=None):
        reads = [in0]
        a1 = s1.ap if isinstance(s1, T) else s1
        a2 = s2.ap if isinstance(s2, T) else s2
        if isinstance(s1, T):
            reads.append(s1)
        if isinstance(s2, T):
            reads.append(s2)
        if op1 is None:
            return s.add(eng, lambda E: E.tensor_single_scalar(out=out.ap, in_=in0.ap, scalar=a1, op=op0), reads, [out])
        return s.add(eng, lambda E: E.tensor_scalar(out=out.ap, in0=in0.ap, scalar1=a1, scalar2=a2, op0=op0, op1=op1),
                     reads, [out])

    def stt(s, eng, out, in0, sc, in1, op0, op1):
        reads = [in0, in1]
        a = sc.ap if isinstance(sc, T) else sc
        if isinstance(sc, T):
            reads.append(sc)
        return s.add(eng, lambda E: E.scalar_tensor_tensor(out=out.ap, in0=in0.ap, scalar=a, in1=in1.ap, op0=op0, op1=op1),
                     reads, [out])

    def rsq(s, out, in_, eps):
        s.act(out, in_, AF.Sqrt, bias=s.epsc[eps][0:out.ap.shape[0]], scale=1.0)
        s.add('dve', lambda E: E.reciprocal(out=out.ap, in_=out.ap), [out], [out])

    def copy(s, eng, out, in_):
        if eng == 'act':
            return s.add('act', lambda E: E.activation(out=out.ap, in_=in_.ap, func=AF.Copy), [in_], [out])
        return s.add(eng, lambda E: E.tensor_copy(out=out.ap, in_=in_.ap), [in_], [out])

    def memset(s, eng, out, val):
        return s.add(eng, lambda E: E.memset(out.ap, val), [], [out])


def build(SEQ_T=2048, TT=512, NS=16, DEBUG=False):
    nc = bass.Bass("TRN2", target_bir_lowering=False)
    P = Prog(nc)
    dbg_seen = set()

    def dbg(name, t, dt=F32):
        if not DEBUG or name in dbg_seen:
            return
        dbg_seen.add(name)
        shape = list(t.ap.shape)
        o = T(nc.dram_tensor("dbg_" + name, shape, dt, kind="ExternalOutput").ap(), Buf("dbg_" + name))
        P.dma(o, t, 'dbg', eng='pool')
    NT = SEQ_T // TT
    NCH = TT // C
    NTB = TT // 128

    def din(name, shape):
        return T(nc.dram_tensor(name, list(shape), F32, kind="ExternalInput").ap(), Buf(name))

    def dout(name, shape):
        return T(nc.dram_tensor(name, list(shape), F32, kind="ExternalOutput").ap(), Buf(name))

    xp = din("xp", [SEQ_T, D])
    xs = din("xs", [NS, D])
    swkv = din("swkv", [L, NS, 8, 64, 64])
    sshift = din("sshift", [L, NS, NSHIFT])
    pp = din("pp", [L, SEQ_T, DPLE])
    psm = din("psm", [L, NS, DPLE])
    norm_g = din("norm_g", [L, D])
    w_in = din("w_in", [L, D, DIN])
    shift_mu = din("shift_mu", [L, NSHIFT])
    w0 = din("w0", [L, DA])
    w_up = din("w_up", [L, 64, DA])
    a0 = din("a0", [L, DA])
    a_up = din("a_up", [L, 64, DA])
    vres_down = din("vres_down", [1, D, 32])
    vres_up = din("vres_up", [1, 32, DA])
    vres_b = din("vres_b", [1, DA])
    k_k = din("k_k", [L, DA])
    k_a = din("k_a", [L, DA])
    r_k = din("r_k", [L, DA])
    gn_g = din("gn_g", [L, DA])
    gn_b = din("gn_b", [L, DA])
    ln_v_g = din("ln_v_g", [L, 512])
    ln_v_b = din("ln_v_b", [L, 512])
    w_spatial = din("w_spatial", [L, 4, 128, 128])
    b_spatial = din("b_spatial", [L, 4, 128])
    w_br_a = din("w_br_a", [L, DA, D])
    w_br_b = din("w_br_b", [L, 512, D])
    w_out = din("w_out", [L, D, D])
    w_ple = din("w_ple", [L, DPLE, D])
    w_ple_gate = din("w_ple_gate", [L, D, D])
    b_ple_gate = din("b_ple_gate", [L, D])
    final_g = din("final_g", [1, D])

    yp = dout("yp", [SEQ_T, D])
    ys = dout("ys", [NS, D])
    owkv_p = dout("owkv_p", [L, 8, 64, 64])
    oshift_p = dout("oshift_p", [L, NSHIFT])
    owkv_s = dout("owkv_s", [L, NS, 8, 64, 64])
    oshift_s = dout("oshift_s", [L, NS, NSHIFT])
    ocv = dout("ocv", [L, NS, 512])

    scr1 = T(nc.dram_tensor("scr1", [L, NS, 8, 6, 64], F32, kind="ExternalOutput").ap(), Buf("scr1"))
    scr2 = T(nc.dram_tensor("scr2", [L, NS, 512], F32, kind="ExternalOutput").ap(), Buf("scr2"))

    ident = P.sbuf("ident", [128, 128], F32)
    identb = P.sbuf("identb", [128, 128], BF16)
    onesb = P.sbuf("onesb", [128, 128], BF16)
    blkb = P.sbuf("blkb", [128, 128], BF16)
    maskscan = P.sbuf("maskscan", [128, TT], F32)
    maskc = P.sbuf("maskc", [128, 128], F32)

    epst = P.sbuf("epst", [128, 4], F32)
    P.epsc = {}
    for i, e in enumerate([D * EPS, 1e-24, GN_EPS, 1e-5]):
        P.memset('pool', epst[:, i:i + 1], e)
        P.epsc[e] = epst[:, i:i + 1]
    P.memset('pool', ident, 0.0)
    P.add('pool', lambda E: E.affine_select(out=ident.ap, in_=ident.ap, pattern=[[-1, 128]], compare_op=ALU.not_equal,
                                            fill=1.0, base=0, channel_multiplier=1), [ident], [ident])
    P.copy('pool', identb, ident)
    P.memset('pool', onesb, 1.0)
    P.memset('pool', blkb, 0.0)
    P.memset('pool', blkb[0:64, 0:64], 1.0)
    P.memset('pool', blkb[64:128, 64:128], 1.0)
    P.memset('pool', maskscan, 1.0)
    P.memset('pool', maskscan.v(lambda a: a.rearrange("p (c t) -> p c t", t=C)[:, :, 0:1]), 0.0)
    P.memset('pool', maskc, 1.0)
    P.add('pool', lambda E: E.affine_select(out=maskc.ap, in_=maskc.ap, pattern=[[1, 128]], compare_op=ALU.is_ge,
                                            fill=0.0, base=0, channel_multiplier=-1), [maskc], [maskc])
    def build_mask128(name, shape, pat, cm, base):
        a = P.sbuf(name, [128] + shape, F32)
        tmp = P.sbuf(name + "_t", [128] + shape, F32)
        P.memset('pool', a, 1.0)
        P.memset('pool', tmp, 1.0)
        P.add('pool', lambda E: E.affine_select(out=a.ap, in_=a.ap, pattern=pat, compare_op=ALU.is_ge, fill=0.0,
                                                base=base, channel_multiplier=cm), [a], [a])
        P.add('pool', lambda E: E.affine_select(out=tmp.ap, in_=tmp.ap, pattern=pat, compare_op=ALU.is_ge, fill=0.0,
                                                base=base - 64 * cm, channel_multiplier=cm), [tmp], [tmp])
        P.copy('pool', a[64:128], tmp[64:128])
        return a
    m_strict = build_mask128("m_strict", [64], [[1, 64]], -1, -1)
    m_incl = build_mask128("m_incl", [64], [[1, 64]], -1, 0)
    mask_si1 = P.sbuf("mask_si", [128, 2, 64], F32)
    P.copy('pool', mask_si1[:, 0, :], m_strict)
    P.copy('pool', mask_si1[:, 1, :], m_incl)
    mask_l1 = build_mask128("mask_l", [64], [[-1, 64]], 1, -1)
    id2 = P.sbuf("id2", [128, 64], F32)
    P.tt('pool', id2, ident[:, 0:64], ident[:, 64:128], ALU.add)
    mask_si = mask_si1.v(lambda a: a.rearrange("p a b -> p (a b)").unsqueeze(1).to_broadcast([128, 4, 128]))
    mask_l4 = mask_l1.v(lambda a: a.unsqueeze(1).to_broadcast([128, 4, 64]))
    ident8 = id2.v(lambda a: a.unsqueeze(1).to_broadcast([128, 8, 64]))

    def load_vec(name, src, nl, ncol):
        t = P.sbuf(name, [128, nl, ncol], F32)
        P.dma(t, src.v(lambda a: a.rearrange("l (c p) -> p l c", p=128)), 'ld', slow=True)
        return t

    g32 = load_vec("g32", norm_g, L, 8)
    fg32 = load_vec("fg32", final_g, 1, 8)
    mu = load_vec("mu", shift_mu, L, 13)
    hw0 = load_vec("hw0", w0, L, 4)
    a0v = load_vec("a0v", a0, L, 4)
    vbv = load_vec("vbv", vres_b, 1, 4)
    kkv = load_vec("kkv", k_k, L, 4)
    kav = load_vec("kav", k_a, L, 4)
    rkv = load_vec("rkv", r_k, L, 4)
    gng = load_vec("gng", gn_g, L, 4)
    gnb = load_vec("gnb", gn_b, L, 4)
    bgv = load_vec("bgv", b_ple_gate, L, 8)
    omka = P.sbuf("omka", [128, L, 4], F32)
    P.ts('pool', g32, g32, 32.0, None, ALU.mult)
    P.ts('pool', fg32, fg32, 32.0, None, ALU.mult)
    P.ts('pool', hw0, hw0, 0.5, None, ALU.mult)
    ha0 = P.sbuf("ha0", [128, L, 4], F32)
    hvb = P.sbuf("hvb", [128, 1, 4], F32)
    P.ts('pool', ha0, a0v, 0.5, None, ALU.mult)
    P.ts('pool', hvb, vbv, 0.5, None, ALU.mult)
    P.ts('pool', omka, kav, -1.0, 1.0, ALU.mult, ALU.add)

    lngb = P.sbuf("lngb", [128, L, 512], F32)
    lnbb = P.sbuf("lnbb", [128, L, 512], F32)
    for l in range(L):
        P.dma(lngb[:, l, :], ln_v_g[l:l + 1, :].v(lambda a: a.partition_broadcast(128)), 'ld')
        P.dma(lnbb[:, l, :], ln_v_b[l:l + 1, :].v(lambda a: a.partition_broadcast(128)), 'ld')
    ws00 = P.sbuf("ws00", [128, L, 4], F32)
    bs0 = P.sbuf("bs0", [128, L, 4], F32)
    P.dma(ws00, w_spatial.v(lambda a: a[:, :, 0, 0:1].rearrange("l g o -> o l g").partition_broadcast(128)), 'ld', slow=True)
    P.dma(bs0, b_spatial.v(lambda a: a[:, :, 0:1].rearrange("l g o -> o l g").partition_broadcast(128)), 'ld', slow=True)
    bsb = P.sbuf("bsb", [1, L * 4 * 128], BF16)
    P.dma(bsb, b_spatial.v(lambda a: a.rearrange("(o l) g t -> o (l g t)", o=1)), 'wl', eng='pool')
    lup = P.sbuf("lup", [128, L, 512], BF16)
    for l in range(L):
        P.dma(lup[0:64, l, :], w_up[l], 'wl', eng='pool')
        P.dma(lup[64:128, l, :], a_up[l], 'wl', eng='pool')
    vdb = P.sbuf("vdb", [128, 8, 32], BF16)
    P.dma(vdb, vres_down[0].v(lambda a: a.rearrange("(kc p) c -> p kc c", p=128)), 'wl', eng='pool')
    vub = P.sbuf("vub", [32, 512], BF16)
    P.dma(vub, vres_up[0], 'wl', eng='pool')

    banks = [P.psum("bank%d" % i, [128, 512], F32) for i in range(8)]
    bank_i = [0]

    sbank_i = [0]
    ssbank_i = [0]

    def ps():
        if P.side_kind == 'S':
            b = banks[6 + ssbank_i[0] % 2]
            ssbank_i[0] += 1
            return b
        if P.side_kind == 'B':
            b = banks[4 + sbank_i[0] % 2]
            sbank_i[0] += 1
            return b
        pool_ = (0, 1, 2, 3) if (P.auto is not None) else (0, 1, 2, 3, 6, 7)
        b = banks[pool_[bank_i[0] % len(pool_)]]
        bank_i[0] += 1
        return b

    wmT = P.sbuf("wmT", [128, L, 4, 128], BF16)
    gtok = P.sbuf("gtok", [128, 512], F32)
    wstage = gtok.v(lambda a: a.rearrange("p (g s) -> p g s", s=128))
    for l in range(L):
        P.dma(wstage, w_spatial[l].v(lambda a: a.rearrange("g t s -> t g s")), 'ld')
        b = ps()
        for g in range(4):
            P.tr(b[:, g * 128:(g + 1) * 128], wstage[:, g, :], ident)
        for g in range(4):
            P.tt('dve', wmT[:, l, g, :], b[:, g * 128:(g + 1) * 128], maskc, ALU.mult)

    NSLOT = 9
    wslots = [P.sbuf("wslot%d" % i, [128, 8 * 256], BF16) for i in range(NSLOT)]
    wlist = []

    def wsrc(src2d, kcs, c0, ncols):
        return src2d.v(lambda a: a.rearrange("(kc p) c -> p kc c", p=128)[:, :, c0:c0 + ncols]), kcs, ncols

    tiles = [('s', 0)] + [('p', i) for i in range(NT)]
    for (kind, ti) in tiles[1:]:
        for l in range(L):
            wlist.append(wsrc(w_in[l], 8, 1536, 128))
            for c in range(4):
                for x in range(3):
                    wlist.append(wsrc(w_in[l], 8, x * 512 + c * 128, 128))
            for hf in range(2):
                wlist.append(wsrc(w_in[l], 8, 2688 + hf * 256, 256))
            for hf in range(2):
                wlist.append(wsrc(w_in[l], 8, 2176 + hf * 256, 256))
            for hf in range(2):
                wlist.append(wsrc(w_in[l], 8, 3200 + hf * 256, 256))
            for hf in range(2):
                wlist.append(wsrc(w_in[l], 8, 1664 + hf * 256, 256))
            for jg in range(4):
                wlist.append(wsrc(w_in[l], 8, 3712 + jg * 256, 256))
                wlist.append(wsrc(w_in[l], 8, 4736 + jg * 256, 256))
                wlist.append(wsrc(w_br_a[l], 4, jg * 256, 256))
                wlist.append(wsrc(w_br_b[l], 4, jg * 256, 256))
            for jg in range(4):
                wlist.append(wsrc(w_out[l], 8, jg * 256, 256))
            for jg in range(4):
                wlist.append(wsrc(w_ple[l], 2, jg * 256, 256))
                wlist.append(wsrc(w_ple_gate[l], 8, jg * 256, 256))
    wcur = [0]
    wissued = [0]
    wdone = set()

    def w_issue_ready():
        while wissued[0] < len(wlist):
            i = wissued[0]
            if i >= NSLOT and (i - NSLOT) not in wdone:
                return
            src, kcs, ncols = wlist[i]
            slot = wslots[i % NSLOT]
            dst = slot[:, 0:kcs * ncols].v(lambda a: a.rearrange("p (kc c) -> p kc c", c=ncols))
            P.dma(dst, src, 'ws%d' % (i % NSLOT), eng='pool', force_main=True)
            wissued[0] += 1

    class WS:
        pass

    s_cur = [0]
    P.main_cur = lambda: wcur[0]

    def w_get():
        if P.side_kind == 'S':
            i = s_cur[0]
            s_cur[0] += 1
            src, kcs, ncols = wlist[i]
            slot = wslots[i % NSLOT]
            t = slot[:, 0:kcs * ncols].v(lambda a: a.rearrange("p (kc c) -> p kc c", c=ncols))
            t.widx = i
            P.marker(('get', i))
            return t
        i = wcur[0]
        w_issue_ready()
        assert wissued[0] > i, "weight slot pool too small"
        src, kcs, ncols = wlist[i]
        slot = wslots[i % NSLOT]
        wcur[0] += 1
        t = slot[:, 0:kcs * ncols].v(lambda a: a.rearrange("p (kc c) -> p kc c", c=ncols))
        t.widx = i
        return t

    per_tile_slices = len(wlist) // NT

    def w_done(*ts_):
        if P.side_kind == 'S':
            P.marker(('done', [t.widx for t in ts_]))
            return

        def go():
            idxs = set(t.widx for t in ts_)
            if P.auto is not None and max(idxs) < per_tile_slices:
                P.pull_s(0, until_done=idxs)
            for t in ts_:
                wdone.add(t.widx)
            w_issue_ready()
        P.action(go)

    hT = [P.sbuf("hT%d" % k, [128, TT], F32) for k in range(8)]
    xnT = [P.sbuf("xnT%d" % k, [128, TT], BF16) for k in range(8)]
    sqb = [P.sbuf("sqb0", [128, TT], BF16)] * 2
    rstd = P.sbuf("rstd", [128, TT], F32)
    pT = [P.sbuf("pT%d" % k, [128, TT], BF16) for k in range(2)]
    xtok = [P.sbuf("xtok%d" % i, [128, D], F32) for i in range(2)]
    zraw = [P.sbuf("zraw0", [128, TT + 1], F32)] * 2
    zraw_i = [0]
    carry = [P.sbuf("carry%d" % l, [128, 13], F32) for l in range(L)]
    zpT = P.sbuf("zpT", [128, 13, NS], F32)
    twb = P.sbuf("twb", [64, TT], BF16)
    adb = P.sbuf("adb", [128, TT], BF16)
    zr = P.sbuf("zr", [128, TT], F32)
    zk = P.sbuf("zk", [128, TT], F32)
    V = [P.sbuf("V%d" % c, [128, TT], BF16) for c in range(4)]
    VF = [P.sbuf("VF%d" % c, [128, TT], BF16) for c in range(4)]
    xvd = P.sbuf("xvd", [32, TT], BF16)
    tA = [P.sbuf("tA%d" % i, [128, TT], F32) for i in range(8)]
    tB = [P.sbuf("tB%d" % i, [128, TT], BF16) for i in range(2)]
    zsl = tA[6]
    ar1 = P.es.enter_context(nc.sbuf_tensor("arena1", [128, 4096], F32))
    bufAR, bufBK = Buf("AR"), Buf("BK")
    ARh = ar1[:, 0:2048].bitcast(BF16)
    BKh = ar1[:, 2048:4096].bitcast(BF16)
    c4v = lambda a: a.rearrange("p (j a b) -> p j a b", a=2, b=C)
    AR = [T(c4v(ARh[:, c * 2 * TT:(c + 1) * 2 * TT]), bufAR) for c in range(4)]
    BK = [T(c4v(BKh[:, c * 2 * TT:(c + 1) * 2 * TT]), bufBK) for c in range(4)]
    MG = [T(ARh[:, k * TT:(k + 1) * TT], bufAR) for k in range(8)]
    YAG = [T(BKh[:, c * TT:(c + 1) * TT], bufBK) for c in range(4)]
    YBG = P.sbuf("YBG", [128, 4, TT], BF16)
    WC = P.sbuf("WC", [128, 4, NCH], F32)
    BON = [P.sbuf("BON%d" % c, [128, TT], BF16) for c in range(4)]
    YA = P.sbuf("YA", [128, 4, TT], F32)
    USG = P.sbuf("USG", [128, 4, TT], BF16)
    VN = [P.sbuf("VN%d" % i, [128, 512], BF16) for i in range(max(NTB, 1))]
    Hf = [P.sbuf("Hf%d" % l, [128, 4, 64], F32) for l in range(L)]
    Hb = [P.sbuf("Hb%d" % l, [128, 4, 64], BF16) for l in range(L)]
    ar2 = P.es.enter_context(nc.sbuf_tensor("arena2", [128, 4096], F32))
    a2off = [0]
    a2bufs = []

    def a2(name, words, dt, shape_fn=None):
        o = a2off[0]
        a2off[0] += words
        assert a2off[0] <= 4096
        ap = ar2[:, o:o + words]
        if dt == BF16:
            ap = ap.bitcast(BF16)
        if shape_fn is not None:
            ap = shape_fn(ap)
        b_ = Buf(name)
        a2bufs.append(b_)
        return T(ap, b_)
    h8 = lambda a: a.rearrange("p (h x) -> p h x", h=8)
    g8 = lambda a: a.rearrange("p (h a b) -> p h a b", h=8, a=2)
    Vt = a2("Vt", 256, BF16)
    Bt = a2("Bt", 256, BF16)
    Kt = a2("Kt", 256, BF16)
    G1s = a2("G1s", 512, BF16, g8)
    G2s = a2("G2s", 512, BF16, g8)
    PQ = [a2("PQ%d" % i, 256, BF16, h8) for i in range(6)]
    TtB = a2("TtB", 256, BF16, h8)
    Xs = a2("Xs", 256, BF16)
    Us = a2("Us", 256, BF16)
    SV = P.sbuf("SV", [128, 4, 6, NS], F32)
    TOKV = [P.sbuf("TOKV0", [NS, 512], F32)] * 2
    SVP = P.sbuf("SVP", [128, 6, 64], F32)
    sa = P.sbuf("sa", [128, 64], F32)
    ysm = P.sbuf("ysm", [128, 64], F32)
    ytok = TOKV[0]
    outtok = [xtok[0]]
    small = P.sbuf("small", [128, 8], F32)
    gtok2 = P.sbuf("gtok2", [128, 512], F32)
    vnf = gtok2[0:NS, :]
    vnfm = P.sbuf("vnfm", [128, 4, NS], F32)

    prm = dict(hT=hT, xnT=xnT, sqb=sqb, rstd=rstd, pT=pT, zraw=zraw, zsl=zsl, twb=twb, adb=adb, zr=zr, zk=zk, V=V, VF=VF,
               xvd=xvd, tA=tA, tB=tB, BON=BON, YA=YA, USG=USG, YBG=YBG, YAG=YAG, MG=MG, small=small, gtok=gtok, gtok2=gtok2)
    S_hT = [P.sbuf("S_hT%d" % k, [128, NS], F32) for k in range(8)]
    S_xnT = [P.sbuf("S_xnT%d" % k, [128, NS], BF16) for k in range(8)]
    S_tA = [P.sbuf("S_tA%d" % i, [128, NS], F32) for i in range(8)]
    S_tB = [P.sbuf("S_tB%d" % i, [128, NS], BF16) for i in range(2)]
    smp = dict(hT=S_hT, xnT=S_xnT, sqb=[P.sbuf("S_sqb", [128, NS], BF16)] * 2, rstd=P.sbuf("S_rstd", [128, NS], F32),
               pT=[P.sbuf("S_pT%d" % k, [128, NS], BF16) for k in range(2)],
               zraw=[P.sbuf("S_zraw", [128, NS + 1], F32)] * 2, zsl=S_tA[6],
               twb=P.sbuf("S_twb", [64, NS], BF16), adb=P.sbuf("S_adb", [128, NS], BF16),
               zr=P.sbuf("S_zr", [128, NS], F32), zk=P.sbuf("S_zk", [128, NS], F32),
               V=[P.sbuf("S_V%d" % c, [128, NS], BF16) for c in range(4)],
               VF=[P.sbuf("S_VF%d" % c, [128, NS], BF16) for c in range(4)],
               xvd=P.sbuf("S_xvd", [32, NS], BF16), tA=S_tA, tB=S_tB,
               BON=[P.sbuf("S_BON%d" % c, [128, NS], BF16) for c in range(4)],
               YA=P.sbuf("S_YA", [128, 4, NS], F32), USG=P.sbuf("S_USG", [128, 4, NS], BF16),
               YBG=P.sbuf("S_YBG", [128, 4, NS], BF16),
               YAG=[P.sbuf("S_YAG%d" % c, [128, NS], BF16) for c in range(4)],
               MG=[P.sbuf("S_MG%d" % k, [128, NS], BF16) for k in range(8)],
               small=P.sbuf("S_small", [128, 8], F32), gtok=TOKV[0], gtok2=P.sbuf("S_gtok2", [NS, 512], F32))
    P.memset('pool', smp['small'], 0.0)
    NPART = 8
    xt1 = xtok[1].ap
    Sst_p = T(xt1[:, 0:512].rearrange("p (i j) -> p i j", j=64), Buf("Sst_p"))
    Stmp_p = T(xt1[:, 512:1024].rearrange("p (i j) -> p i j", j=64), Buf("Stmp_p"))
    for l in range(L):
        P.memset('pool', Hf[l], 0.0)
        P.memset('pool', Hb[l], 0.0)
        P.memset('pool', carry[l], 0.0)

    def bc_last(t, shape):
        return t.v(lambda a: a.to_broadcast(shape))

    for (kind, ti) in tiles:
        samp = (kind == 's')
        bs_ = smp if samp else prm
        hT, xnT, sqb, rstd, pT, zraw, zsl, twb, adb, zr, zk, V, VF = (bs_[k_] for k_ in (
            'hT', 'xnT', 'sqb', 'rstd', 'pT', 'zraw', 'zsl', 'twb', 'adb', 'zr', 'zk', 'V', 'VF'))
        xvd, tA, tB, BON, YA, USG, YBG, YAG, MG, small, gtok, gtok2 = (bs_[k_] for k_ in (
            'xvd', 'tA', 'tB', 'BON', 'YA', 'USG', 'YBG', 'YAG', 'MG', 'small', 'gtok', 'gtok2'))
        vnf = gtok2[0:NS, :]
        SW = 512 if samp else 1024
        xst = TOKV[0] if samp else xtok[0]
        if samp:
            P.begin_side('S')
        n = NS if samp else TT
        t0 = ti * TT
        ntb = 1 if samp else NTB
        rows = NS if samp else 128

        for tb in range(ntb):
            for pc in range(D // SW):
                src = xs if samp else xp[t0 + tb * 128: t0 + (tb + 1) * 128, :]
                P.dma(xst[0:rows, 0:SW], src[:, pc * SW:(pc + 1) * SW], 'ld')
                for half in range(SW // 512):
                    b = ps()
                    for kk_ in range(4):
                        P.tr(b[:, kk_ * 128: kk_ * 128 + rows], xst[0:rows, (half * 4 + kk_) * 128:(half * 4 + kk_ + 1) * 128],
                             ident[0:rows, 0:rows])
                    for kk_ in range(4):
                        kc = pc * (SW // 128) + half * 4 + kk_
                        P.copy('dve', hT[kc][:, tb * 128: tb * 128 + rows], b[:, kk_ * 128: kk_ * 128 + rows])
        if samp:
            pass

        def rmsnorm(gv, outs, f32out=False):
            b = ps()
            for kc in range(8):
                sq = sqb[kc % 2]
                P.act(sq[:, 0:n], hT[kc][:, 0:n], AF.Square)
                P.mm(b[:, 0:n], onesb, sq[:, 0:n], start=(kc == 0), stop=(kc == 7))
            P.rsq(rstd[:, 0:n], b[:, 0:n], D * EPS)
            for kc in range(8):
                P.stt('dve', outs[kc][:, 0:n], hT[kc][:, 0:n], gv[:, kc:kc + 1], rstd[:, 0:n], ALU.mult, ALU.mult)

        def proj(wt, col0, ncol_chunks=1, M=128):
            b = ps()
            for kc in range(8):
                P.mm(b[0:M, 0:n], wt[:, kc, col0:col0 + M], xnT[kc][:, 0:n], start=(kc == 0), stop=(kc == 7))
            return b

        for l in range(L):
            rmsnorm(g32[:, l, :], xnT)

            def shifted(b, f, dst):
                zz = zraw[zraw_i[0] % 2]
                zraw_i[0] += 1
                P.copy('act', zz[:, 1:n + 1], b[:, 0:n])
                if samp:
                    prev = zpT[:, f, :]
                    P.dma(oshift_s[l].v(lambda a: a[:, f * 128:(f + 1) * 128].rearrange("b p -> p b")), zz[:, 1:n + 1], 'st', slow=True)
                else:
                    P.copy('pool', zz[:, 0:1], carry[l][:, f:f + 1])
                    prev = zz[:, 0:n]
                    P.copy('pool', carry[l][:, f:f + 1], zz[:, n:n + 1])
                d = tA[7]
                P.tt('pool', d[:, 0:n], prev, zz[:, 1:n + 1], ALU.subtract)
                P.stt('dve', dst[:, 0:n], d[:, 0:n], mu[:, l, f:f + 1], zz[:, 1:n + 1], ALU.mult, ALU.add)

            if samp:
                for f0 in range(0, 13, 4):
                    fs = list(range(f0, min(13, f0 + 4)))
                    P.dma(xst[0:NS, 0:len(fs) * 128], sshift[l][:, f0 * 128:(f0 + len(fs)) * 128], 'ld')
                    b = ps()
                    for i, f in enumerate(fs):
                        P.tr(b[:, i * 128: i * 128 + NS], xst[0:NS, i * 128:(i + 1) * 128], ident[0:NS, 0:NS])
                    for i, f in enumerate(fs):
                        P.copy('dve', zpT[:, f, :], b[:, i * 128: i * 128 + NS])

            wt = w_get()
            b = proj(wt, 0)
            shifted(b, 12, zsl)
            w_done(wt)
            P.act(twb[0:64, 0:n], zsl[0:64, 0:n], AF.Tanh)
            P.copy('act', adb[64:128, 0:n], zsl[64:128, 0:n])
            if l == 1:
                b = ps()
                for kc in range(8):
                    P.mm(b[0:32, 0:n], vdb[:, kc, :], xnT[kc][:, 0:n], start=(kc == 0), stop=(kc == 7))
                P.copy('act', xvd[0:32, 0:n], b[0:32, 0:n])

            for c in range(4):
                wr = w_get()
                b = proj(wr, 0)
                shifted(b, c, zr)
                w_done(wr)
                wk = w_get()
                b = proj(wk, 0)
                shifted(b, 4 + c, zk)
                w_done(wk)
                wv = w_get()
                b = proj(wv, 0)
                shifted(b, 8 + c, V[c])
                w_done(wv)
                lw, av, kk, t1, kmod, bbv, cs, ex = tA[0], tA[1], tA[2], tA[3], tA[4], tA[5], tA[6], tA[7]
                b = ps()
                P.mm(b[:, 0:n], lup[0:64, l, c * 128:(c + 1) * 128], twb[0:64, 0:n])
                P.act(lw[:, 0:n], b[:, 0:n], AF.Tanh, bias=hw0[:, l, c:c + 1], scale=0.5)
                P.ts('dve', lw[:, 0:n], lw[:, 0:n], LOGW_C, LOGW_C, ALU.mult, ALU.add)
                b = ps()
                P.mm(b[:, 0:n], lup[64:128, l, c * 128:(c + 1) * 128], adb[64:128, 0:n])
                P.act(av[:, 0:n], b[:, 0:n], AF.Tanh, bias=ha0[:, l, c:c + 1], scale=0.5)
                P.ts('dve', av[:, 0:n], av[:, 0:n], 0.5, 0.5, ALU.mult, ALU.add)
                if l == 0:
                    P.copy('act', VF[c][:, 0:n], V[c][:, 0:n])
                else:
                    b = ps()
                    P.mm(b[:, 0:n], vub[0:32, c * 128:(c + 1) * 128], xvd[0:32, 0:n])
                    sg = tA[6]
                    P.act(sg[:, 0:n], b[:, 0:n], AF.Tanh, bias=hvb[:, 0, c:c + 1], scale=0.5)
                    P.ts('dve', sg[:, 0:n], sg[:, 0:n], 0.5, 0.5, ALU.mult, ALU.add)
                    dd = tA[7]
                    P.tt('pool', dd[:, 0:n], VF[c][:, 0:n], V[c][:, 0:n], ALU.subtract)
                    P.tt('dve', dd[:, 0:n], dd[:, 0:n], sg[:, 0:n], ALU.mult)
                    P.tt('pool', V[c][:, 0:n], V[c][:, 0:n], dd[:, 0:n], ALU.add)
                sq = tB[0]
                P.act(sq[:, 0:n], zk[:, 0:n], AF.Square, scale=kkv[:, l, c:c + 1])
                b = ps()
                P.mm(b[:, 0:n], blkb, sq[:, 0:n])
                rn = tA[3]
                P.rsq(rn[:, 0:n], b[:, 0:n], 1e-24)
                P.stt('dve', kk[:, 0:n], zk[:, 0:n], kkv[:, l, c:c + 1], rn[:, 0:n], ALU.mult, ALU.mult)
                P.act(t1[:, 0:n], av[:, 0:n], AF.Identity, bias=omka[:, l, c:c + 1], scale=kav[:, l, c:c + 1])
                P.tt('pool', kmod[:, 0:n], zk[:, 0:n], t1[:, 0:n], ALU.mult)
                P.tt('pool', bbv[:, 0:n], kk[:, 0:n], av[:, 0:n], ALU.mult)
                rkr = tB[1]
                P.stt('dve', rkr[:, 0:n], zr[:, 0:n], rkv[:, l, c:c + 1], kmod[:, 0:n], ALU.mult, ALU.mult)
                b = ps()
                P.mm(b[:, 0:n], blkb, rkr[:, 0:n])
                P.tt('dve', BON[c][:, 0:n], b[:, 0:n], V[c][:, 0:n], ALU.mult)
                if not samp:
                    P.add('dve', lambda E, cs=cs, lw=lw: E.tensor_tensor_scan(out=cs.ap, data0=maskscan.ap, data1=lw.ap, initial=0.0,
                                                                              op0=ALU.mult, op1=ALU.add), [maskscan, lw], [cs])
                    P.tt('pool', ex, cs, lw, ALU.subtract)
                    epos, eneg, eex = tA[3], tA[1], tA[7]
                    P.act(epos, cs, AF.Exp)
                    P.act(eex, ex, AF.Exp)
                    P.act(eneg, cs, AF.Exp, scale=-1.0)
                    c3 = lambda a: a.rearrange("p (c t) -> p c t", t=C)
                    P.tt('dve', AR[c][:, :, 1, :], zr.v(c3), epos.v(c3), ALU.mult)
                    P.stt('dve', AR[c][:, :, 0, :], kk.v(c3), -1.0, eex.v(c3), ALU.mult, ALU.mult)
                    P.tt('pool', BK[c][:, :, 1, :], kmod.v(c3), eneg.v(c3), ALU.mult)
                    P.tt('pool', BK[c][:, :, 0, :], bbv.v(c3), eneg.v(c3), ALU.mult)
                    P.copy('act', WC[:, c, :], epos.v(c3)[:, :, C - 1])
                else:
                    P.copy('pool', SV[:, c, 0, :], zr[:, 0:n])
                    P.act(SV[:, c, 1, :], lw[:, 0:n], AF.Exp)
                    P.copy('pool', SV[:, c, 2, :], kmod[:, 0:n])
                    P.copy('pool', SV[:, c, 3, :], V[c][:, 0:n])
                    P.ts('pool', SV[:, c, 4, :], kk[:, 0:n], -1.0, None, ALU.mult)
                    P.copy('pool', SV[:, c, 5, :], bbv[:, 0:n])

            wv_ = {}

            def vb_blocks(tbs):
                for tb in tbs:
                    b = ps()
                    for hf, wvb in enumerate((wv_['0'], wv_['1'])):
                        for kc in range(8):
                            P.mm(b[0:rows, hf * 256:(hf + 1) * 256], xnT[kc][:, tb * 128: tb * 128 + rows], wvb[:, kc, :],
                                 start=(kc == 0), stop=(kc == 7))
                    g = gtok
                    P.act(g[0:rows, :], b[0:rows, :], AF.Gelu_apprx_tanh)
                    P.add('dve', lambda E, g=g, rows=rows, small=small: E.reduce_sum(out=small[0:rows, 0:1].ap, in_=g[0:rows, :].ap, axis=AX.X), [g], [small])
                    P.act(gtok2[0:rows, :], g[0:rows, :], AF.Square)
                    P.add('dve', lambda E, rows=rows, small=small, gtok2=gtok2: E.reduce_sum(out=small[0:rows, 1:2].ap, in_=gtok2[0:rows, :].ap, axis=AX.X), [gtok2], [small])
                    P.ts('dve', small[0:rows, 2:3], small[0:rows, 0:1], 1.0 / 512, None, ALU.mult)
                    P.tt('dve', small[0:rows, 3:4], small[0:rows, 2:3], small[0:rows, 2:3], ALU.mult)
                    P.stt('dve', small[0:rows, 4:5], small[0:rows, 1:2], 1.0 / 512, small[0:rows, 3:4], ALU.mult, ALU.subtract)
                    P.rsq(small[0:rows, 5:6], small[0:rows, 4:5], 1e-5)
                    P.ts('dve', g[0:rows, :], g[0:rows, :], small[0:rows, 2:3], small[0:rows, 5:6], ALU.subtract, ALU.mult)
                    P.tt('pool', g[0:rows, :], g[0:rows, :], lngb[0:rows, l, :], ALU.mult)
                    if samp:
                        P.tt('pool', vnf, g[0:rows, :], lnbb[0:rows, l, :], ALU.add)
                        P.dma(ocv[l], vnf, 'st')
                        b = ps()
                        for g4 in range(4):
                            P.tr(b[:, g4 * 128: g4 * 128 + NS], vnf[:, g4 * 128:(g4 + 1) * 128], ident[0:NS, 0:NS])
                        for g4 in range(4):
                            P.copy('dve', vnfm[:, g4, :], b[:, g4 * 128: g4 * 128 + NS])
                    else:
                        P.tt('pool', VN[tb], g, lnbb[:, l, :], ALU.add)

            def partB0():
                wv_['0'] = w_get()
                wv_['1'] = w_get()
                vb_blocks(range(0, (ntb + 1) // 2))

            def partB1():
                vb_blocks(range((ntb + 1) // 2, ntb))
                w_done(wv_['0'], wv_['1'])

            def partB2():
                for hf in range(2):
                    wu = w_get()
                    for cc in range(2):
                        g4 = hf * 2 + cc
                        b = proj(wu, cc * 128)
                        P.act(USG[:, g4, 0:n], b[:, 0:n], AF.Gelu_apprx_tanh)
                    w_done(wu)

            def partB3():
                wgb = None
                for g4 in range(4):
                    if g4 % 2 == 0:
                        wgb = w_get()
                    b = proj(wgb, (g4 % 2) * 128)
                    sg = tB[g4 % 2]
                    P.act(sg[:, 0:n], b[:, 0:n], AF.Silu)
                    P.tt('dve' if g4 % 2 else 'pool', USG[:, g4, 0:n], USG[:, g4, 0:n], sg[:, 0:n], ALU.mult)
                    if g4 % 2 == 1:
                        w_done(wgb)
                if samp:
                    for g4 in range(4):
                        mx = tA[0][:, 0:n]
                        P.ts('dve', mx, vnfm[:, g4, :], ws00[:, l, g4:g4 + 1], bs0[:, l, g4:g4 + 1], ALU.mult, ALU.add)
                        P.tt('dve', YBG[:, g4, 0:n], mx, USG[:, g4, 0:n], ALU.mult)
                else:
                    for tb in range(ntb):
                        b = ps()
                        for g4 in range(4):
                            o = b[:, g4 * 128:(g4 + 1) * 128]
                            P.mm(o, VN[tb][:, g4 * 128:(g4 + 1) * 128], wmT[:, l, g4, :], start=True, stop=False)
                            P.mm(o, onesb[0:1, :], bsb[0:1, (l * 4 + g4) * 128:(l * 4 + g4 + 1) * 128], start=False, stop=True)
                        tsl = slice(tb * 128, (tb + 1) * 128)
                        P.tt('dve', YBG[:, :, tsl], b.v(lambda a: a.rearrange("p (g t) -> p g t", t=128)), USG[:, :, tsl], ALU.mult)
            partsB = [partB0, partB1, partB2, partB3]

            if not samp:
                h3 = lambda a: a.rearrange("p (h x) -> p h x", x=64)
                c4 = lambda a: a.rearrange("p (c x) -> p c x", x=64)
                cq = lambda a: a.rearrange("p (c q) a b -> p c q (a b)", q=2)
                cq3 = lambda a: a.rearrange("p (c q) x -> p c q x", q=2)
                xq = lambda a: a.rearrange("p (c q x) -> p c q x", q=2, x=64)
                P.begin_side()
                for fB in partsB:
                    fB()
                sideB = P.end_side()
                nB = sum(1 for it in sideB if isinstance(it, Op))
                npull = (NCH // 2) * 60
                kB = max(1, -(-nB // npull))
                pullB = lambda: P.pull(sideB, kB)
                for jp in range(NCH // 2):
                    tsl = slice(jp * 128, (jp + 1) * 128)
                    b = ps()
                    for c in range(4):
                        P.mm(b[:, c * 128:(c + 1) * 128], V[c][:, tsl], identb)
                    P.copy('act', Vt, b)
                    pullB()
                    bb1 = ps()
                    bb2 = ps()
                    for c in range(4):
                        for jj in range(2):
                            po = 64 * jj
                            P.mm(bb1[po:po + 64, c * 128:(c + 1) * 128], BK[c][:, 2 * jp + jj, 0, :], identb, tp=(0, po))
                            P.mm(bb2[po:po + 64, c * 128:(c + 1) * 128], BK[c][:, 2 * jp + jj, 1, :], identb, tp=(0, po))
                    P.copy('dve', Bt, bb1)
                    pullB()
                    P.copy('act', Kt, bb2)
                    pullB()
                    for q in range(2):
                        qs = slice(q * 64, (q + 1) * 64)
                        b1, b2 = ps(), ps()
                        for c in range(4):
                            for jj in range(2):
                                po = 64 * jj
                                j = 2 * jp + jj
                                arv = AR[c][qs, j, :, :].v(lambda a: a.rearrange("p a b -> p (a b)"))
                                P.mm(b1[po:po + 64, c * 128:(c + 1) * 128], BK[c][qs, j, 0, :], arv, tp=(q * 64, po))
                                P.mm(b2[po:po + 64, c * 128:(c + 1) * 128], BK[c][qs, j, 1, :], arv, tp=(q * 64, po))
                        i1 = b1.v(lambda a: a.rearrange("p (h x) -> p h x", x=128))
                        i2 = b2.v(lambda a: a.rearrange("p (h x) -> p h x", x=128))
                        P.tt('dve', G1s.v(cq)[:, :, q, :], i1, mask_si, ALU.mult)
                        pullB()
                        P.tt('dve', G2s.v(cq)[:, :, q, :], i2, mask_si, ALU.mult)
                        pullB()
                    Pk, Qk = PQ[0], PQ[1]
                    for q in range(2):
                        qs = slice(q * 64, (q + 1) * 64)
                        b3 = ps()
                        for c in range(4):
                            for jj in range(2):
                                po = 64 * jj
                                j = 2 * jp + jj
                                P.mm(b3[po:po + 64, c * 64:(c + 1) * 64], AR[c][qs, j, 0, :], BK[c][qs, j, 0, :], tp=(q * 64, po))
                        P.tt('dve', Pk.v(cq3)[:, :, q, :], b3[:, 0:256].v(c4), mask_l4, ALU.mult)
                        pullB()
                    P.copy('pool', Qk, G1s[:, :, 0, :])
                    pullB()
                    P.tt('pool', TtB, G1s[:, :, 0, :], ident8, ALU.add)
                    pullB()
                    free = [PQ[2], PQ[3], PQ[4], PQ[5]]
                    for k in range(6):
                        if k >= 1:
                            b = ps()
                            for h in range(8):
                                for po in (0, 64):
                                    P.mm(b[po:po + 64, h * 64:(h + 1) * 64], Pk[po:po + 64, h, :], TtB[po:po + 64, h, :], tp=(po, po))
                            P.tt('dve', TtB, b.v(h3), TtB, ALU.add)
                            pullB()
                        Pn = Qn = None
                        if k <= 4:
                            b = ps()
                            for h in range(8):
                                for po in (0, 64):
                                    P.mm(b[po:po + 64, h * 64:(h + 1) * 64], Qk[po:po + 64, h, :], Pk[po:po + 64, h, :], tp=(po, po))
                            Pn = free.pop(0)
                            P.copy('act', Pn, b.v(h3))
                            pullB()
                        if k <= 3:
                            b = ps()
                            for h in range(8):
                                for po in (0, 64):
                                    P.mm(b[po:po + 64, h * 64:(h + 1) * 64], Pk[po:po + 64, h, :], Qk[po:po + 64, h, :], tp=(po, po))
                            Qn = free.pop(0)
                            P.copy('dve', Qn, b.v(h3))
                            pullB()
                        free.append(Pk)
                        free.append(Qk)
                        Pk, Qk = Pn, Qn
                    for jj in range(2):
                        po = 64 * jj
                        pp_ = slice(po, po + 64)
                        j = 2 * jp + jj
                        js = slice(j * C, (j + 1) * C)
                        bx2 = ps()
                        for h in range(8):
                            P.mm(bx2[pp_, h * 64:(h + 1) * 64], G2s[pp_, h, 0, :], Vt[pp_, h * 64:(h + 1) * 64], tp=(po, po))
                        P.copy('act', Xs[pp_, :], bx2[pp_, :])
                        pullB()
                        for q in range(2):
                            qs = slice(q * 64, (q + 1) * 64)
                            bxq = ps()
                            for c in range(4):
                                P.mm(bxq[pp_, c * 64:(c + 1) * 64], AR[c][qs, j, 0, :], Hb[l][qs, c, :], tp=(q * 64, po))
                            xv = Xs[pp_, :].v(xq)[:, :, q, :]
                            P.tt('dve', xv, bxq[pp_, 0:256].v(c4), xv, ALU.add)
                            pullB()
                        bu = ps()
                        for h in range(8):
                            P.mm(bu[pp_, h * 64:(h + 1) * 64], TtB[pp_, h, :], Xs[pp_, h * 64:(h + 1) * 64], tp=(po, po))
                        P.copy('dve', Us[pp_, :], bu[pp_, :])
                        pullB()
                        by2 = ps()
                        for h in range(8):
                            c, q = h // 2, h % 2
                            qs = slice(q * 64, (q + 1) * 64)
                            hs = slice(h * 64, (h + 1) * 64)
                            o = by2[qs, c * 64:(c + 1) * 64]
                            P.mm(o, Us[pp_, hs], G1s[pp_, h, 1, :], start=True, stop=False, tp=(po, q * 64))
                            P.mm(o, Vt[pp_, hs], G2s[pp_, h, 1, :], start=False, stop=True, tp=(po, q * 64))
                        P.copy('act', YA[:, :, js], by2[:, 0:256].v(c4))
                        pullB()
                        for q in range(2):
                            qs = slice(q * 64, (q + 1) * 64)
                            byq = ps()
                            for c in range(4):
                                P.mm(byq[qs, c * 64:(c + 1) * 64], Hb[l][qs, c, :], AR[c][qs, j, 1, :], tp=(q * 64, q * 64))
                            P.tt('dve', YA[qs, :, js], byq[qs, 0:256].v(c4), YA[qs, :, js], ALU.add)
                            pullB()
                        bh = ps()
                        for h in range(8):
                            c, q = h // 2, h % 2
                            qs = slice(q * 64, (q + 1) * 64)
                            hs = slice(h * 64, (h + 1) * 64)
                            o = bh[qs, c * 64:(c + 1) * 64]
                            P.mm(o, Bt[pp_, hs], Us[pp_, hs], start=True, stop=False, tp=(po, q * 64))
                            P.mm(o, Kt[pp_, hs], Vt[pp_, hs], start=False, stop=True, tp=(po, q * 64))
                        P.tt('dve', Hf[l], bh[:, 0:256].v(c4), Hf[l], ALU.add)
                        pullB()
                        P.tt('pool', Hf[l], Hf[l], bc_last(WC[:, :, j:j + 1], [128, 4, 64]), ALU.mult)
                        pullB()
                        P.copy('act', Hb[l], Hf[l])
                        pullB()
                P.pull(sideB, 10 ** 9)
                if ti == NT - 1:
                    for c in range(4):
                        b = ps()
                        P.tr(b[0:64, 0:128], Hf[l][:, c, :], ident)
                        st = tA[0]
                        P.copy('dve', st[0:64, 0:128], b[0:64, 0:128])
                        P.dma(owkv_p[l, 2 * c:2 * c + 2].v(lambda a: a.rearrange("q v k -> v q k")),
                              st[0:64, 0:128].v(lambda a: a.rearrange("p (q k) -> p q k", q=2)), 'st')
                    P.dma(oshift_p[l:l + 1, :].v(lambda a: a.rearrange("o (f p) -> p (o f)", p=128)), carry[l], 'st', slow=True)
            else:
                for fB in partsB:
                    fB()
                for x in range(6):
                    b = ps()
                    for c in range(4):
                        P.tr(b[0:NS, c * 128:(c + 1) * 128], SV[:, c, x, :], ident)
                    P.copy('act' if x % 2 else 'dve', TOKV[x % 2], b[0:NS, :])
                    P.dma(scr1[l][:, :, x, :], TOKV[x % 2].v(lambda a: a.rearrange("b (h j) -> b h j", j=64)), 'sc')
                P.dma(SVP, scr1[l].v(lambda a: a.rearrange("b h x j -> (b h) x j")), 'sc')
                bi = lambda x: SVP[:, x:x + 1, :].v(lambda a: a.to_broadcast([128, 8, 64]))
                bj = lambda t: t.v(lambda a: a.unsqueeze(2).to_broadcast([128, 8, 64]))
                sw_in = swkv[l].v(lambda a: a.rearrange("b h i j -> (b h) i j"))
                sw_out = owkv_s[l].v(lambda a: a.rearrange("b h i j -> (b h) i j"))
                for pt in range(NPART):
                    isl = slice(pt * 8, pt * 8 + 8)
                    P.dma(Sst_p, sw_in[:, isl, :], 'ld')
                    P.tt('dve', Stmp_p, Sst_p, bi(4), ALU.mult)
                    P.add('dve', lambda E, isl=isl: E.reduce_sum(out=sa[:, isl].ap, in_=Stmp_p.ap, axis=AX.X), [Stmp_p], [sa])
                    P.tt('pool', Sst_p, Sst_p, bi(1), ALU.mult)
                    P.tt('dve', Stmp_p, bj(sa[:, isl]), bi(5), ALU.mult)
                    P.tt('pool', Sst_p, Sst_p, Stmp_p, ALU.add)
                    P.tt('dve', Stmp_p, bj(SVP[:, 3, isl]), bi(2), ALU.mult)
                    P.tt('pool', Sst_p, Sst_p, Stmp_p, ALU.add)
                    P.dma(sw_out[:, isl, :], Sst_p, 'st')
                    P.tt('dve', Stmp_p, Sst_p, bi(0), ALU.mult)
                    P.add('dve', lambda E, isl=isl: E.reduce_sum(out=ysm[:, isl].ap, in_=Stmp_p.ap, axis=AX.X), [Stmp_p], [ysm])
                P.dma(scr2[l].v(lambda a: a.rearrange("b (h i) -> (b h) i", i=64)), ysm, 'sc')
                P.dma(ytok, scr2[l], 'sc')
                b = ps()
                for c in range(4):
                    P.tr(b[:, c * 128: c * 128 + NS], ytok[:, c * 128:(c + 1) * 128], ident[0:NS, 0:NS])
                for c in range(4):
                    P.copy('dve', YA[:, c, 0:n], b[:, c * 128: c * 128 + NS])

            if not samp and l == 0:
                dbg("YA", YA)
                dbg("V0", V[0], BF16)
                dbg("BON0", BON[0])
            for hf in range(2):
                wga = w_get()
                for cc in range(2):
                    c = hf * 2 + cc
                    b = proj(wga, cc * 128)
                    P.act(YAG[c][:, 0:n], b[:, 0:n], AF.Silu)
                w_done(wga)
            for c in range(4):
                y = YA[:, c, 0:n]
                ybf, ysq = tB[0], tB[1]
                P.copy('act', ybf[:, 0:n], y)
                P.act(ysq[:, 0:n], y, AF.Square)
                b1, b2 = ps(), ps()
                P.mm(b1[:, 0:n], blkb, ybf[:, 0:n])
                P.mm(b2[:, 0:n], blkb, ysq[:, 0:n])
                m, msq, var, yc = tA[0], tA[1], tA[2], tA[3]
                P.ts('dve', m[:, 0:n], b1[:, 0:n], 1.0 / 64, None, ALU.mult)
                P.tt('pool', msq[:, 0:n], m[:, 0:n], m[:, 0:n], ALU.mult)
                P.stt('dve', var[:, 0:n], b2[:, 0:n], 1.0 / 64, msq[:, 0:n], ALU.mult, ALU.subtract)
                P.rsq(var[:, 0:n], var[:, 0:n], GN_EPS)
                P.tt('pool', yc[:, 0:n], y, m[:, 0:n], ALU.subtract)
                P.tt('dve', yc[:, 0:n], yc[:, 0:n], var[:, 0:n], ALU.mult)
                P.act(yc[:, 0:n], yc[:, 0:n], AF.Identity, bias=gnb[:, l, c:c + 1], scale=gng[:, l, c:c + 1])
                P.tt('pool', yc[:, 0:n], yc[:, 0:n], BON[c][:, 0:n], ALU.add)
                P.tt('dve', YAG[c][:, 0:n], yc[:, 0:n], YAG[c][:, 0:n], ALU.mult)

            if not samp and l == 0:
                for c in range(4):
                    dbg("YAG%d" % c, YAG[c], BF16)
                dbg("YBG", YBG, BF16)
                dbg("USG", USG)
            for jg in range(4):
                wma = w_get()
                wmb = w_get()
                wba = w_get()
                wbb = w_get()
                for jj in range(2):
                    j = jg * 2 + jj
                    b = proj(wma, jj * 128)
                    sma = tA[0]
                    P.act(sma[:, 0:n], b[:, 0:n], AF.Sigmoid)
                    b = proj(wmb, jj * 128)
                    smb = tA[1]
                    P.act(smb[:, 0:n], b[:, 0:n], AF.Sigmoid)
                    ba = ps()
                    for c in range(4):
                        P.mm(ba[:, 0:n], wba[:, c, jj * 128:(jj + 1) * 128], YAG[c][:, 0:n], start=(c == 0), stop=(c == 3))
                    bb_ = ps()
                    for c in range(4):
                        P.mm(bb_[:, 0:n], wbb[:, c, jj * 128:(jj + 1) * 128], YBG[:, c, 0:n], start=(c == 0), stop=(c == 3))
                    ma, mb = tA[2], tA[3]
                    P.tt('dve', ma[:, 0:n], ba[:, 0:n], sma[:, 0:n], ALU.mult)
                    P.tt('dve', mb[:, 0:n], bb_[:, 0:n], smb[:, 0:n], ALU.mult)
                    P.tt('pool', MG[j][:, 0:n], ma[:, 0:n], mb[:, 0:n], ALU.add)
                w_done(wma, wmb, wba, wbb)
            for jg in range(4):
                wo = w_get()
                for jj in range(2):
                    j = jg * 2 + jj
                    b = ps()
                    for kc in range(8):
                        P.mm(b[:, 0:n], wo[:, kc, jj * 128:(jj + 1) * 128], MG[kc][:, 0:n], start=(kc == 0), stop=(kc == 7))
                    P.tt('dve', hT[j][:, 0:n], hT[j][:, 0:n], b[:, 0:n], ALU.add)
                w_done(wo)
            if not samp and l == 0:
                for k in range(8):
                    dbg("MG%d" % k, MG[k], BF16)
                    dbg("hmid%d" % k, hT[k])
            for tb in range(ntb):
                xt = xst
                src = psm[l] if samp else pp[l, t0 + tb * 128: t0 + (tb + 1) * 128, :]
                P.dma(xt[0:rows, 0:DPLE], src, 'ld')
                b = ps()
                for kc in range(2):
                    P.tr(b[:, kc * 128: kc * 128 + rows], xt[0:rows, kc * 128:(kc + 1) * 128], ident[0:rows, 0:rows])
                for kc in range(2):
                    P.copy('act', pT[kc][:, tb * 128: tb * 128 + rows], b[:, kc * 128: kc * 128 + rows])
            for kc in range(8):
                P.copy('act', xnT[kc][:, 0:n], hT[kc][:, 0:n])
            for jg in range(4):
                wp = w_get()
                wg = w_get()
                for jj in range(2):
                    j = jg * 2 + jj
                    bp = ps()
                    for kc in range(2):
                        P.mm(bp[:, 0:n], wp[:, kc, jj * 128:(jj + 1) * 128], pT[kc][:, 0:n], start=(kc == 0), stop=(kc == 1))
                    bg = ps()
                    for kc in range(8):
                        P.mm(bg[:, 0:n], wg[:, kc, jj * 128:(jj + 1) * 128], xnT[kc][:, 0:n], start=(kc == 0), stop=(kc == 7))
                    sg = tA[0]
                    P.act(sg[:, 0:n], bg[:, 0:n], AF.Sigmoid, bias=bgv[:, l, j:j + 1], scale=1.0)
                    tp_ = tA[1]
                    P.tt('dve', tp_[:, 0:n], bp[:, 0:n], sg[:, 0:n], ALU.mult)
                    P.tt('pool', hT[j][:, 0:n], hT[j][:, 0:n], tp_[:, 0:n], ALU.add)
                w_done(wp, wg)

        if not samp:
            for k in range(8):
                dbg("hend%d" % k, hT[k])
        yT = tA
        rmsnorm(fg32[:, 0, :], yT)
        for tb in range(ntb):
            dst = ys if samp else yp[t0 + tb * 128: t0 + (tb + 1) * 128, :]
            for half in range(2):
                b = ps()
                for kk_ in range(4):
                    kc = half * 4 + kk_
                    P.tr(b[0:rows, kk_ * 128:(kk_ + 1) * 128], yT[kc][:, tb * 128: tb * 128 + rows], ident)
                if samp:
                    P.copy('act' if half else 'dve', xst[0:rows, 0:512], b[0:rows, :])
                    P.dma(dst[:, half * 512:(half + 1) * 512], xst[0:rows, 0:512], 'st')
                else:
                    P.copy('act' if half else 'dve', xst[0:rows, half * 512:(half + 1) * 512], b[0:rows, :])
            if not samp:
                P.dma(dst, xst[0:rows, :], 'st')
        if samp:
            S_list = P.end_side()
            nS = sum(1 for it in S_list if isinstance(it, Op))
            P.auto = S_list
            P.auto_ratio = float(os.environ.get("K_RATIO", nS / (0.8 * (1800.0 + 8.5 * TT))))
            P.auto_acc = 0.0
        elif ti == 0:
            P.pull_s(10 ** 9)
            assert not P.auto
            P.auto = None

    print('sbuf bytes remaining', nc.sbuf_bytes_remaining)
    P.emit()
    return nc, P


_CACHE = {}


def kernel(**inp):
    NCORE = 8
    f = lambda a: np.ascontiguousarray(np.asarray(a, dtype=np.float32))
    if 'nc' not in _CACHE:
        _CACHE['nc'] = build()[0]
    nc = _CACHE['nc']
    wnames = ["norm_g", "w_in", "shift_mu", "w0", "w_up", "a0", "a_up", "vres_down", "vres_up", "vres_b", "k_k", "k_a",
              "gn_g", "gn_b", "ln_v_g", "ln_v_b", "w_spatial", "b_spatial", "w_br_a", "w_br_b", "w_out", "w_ple",
              "w_ple_gate", "b_ple_gate"]
    shared = {k: f(inp[k]) for k in wnames}
    shared["r_k"] = f(inp["r_k"]).reshape(L, DA)
    shared["final_g"] = f(inp["final_g"]).reshape(1, D)
    in_maps = []
    for c in range(NCORE):
        m = dict(shared)
        sl = slice(16 * c, 16 * c + 16)
        m["xp"] = f(inp["x_prompt"][c])
        m["xs"] = f(inp["x_sample"][sl, 0])
        m["swkv"] = f(inp["state_rwkv_wkv"][:, sl])
        m["sshift"] = f(inp["state_rwkv_shift"][:, sl])
        m["pp"] = f(inp["p_prompt"][:, c])
        m["psm"] = f(inp["p_sample"][:, sl, 0])
        in_maps.append(m)
    res = run_bass_kernel_spmd(nc, in_maps, core_ids=list(range(NCORE))).results
    y_prompt = np.stack([res[c]["yp"] for c in range(NCORE)], 0).astype(np.float32)
    y_sample = np.concatenate([res[c]["ys"] for c in range(NCORE)], 0)[:, None, :].astype(np.float32)
    wkv_prompt = np.stack([res[c]["owkv_p"] for c in range(NCORE)], 1).astype(np.float32)
    shift_prompt = np.stack([res[c]["oshift_p"] for c in range(NCORE)], 1).astype(np.float32)
    wkv_sample = np.concatenate([res[c]["owkv_s"] for c in range(NCORE)], 1).astype(np.float32)
    shift_sample = np.concatenate([res[c]["oshift_s"] for c in range(NCORE)], 1).astype(np.float32)
    chunk_v = np.concatenate([res[c]["ocv"] for c in range(NCORE)], 1)[:, :, None, :].astype(np.float32)
    return (y_prompt, y_sample, wkv_prompt, shift_prompt, wkv_sample, shift_sample, chunk_v)
```

```python
import sys
import os
import numpy as np
from contextlib import ExitStack
import concourse.bass as bass
import concourse.mybir as mybir
from concourse.bass_utils import run_bass_kernel_spmd

F32 = mybir.dt.float32
BF16 = mybir.dt.bfloat16
AF = mybir.ActivationFunctionType
ALU = mybir.AluOpType
AX = mybir.AxisListType

D = 1024
DA = 512
NSHIFT = 1664
DIN = 5760
DPLE = 256
L = 2
EPS = 1e-6
GN_EPS = 64e-5
C = 64
LOGW_C = -0.3032653298563167


class Buf:
    def __init__(s, name):
        s.name = name
        s.w = None
        s.r = {}


class T:
    def __init__(s, ap, buf):
        s.ap = ap
        s.buf = buf

    def __getitem__(s, k):
        return T(s.ap[k], s.buf)

    def v(s, f):
        return T(f(s.ap), s.buf)


class Op:
    pass


class Prog:
    def __init__(s, nc):
        s.nc = nc
        s.ops = []
        s.es = ExitStack()
        s.eng = {'pe': nc.tensor, 'act': nc.scalar, 'dve': nc.vector, 'pool': nc.gpsimd, 'sp': nc.sync}
        s.nbuf = 0
        s.rr = {}
        s.side = None
        s.side_kind = None
        s.auto = None
        s._in_auto = False
        s.auto_acc = 0.0
        s.auto_ratio = 0.0
        s.s_done = set()

    def sbuf(s, name, shape, dt, nsplit=None):
        t = s.es.enter_context(s.nc.sbuf_tensor(name, shape, dt))
        return T(t[:], Buf(name))

    def psum(s, name, shape, dt):
        t = s.es.enter_context(s.nc.psum_tensor(name, shape, dt))
        return T(t[:], Buf(name))

    def sub(s, t, name=None):
        s.nbuf += 1
        return T(t.ap, Buf(name or ("b%d" % s.nbuf)))

    def add(s, eng, fn, reads, writes, dma=None, force_main=False):
        op = Op()
        op.eng = eng
        op.fn = fn
        op.dma = dma
        op.sig = False
        op.cnt = 0
        op.reads = reads
        op.writes = writes
        fr = sys._getframe(1)
        wh = []
        while fr is not None and len(wh) < 4:
            wh.append(fr.f_lineno)
            fr = fr.f_back
        op.where = wh
        if s.side is not None and not force_main:
            s.side.append(op)
        else:
            s._append_main(op)
        return op

    def begin_side(s, kind='B'):
        s.side = []
        s.side_kind = kind

    def end_side(s):
        lst = s.side
        s.side = None
        s.side_kind = None
        return lst

    def marker(s, m):
        assert s.side is not None
        s.side.append(m)

    def _append_main(s, op):
        s.ops.append(op)
        if s.auto is not None and not s._in_auto:
            s._in_auto = True
            s.auto_acc += s.auto_ratio
            while s.auto_acc >= 1.0 and s.auto:
                s.auto_acc -= 1.0
                if not s.pull_s(1):
                    break
            s._in_auto = False

    def pull_s(s, k, until_done=None):
        lst = s.auto
        moved = 0
        while lst and (moved < k or (until_done and not until_done <= s.s_done)):
            it = lst[0]
            if isinstance(it, tuple):
                if it[0] == 'get':
                    if s.main_cur() <= it[1]:
                        assert not until_done, "S stream would wait for a slice the main stream has not acquired"
                        return False
                    lst.pop(0)
                else:
                    lst.pop(0)
                    s.s_done |= set(it[1])
                continue
            lst.pop(0)
            if isinstance(it, Op):
                s.ops.append(it)
                moved += 1
            else:
                it()
        return True

    def action(s, fn):
        if s.side is not None:
            s.side.append(fn)
        else:
            fn()

    def pull(s, lst, k):
        while k > 0 and lst:
            it = lst.pop(0)
            if isinstance(it, Op):
                s._append_main(it)
                k -= 1
            else:
                it()

    def analyse(s):
        last_dma = {}
        for idx, op in enumerate(s.ops):
            op.idx = idx
            deps = {}

            def dep(o):
                if o is None:
                    return
                key = o.dma if o.dma else o.eng
                if key not in deps or deps[key].idx < o.idx:
                    deps[key] = o

            def bl(ts_):
                o = []
                for t in ts_:
                    if isinstance(t.buf, (list, tuple)):
                        o.extend(t.buf)
                    else:
                        o.append(t.buf)
                return o
            rb = bl(op.reads)
            wb = bl(op.writes)
            if op.dma:
                dep(last_dma.get(op.dma))
                last_dma[op.dma] = op
            for b in rb:
                dep(b.w)
            for b in wb:
                dep(b.w)
                for o in b.r.values():
                    dep(o)
            mykey = op.dma if op.dma else op.eng
            for b in rb:
                b.r[mykey] = op
            for b in wb:
                b.w = op
                b.r = {}
            op.deps = list(deps.values())

    def emit(s):
        nc = s.nc
        import os
        if os.environ.get('K_STOP'):
            s.ops = s.ops[:int(os.environ['K_STOP'])]
        s.analyse()
        pos = {}
        for op in s.ops:
            op.pos = pos.get(op.eng, 0)
            pos[op.eng] = op.pos + 1
        for op in s.ops:
            op.waits = []
            for d in op.deps:
                if d.dma is None and d.eng == op.eng:
                    if op.eng == 'pe' and op.dma is None:
                        continue
                op.waits.append(d)
                d.sig = True
        cnt = {}
        keys = []
        for op in s.ops:
            k = op.dma if op.dma else op.eng
            if k not in cnt:
                cnt[k] = 0
                keys.append(k)
            if op.dma:
                cnt[k] += 16
                op.cnt = cnt[k]
            elif op.sig:
                cnt[k] += 1
                op.cnt = cnt[k]
        sems = {k: s.es.enter_context(nc.semaphore("s_" + k)) for k in keys}
        waited = {e: {} for e in s.eng}
        for op in s.ops:
            E = s.eng[op.eng]
            for d in op.waits:
                k = d.dma if d.dma else d.eng
                if waited[op.eng].get(k, 0) >= d.cnt:
                    continue
                E.wait_ge(sems[k], d.cnt)
                waited[op.eng][k] = d.cnt
            try:
                ins = op.fn(E)
                if os.environ.get('K_ANNOT'):
                    ins.annotate("W%s" % (op.where,))
            except BaseException:
                print("FAILED op at lines", op.where, op.eng)
                raise
            if op.dma:
                ins.then_inc(sems[op.dma], 16)
            elif op.sig:
                ins.then_inc(sems[op.eng], 1)
        for k in keys:
            if cnt[k] > 0:
                nc.sync.wait_ge(sems[k], cnt[k])

    def dma(s, out, in_, lane, eng='sp', slow=False, nlanes=1, force_main=False):
        kw = {'allow_slow_non_contiguous': True} if slow else {}
        nlanes = {'ld': 4, 'st': 4, 'wl': 2}.get(lane, nlanes)
        if nlanes > 1:
            i = s.rr.get(lane, 0)
            s.rr[lane] = i + 1
            lane = "%s%d" % (lane, i % nlanes)
        return s.add(eng, lambda E: E.dma_start(out=out.ap, in_=in_.ap, **kw), [in_], [out], dma=lane, force_main=force_main)

    def mm(s, out, lhsT, rhs, start=True, stop=True, tp=None, extra_r=()):
        kw = {}
        if tp is not None:
            kw['tile_position'] = tp
        return s.add('pe', lambda E: E.matmul(out.ap, lhsT=lhsT.ap, rhs=rhs.ap, start=start, stop=stop, **kw),
                     [lhsT, rhs] + list(extra_r), [out])

    def tr(s, out, in_, ident):
        return s.add('pe', lambda E: E.transpose(out.ap, in_.ap, ident.ap), [in_, ident], [out])

    def act(s, out, in_, func, bias=None, scale=1.0):
        reads = [in_]
        kw = {}
        if isinstance(bias, T):
            reads.append(bias)
            kw['bias'] = bias.ap
        elif bias is not None:
            kw['bias'] = bias
        if isinstance(scale, T):
            reads.append(scale)
            kw['scale'] = scale.ap
        else:
            kw['scale'] = scale
        return s.add('act', lambda E: E.activation(out=out.ap, in_=in_.ap, func=func, **kw), reads, [out])

    def tt(s, eng, out, in0, in1, op):
        return s.add(eng, lambda E: E.tensor_tensor(out=out.ap, in0=in0.ap, in1=in1.ap, op=op), [in0, in1], [out])

    def ts(s, eng, out, in0, s1, s2, op0, op1=None):
        reads = [in0]
        a1 = s1.ap if isinstance(s1, T) else s1
        a2 = s2.ap if isinstance(s2, T) else s2
        if isinstance(s1, T):
            reads.append(s1)
        if isinstance(s2, T):
            reads.append(s2)
        if op1 is None:
            return s.add(eng, lambda E: E.tensor_single_scalar(out=out.ap, in_=in0.ap, scalar=a1, op=op0), reads, [out])
        return s.add(eng, lambda E: E.tensor_scalar(out=out.ap, in0=in0.ap, scalar1=a1, scalar2=a2, op0=op0, op1=op1),
                     reads, [out])

    def stt(s, eng, out, in0, sc, in1, op0, op1):
        reads = [in0, in1]
        a = sc.ap if isinstance(sc, T) else sc
        if isinstance(sc, T):
            reads.append(sc)
        return s.add(eng, lambda E: E.scalar_tensor_tensor(out=out.ap, in0=in0.ap, scalar=a, in1=in1.ap, op0=op0, op1=op1),
                     reads, [out])

    def rsq(s, out, in_, eps):
        s.act(out, in_, AF.Ln, bias=s.epsc[eps][0:out.ap.shape[0]], scale=1.0)
        s.act(out, out, AF.Exp, scale=-0.5)

    def copy(s, eng, out, in_):
        if eng == 'act':
            return s.add('act', lambda E: E.activation(out=out.ap, in_=in_.ap, func=AF.Copy), [in_], [out])
        return s.add(eng, lambda E: E.tensor_copy(out=out.ap, in_=in_.ap), [in_], [out])

    def memset(s, eng, out, val):
        return s.add(eng, lambda E: E.memset(out.ap, val), [], [out])


def build(SEQ_T=2048, TT=512, NS=16, DEBUG=False):
    nc = bass.Bass("TRN2", target_bir_lowering=False)
    P = Prog(nc)
    dbg_seen = set()

    def dbg(name, t, dt=F32):
        if not DEBUG or name in dbg_seen:
            return
        dbg_seen.add(name)
        shape = list(t.ap.shape)
        o = T(nc.dram_tensor("dbg_" + name, shape, dt, kind="ExternalOutput").ap(), Buf("dbg_" + name))
        P.dma(o, t, 'dbg', eng='pool')
    NT = SEQ_T // TT
    NCH = TT // C
    NTB = TT // 128

    def din(name, shape):
        return T(nc.dram_tensor(name, list(shape), F32, kind="ExternalInput").ap(), Buf(name))

    def dout(name, shape):
        return T(nc.dram_tensor(name, list(shape), F32, kind="ExternalOutput").ap(), Buf(name))

    xp = din("xp", [SEQ_T, D])
    xs = din("xs", [NS, D])
    swkv = din("swkv", [L, NS, 8, 64, 64])
    sshift = din("sshift", [L, NS, NSHIFT])
    pp = din("pp", [L, SEQ_T, DPLE])
    psm = din("psm", [L, NS, DPLE])
    norm_g = din("norm_g", [L, D])
    w_in = din("w_in", [L, D, DIN])
    shift_mu = din("shift_mu", [L, NSHIFT])
    w0 = din("w0", [L, DA])
    w_up = din("w_up", [L, 64, DA])
    a0 = din("a0", [L, DA])
    a_up = din("a_up", [L, 64, DA])
    vres_down = din("vres_down", [1, D, 32])
    vres_up = din("vres_up", [1, 32, DA])
    vres_b = din("vres_b", [1, DA])
    k_k = din("k_k", [L, DA])
    k_a = din("k_a", [L, DA])
    r_k = din("r_k", [L, DA])
    gn_g = din("gn_g", [L, DA])
    gn_b = din("gn_b", [L, DA])
    ln_v_g = din("ln_v_g", [L, 512])
    ln_v_b = din("ln_v_b", [L, 512])
    w_spatial = din("w_spatial", [L, 4, 128, 128])
    b_spatial = din("b_spatial", [L, 4, 128])
    w_br_a = din("w_br_a", [L, DA, D])
    w_br_b = din("w_br_b", [L, 512, D])
    w_out = din("w_out", [L, D, D])
    w_ple = din("w_ple", [L, DPLE, D])
    w_ple_gate = din("w_ple_gate", [L, D, D])
    b_ple_gate = din("b_ple_gate", [L, D])
    final_g = din("final_g", [1, D])

    yp = dout("yp", [SEQ_T, D])
    ys = dout("ys", [NS, D])
    owkv_p = dout("owkv_p", [L, 8, 64, 64])
    oshift_p = dout("oshift_p", [L, NSHIFT])
    owkv_s = dout("owkv_s", [L, NS, 8, 64, 64])
    oshift_s = dout("oshift_s", [L, NS, NSHIFT])
    ocv = dout("ocv", [L, NS, 512])

    scr1 = T(nc.dram_tensor("scr1", [L, NS, 8, 6, 64], F32, kind="ExternalOutput").ap(), Buf("scr1"))
    scr2 = T(nc.dram_tensor("scr2", [L, NS, 512], F32, kind="ExternalOutput").ap(), Buf("scr2"))

    ident = P.sbuf("ident", [128, 128], F32)
    identb = P.sbuf("identb", [128, 128], BF16)
    onesb = P.sbuf("onesb", [128, 128], BF16)
    blkb = P.sbuf("blkb", [128, 128], BF16)
    maskscan = P.sbuf("maskscan", [128, TT], F32)
    maskc = P.sbuf("maskc", [128, 128], F32)

    epst = P.sbuf("epst", [128, 4], F32)
    P.epsc = {}
    for i, e in enumerate([D * EPS, 1e-24, GN_EPS, 1e-5]):
        P.memset('pool', epst[:, i:i + 1], e)
        P.epsc[e] = epst[:, i:i + 1]
    P.memset('pool', ident, 0.0)
    P.add('pool', lambda E: E.affine_select(out=ident.ap, in_=ident.ap, pattern=[[-1, 128]], compare_op=ALU.not_equal,
                                            fill=1.0, base=0, channel_multiplier=1), [ident], [ident])
    P.copy('pool', identb, ident)
    P.memset('pool', onesb, 1.0)
    P.memset('pool', blkb, 0.0)
    P.memset('pool', blkb[0:64, 0:64], 1.0)
    P.memset('pool', blkb[64:128, 64:128], 1.0)
    P.memset('pool', maskscan, 1.0)
    P.memset('pool', maskscan.v(lambda a: a.rearrange("p (c t) -> p c t", t=C)[:, :, 0:1]), 0.0)
    P.memset('pool', maskc, 1.0)
    P.add('pool', lambda E: E.affine_select(out=maskc.ap, in_=maskc.ap, pattern=[[1, 128]], compare_op=ALU.is_ge,
                                            fill=0.0, base=0, channel_multiplier=-1), [maskc], [maskc])
    def build_mask128(name, shape, pat, cm, base):
        a = P.sbuf(name, [128] + shape, F32)
        tmp = P.sbuf(name + "_t", [128] + shape, F32)
        P.memset('pool', a, 1.0)
        P.memset('pool', tmp, 1.0)
        P.add('pool', lambda E: E.affine_select(out=a.ap, in_=a.ap, pattern=pat, compare_op=ALU.is_ge, fill=0.0,
                                                base=base, channel_multiplier=cm), [a], [a])
        P.add('pool', lambda E: E.affine_select(out=tmp.ap, in_=tmp.ap, pattern=pat, compare_op=ALU.is_ge, fill=0.0,
                                                base=base - 64 * cm, channel_multiplier=cm), [tmp], [tmp])
        P.copy('pool', a[64:128], tmp[64:128])
        return a
    m_strict = build_mask128("m_strict", [64], [[1, 64]], -1, -1)
    m_incl = build_mask128("m_incl", [64], [[1, 64]], -1, 0)
    mask_si1 = P.sbuf("mask_si", [128, 2, 64], F32)
    P.copy('pool', mask_si1[:, 0, :], m_strict)
    P.copy('pool', mask_si1[:, 1, :], m_incl)
    mask_l1 = build_mask128("mask_l", [64], [[-1, 64]], 1, -1)
    id2 = P.sbuf("id2", [128, 64], F32)
    P.tt('pool', id2, ident[:, 0:64], ident[:, 64:128], ALU.add)
    mask_si = mask_si1.v(lambda a: a.rearrange("p a b -> p (a b)").unsqueeze(1).to_broadcast([128, 4, 128]))
    mask_l4 = mask_l1.v(lambda a: a.unsqueeze(1).to_broadcast([128, 4, 64]))
    ident8 = id2.v(lambda a: a.unsqueeze(1).to_broadcast([128, 8, 64]))

    def load_vec(name, src, nl, ncol):
        t = P.sbuf(name, [128, nl, ncol], F32)
        P.dma(t, src.v(lambda a: a.rearrange("l (c p) -> p l c", p=128)), 'ld', slow=True)
        return t

    g32 = load_vec("g32", norm_g, L, 8)
    fg32 = load_vec("fg32", final_g, 1, 8)
    mu = load_vec("mu", shift_mu, L, 13)
    hw0 = load_vec("hw0", w0, L, 4)
    a0v = load_vec("a0v", a0, L, 4)
    vbv = load_vec("vbv", vres_b, 1, 4)
    kkv = load_vec("kkv", k_k, L, 4)
    kav = load_vec("kav", k_a, L, 4)
    rkv = load_vec("rkv", r_k, L, 4)
    gng = load_vec("gng", gn_g, L, 4)
    gnb = load_vec("gnb", gn_b, L, 4)
    bgv = load_vec("bgv", b_ple_gate, L, 8)
    omka = P.sbuf("omka", [128, L, 4], F32)
    P.ts('pool', g32, g32, 32.0, None, ALU.mult)
    P.ts('pool', fg32, fg32, 32.0, None, ALU.mult)
    P.ts('pool', hw0, hw0, 0.5, None, ALU.mult)
    ha0 = P.sbuf("ha0", [128, L, 4], F32)
    hvb = P.sbuf("hvb", [128, 1, 4], F32)
    P.ts('pool', ha0, a0v, 0.5, None, ALU.mult)
    P.ts('pool', hvb, vbv, 0.5, None, ALU.mult)
    P.ts('pool', omka, kav, -1.0, 1.0, ALU.mult, ALU.add)

    lngb = P.sbuf("lngb", [128, L, 512], F32)
    lnbb = P.sbuf("lnbb", [128, L, 512], F32)
    for l in range(L):
        P.dma(lngb[:, l, :], ln_v_g[l:l + 1, :].v(lambda a: a.partition_broadcast(128)), 'ld')
        P.dma(lnbb[:, l, :], ln_v_b[l:l + 1, :].v(lambda a: a.partition_broadcast(128)), 'ld')
    ws00 = P.sbuf("ws00", [128, L, 4], F32)
    bs0 = P.sbuf("bs0", [128, L, 4], F32)
    P.dma(ws00, w_spatial.v(lambda a: a[:, :, 0, 0:1].rearrange("l g o -> o l g").partition_broadcast(128)), 'ld', slow=True)
    P.dma(bs0, b_spatial.v(lambda a: a[:, :, 0:1].rearrange("l g o -> o l g").partition_broadcast(128)), 'ld', slow=True)
    bsb = P.sbuf("bsb", [1, L * 4 * 128], BF16)
    P.dma(bsb, b_spatial.v(lambda a: a.rearrange("(o l) g t -> o (l g t)", o=1)), 'wl', eng='pool')
    lup = P.sbuf("lup", [128, L, 512], BF16)
    for l in range(L):
        P.dma(lup[0:64, l, :], w_up[l], 'wl', eng='pool')
        P.dma(lup[64:128, l, :], a_up[l], 'wl', eng='pool')
    vdb = P.sbuf("vdb", [128, 8, 32], BF16)
    P.dma(vdb, vres_down[0].v(lambda a: a.rearrange("(kc p) c -> p kc c", p=128)), 'wl', eng='pool')
    vub = P.sbuf("vub", [32, 512], BF16)
    P.dma(vub, vres_up[0], 'wl', eng='pool')

    banks = [P.psum("bank%d" % i, [128, 512], F32) for i in range(8)]
    bank_i = [0]

    sbank_i = [0]
    ssbank_i = [0]

    def ps():
        if P.side_kind == 'S':
            b = banks[6 + ssbank_i[0] % 2]
            ssbank_i[0] += 1
            return b
        if P.side_kind == 'B':
            b = banks[4 + sbank_i[0] % 2]
            sbank_i[0] += 1
            return b
        b = banks[bank_i[0] % 4]
        bank_i[0] += 1
        return b

    wmT = P.sbuf("wmT", [128, L, 4, 128], BF16)
    gtok = P.sbuf("gtok", [128, 512], F32)
    wstage = gtok.v(lambda a: a.rearrange("p (g s) -> p g s", s=128))
    for l in range(L):
        P.dma(wstage, w_spatial[l].v(lambda a: a.rearrange("g t s -> t g s")), 'ld')
        b = ps()
        for g in range(4):
            P.tr(b[:, g * 128:(g + 1) * 128], wstage[:, g, :], ident)
        for g in range(4):
            P.tt('dve', wmT[:, l, g, :], b[:, g * 128:(g + 1) * 128], maskc, ALU.mult)

    NSLOT = 9
    wslots = [P.sbuf("wslot%d" % i, [128, 8 * 256], BF16) for i in range(NSLOT)]
    wlist = []

    def wsrc(src2d, kcs, c0, ncols):
        return src2d.v(lambda a: a.rearrange("(kc p) c -> p kc c", p=128)[:, :, c0:c0 + ncols]), kcs, ncols

    tiles = [('s', 0)] + [('p', i) for i in range(NT)]
    for (kind, ti) in tiles[1:]:
        for l in range(L):
            wlist.append(wsrc(w_in[l], 8, 1536, 128))
            for c in range(4):
                for x in range(3):
                    wlist.append(wsrc(w_in[l], 8, x * 512 + c * 128, 128))
            for hf in range(2):
                wlist.append(wsrc(w_in[l], 8, 2688 + hf * 256, 256))
            for hf in range(2):
                wlist.append(wsrc(w_in[l], 8, 2176 + hf * 256, 256))
            for hf in range(2):
                wlist.append(wsrc(w_in[l], 8, 3200 + hf * 256, 256))
            for hf in range(2):
                wlist.append(wsrc(w_in[l], 8, 1664 + hf * 256, 256))
            for jg in range(4):
                wlist.append(wsrc(w_in[l], 8, 3712 + jg * 256, 256))
                wlist.append(wsrc(w_in[l], 8, 4736 + jg * 256, 256))
                wlist.append(wsrc(w_br_a[l], 4, jg * 256, 256))
                wlist.append(wsrc(w_br_b[l], 4, jg * 256, 256))
            for jg in range(4):
                wlist.append(wsrc(w_out[l], 8, jg * 256, 256))
            for jg in range(4):
                wlist.append(wsrc(w_ple[l], 2, jg * 256, 256))
                wlist.append(wsrc(w_ple_gate[l], 8, jg * 256, 256))
    wcur = [0]
    wissued = [0]
    wdone = set()

    def w_issue_ready():
        while wissued[0] < len(wlist):
            i = wissued[0]
            if i >= NSLOT and (i - NSLOT) not in wdone:
                return
            src, kcs, ncols = wlist[i]
            slot = wslots[i % NSLOT]
            dst = slot[:, 0:kcs * ncols].v(lambda a: a.rearrange("p (kc c) -> p kc c", c=ncols))
            P.dma(dst, src, 'ws%d' % (i % NSLOT), eng='pool', force_main=True)
            wissued[0] += 1

    class WS:
        pass

    s_cur = [0]
    P.main_cur = lambda: wcur[0]

    def w_get():
        if P.side_kind == 'S':
            i = s_cur[0]
            s_cur[0] += 1
            src, kcs, ncols = wlist[i]
            slot = wslots[i % NSLOT]
            t = slot[:, 0:kcs * ncols].v(lambda a: a.rearrange("p (kc c) -> p kc c", c=ncols))
            t.widx = i
            P.marker(('get', i))
            return t
        i = wcur[0]
        w_issue_ready()
        assert wissued[0] > i, "weight slot pool too small"
        src, kcs, ncols = wlist[i]
        slot = wslots[i % NSLOT]
        wcur[0] += 1
        t = slot[:, 0:kcs * ncols].v(lambda a: a.rearrange("p (kc c) -> p kc c", c=ncols))
        t.widx = i
        return t

    per_tile_slices = len(wlist) // NT

    def w_done(*ts_):
        if P.side_kind == 'S':
            P.marker(('done', [t.widx for t in ts_]))
            return

        def go():
            idxs = set(t.widx for t in ts_)
            if P.auto is not None and max(idxs) < per_tile_slices:
                P.pull_s(0, until_done=idxs)
            for t in ts_:
                wdone.add(t.widx)
            w_issue_ready()
        P.action(go)

    hT = [P.sbuf("hT%d" % k, [128, TT], F32) for k in range(8)]
    xnT = [P.sbuf("xnT%d" % k, [128, TT], BF16) for k in range(8)]
    sqb = [P.sbuf("sqb0", [128, TT], BF16)] * 2
    rstd = P.sbuf("rstd", [128, TT], F32)
    pT = [P.sbuf("pT%d" % k, [128, TT], BF16) for k in range(2)]
    xtok = [P.sbuf("xtok%d" % i, [128, D], F32) for i in range(2)]
    zraw = [P.sbuf("zraw0", [128, TT + 1], F32)] * 2
    zraw_i = [0]
    carry = [P.sbuf("carry%d" % l, [128, 13], F32) for l in range(L)]
    zpT = P.sbuf("zpT", [128, 13, NS], F32)
    twb = P.sbuf("twb", [64, TT], BF16)
    adb = P.sbuf("adb", [128, TT], BF16)
    zr = P.sbuf("zr", [128, TT], F32)
    zk = P.sbuf("zk", [128, TT], F32)
    V = [P.sbuf("V%d" % c, [128, TT], BF16) for c in range(4)]
    VF = [P.sbuf("VF%d" % c, [128, TT], BF16) for c in range(4)]
    xvd = P.sbuf("xvd", [32, TT], BF16)
    tA = [P.sbuf("tA%d" % i, [128, TT], F32) for i in range(8)]
    tB = [P.sbuf("tB%d" % i, [128, TT], BF16) for i in range(2)]
    zsl = tA[6]
    ar1 = P.es.enter_context(nc.sbuf_tensor("arena1", [128, 4096], F32))
    bufAR, bufBK = Buf("AR"), Buf("BK")
    ARh = ar1[:, 0:2048].bitcast(BF16)
    BKh = ar1[:, 2048:4096].bitcast(BF16)
    c4v = lambda a: a.rearrange("p (j a b) -> p j a b", a=2, b=C)
    AR = [T(c4v(ARh[:, c * 2 * TT:(c + 1) * 2 * TT]), bufAR) for c in range(4)]
    BK = [T(c4v(BKh[:, c * 2 * TT:(c + 1) * 2 * TT]), bufBK) for c in range(4)]
    MG = [T(ARh[:, k * TT:(k + 1) * TT], bufAR) for k in range(8)]
    YAG = [T(BKh[:, c * TT:(c + 1) * TT], bufBK) for c in range(4)]
    YBG = P.sbuf("YBG", [128, 4, TT], BF16)
    WC = P.sbuf("WC", [128, 4, NCH], F32)
    BON = [P.sbuf("BON%d" % c, [128, TT], BF16) for c in range(4)]
    YA = P.sbuf("YA", [128, 4, TT], F32)
    USG = P.sbuf("USG", [128, 4, TT], BF16)
    VN = [P.sbuf("VN%d" % i, [128, 512], BF16) for i in range(max(NTB, 1))]
    Hf = [P.sbuf("Hf%d" % l, [128, 4, 64], F32) for l in range(L)]
    Hb = [P.sbuf("Hb%d" % l, [128, 4, 64], BF16) for l in range(L)]
    ar2 = P.es.enter_context(nc.sbuf_tensor("arena2", [128, 4096], F32))
    a2off = [0]
    a2bufs = []

    def a2(name, words, dt, shape_fn=None):
        o = a2off[0]
        a2off[0] += words
        assert a2off[0] <= 4096
        ap = ar2[:, o:o + words]
        if dt == BF16:
            ap = ap.bitcast(BF16)
        if shape_fn is not None:
            ap = shape_fn(ap)
        b_ = Buf(name)
        a2bufs.append(b_)
        return T(ap, b_)
    h8 = lambda a: a.rearrange("p (h x) -> p h x", h=8)
    g8 = lambda a: a.rearrange("p (h a b) -> p h a b", h=8, a=2)
    Vt = a2("Vt", 256, BF16)
    Bt = a2("Bt", 256, BF16)
    Kt = a2("Kt", 256, BF16)
    G1s = a2("G1s", 512, BF16, g8)
    G2s = a2("G2s", 512, BF16, g8)
    PQ = [a2("PQ%d" % i, 256, BF16, h8) for i in range(6)]
    TtB = a2("TtB", 256, BF16, h8)
    Xs = a2("Xs", 256, BF16)
    Us = a2("Us", 256, BF16)
    SV = P.sbuf("SV", [128, 4, 6, NS], F32)
    TOKV = [P.sbuf("TOKV0", [NS, 512], F32)] * 2
    SVP = P.sbuf("SVP", [128, 6, 64], F32)
    sa = P.sbuf("sa", [128, 64], F32)
    ysm = P.sbuf("ysm", [128, 64], F32)
    ytok = TOKV[0]
    outtok = [xtok[0]]
    small = P.sbuf("small", [128, 8], F32)
    gtok2 = P.sbuf("gtok2", [128, 512], F32)
    vnf = gtok2[0:NS, :]
    vnfm = P.sbuf("vnfm", [128, 4, NS], F32)

    prm = dict(hT=hT, xnT=xnT, sqb=sqb, rstd=rstd, pT=pT, zraw=zraw, zsl=zsl, twb=twb, adb=adb, zr=zr, zk=zk, V=V, VF=VF,
               xvd=xvd, tA=tA, tB=tB, BON=BON, YA=YA, USG=USG, YBG=YBG, YAG=YAG, MG=MG, small=small, gtok=gtok, gtok2=gtok2)
    S_hT = [P.sbuf("S_hT%d" % k, [128, NS], F32) for k in range(8)]
    S_xnT = [P.sbuf("S_xnT%d" % k, [128, NS], BF16) for k in range(8)]
    S_tA = [P.sbuf("S_tA%d" % i, [128, NS], F32) for i in range(8)]
    S_tB = [P.sbuf("S_tB%d" % i, [128, NS], BF16) for i in range(2)]
    smp = dict(hT=S_hT, xnT=S_xnT, sqb=[P.sbuf("S_sqb", [128, NS], BF16)] * 2, rstd=P.sbuf("S_rstd", [128, NS], F32),
               pT=[P.sbuf("S_pT%d" % k, [128, NS], BF16) for k in range(2)],
               zraw=[P.sbuf("S_zraw", [128, NS + 1], F32)] * 2, zsl=S_tA[6],
               twb=P.sbuf("S_twb", [64, NS], BF16), adb=P.sbuf("S_adb", [128, NS], BF16),
               zr=P.sbuf("S_zr", [128, NS], F32), zk=P.sbuf("S_zk", [128, NS], F32),
               V=[P.sbuf("S_V%d" % c, [128, NS], BF16) for c in range(4)],
               VF=[P.sbuf("S_VF%d" % c, [128, NS], BF16) for c in range(4)],
               xvd=P.sbuf("S_xvd", [32, NS], BF16), tA=S_tA, tB=S_tB,
               BON=[P.sbuf("S_BON%d" % c, [128, NS], BF16) for c in range(4)],
               YA=P.sbuf("S_YA", [128, 4, NS], F32), USG=P.sbuf("S_USG", [128, 4, NS], BF16),
               YBG=P.sbuf("S_YBG", [128, 4, NS], BF16),
               YAG=[P.sbuf("S_YAG%d" % c, [128, NS], BF16) for c in range(4)],
               MG=[P.sbuf("S_MG%d" % k, [128, NS], BF16) for k in range(8)],
               small=P.sbuf("S_small", [128, 8], F32), gtok=TOKV[0], gtok2=P.sbuf("S_gtok2", [NS, 512], F32))
    P.memset('pool', smp['small'], 0.0)
    NPART = 8
    xt1 = xtok[1].ap
    Sst_p = T(xt1[:, 0:512].rearrange("p (i j) -> p i j", j=64), Buf("Sst_p"))
    Stmp_p = T(xt1[:, 512:1024].rearrange("p (i j) -> p i j", j=64), Buf("Stmp_p"))
    for l in range(L):
        P.memset('pool', Hf[l], 0.0)
        P.memset('pool', Hb[l], 0.0)
        P.memset('pool', carry[l], 0.0)

    def bc_last(t, shape):
        return t.v(lambda a: a.to_broadcast(shape))

    for (kind, ti) in tiles:
        samp = (kind == 's')
        bs_ = smp if samp else prm
        hT, xnT, sqb, rstd, pT, zraw, zsl, twb, adb, zr, zk, V, VF = (bs_[k_] for k_ in (
            'hT', 'xnT', 'sqb', 'rstd', 'pT', 'zraw', 'zsl', 'twb', 'adb', 'zr', 'zk', 'V', 'VF'))
        xvd, tA, tB, BON, YA, USG, YBG, YAG, MG, small, gtok, gtok2 = (bs_[k_] for k_ in (
            'xvd', 'tA', 'tB', 'BON', 'YA', 'USG', 'YBG', 'YAG', 'MG', 'small', 'gtok', 'gtok2'))
        vnf = gtok2[0:NS, :]
        SW = 512 if samp else 1024
        xst = TOKV[0] if samp else xtok[0]
        if samp:
            P.begin_side('S')
        n = NS if samp else TT
        t0 = ti * TT
        ntb = 1 if samp else NTB
        rows = NS if samp else 128

        for tb in range(ntb):
            for pc in range(D // SW):
                src = xs if samp else xp[t0 + tb * 128: t0 + (tb + 1) * 128, :]
                P.dma(xst[0:rows, 0:SW], src[:, pc * SW:(pc + 1) * SW], 'ld')
                for half in range(SW // 512):
                    b = ps()
                    for kk_ in range(4):
                        P.tr(b[:, kk_ * 128: kk_ * 128 + rows], xst[0:rows, (half * 4 + kk_) * 128:(half * 4 + kk_ + 1) * 128],
                             ident[0:rows, 0:rows])
                    for kk_ in range(4):
                        kc = pc * (SW // 128) + half * 4 + kk_
                        P.copy('dve', hT[kc][:, tb * 128: tb * 128 + rows], b[:, kk_ * 128: kk_ * 128 + rows])
        if samp:
            pass

        def rmsnorm(gv, outs, f32out=False):
            b = ps()
            for kc in range(8):
                sq = sqb[kc % 2]
                P.act(sq[:, 0:n], hT[kc][:, 0:n], AF.Square)
                P.mm(b[:, 0:n], onesb, sq[:, 0:n], start=(kc == 0), stop=(kc == 7))
            P.rsq(rstd[:, 0:n], b[:, 0:n], D * EPS)
            for kc in range(8):
                P.stt('dve', outs[kc][:, 0:n], hT[kc][:, 0:n], gv[:, kc:kc + 1], rstd[:, 0:n], ALU.mult, ALU.mult)

        def proj(wt, col0, ncol_chunks=1, M=128):
            b = ps()
            for kc in range(8):
                P.mm(b[0:M, 0:n], wt[:, kc, col0:col0 + M], xnT[kc][:, 0:n], start=(kc == 0), stop=(kc == 7))
            return b

        for l in range(L):
            rmsnorm(g32[:, l, :], xnT)

            def shifted(b, f, dst):
                zz = zraw[zraw_i[0] % 2]
                zraw_i[0] += 1
                P.copy('act', zz[:, 1:n + 1], b[:, 0:n])
                if samp:
                    prev = zpT[:, f, :]
                    P.dma(oshift_s[l].v(lambda a: a[:, f * 128:(f + 1) * 128].rearrange("b p -> p b")), zz[:, 1:n + 1], 'st', slow=True)
                else:
                    P.copy('pool', zz[:, 0:1], carry[l][:, f:f + 1])
                    prev = zz[:, 0:n]
                    P.copy('pool', carry[l][:, f:f + 1], zz[:, n:n + 1])
                d = tA[7]
                P.tt('pool', d[:, 0:n], prev, zz[:, 1:n + 1], ALU.subtract)
                P.stt('dve', dst[:, 0:n], d[:, 0:n], mu[:, l, f:f + 1], zz[:, 1:n + 1], ALU.mult, ALU.add)

            if samp:
                for f0 in range(0, 13, 4):
                    fs = list(range(f0, min(13, f0 + 4)))
                    P.dma(xst[0:NS, 0:len(fs) * 128], sshift[l][:, f0 * 128:(f0 + len(fs)) * 128], 'ld')
                    b = ps()
                    for i, f in enumerate(fs):
                        P.tr(b[:, i * 128: i * 128 + NS], xst[0:NS, i * 128:(i + 1) * 128], ident[0:NS, 0:NS])
                    for i, f in enumerate(fs):
                        P.copy('dve', zpT[:, f, :], b[:, i * 128: i * 128 + NS])

            wt = w_get()
            b = proj(wt, 0)
            shifted(b, 12, zsl)
            w_done(wt)
            P.act(twb[0:64, 0:n], zsl[0:64, 0:n], AF.Tanh)
            P.copy('act', adb[64:128, 0:n], zsl[64:128, 0:n])
            if l == 1:
                b = ps()
                for kc in range(8):
                    P.mm(b[0:32, 0:n], vdb[:, kc, :], xnT[kc][:, 0:n], start=(kc == 0), stop=(kc == 7))
                P.copy('act', xvd[0:32, 0:n], b[0:32, 0:n])

            for c in range(4):
                wr = w_get()
                b = proj(wr, 0)
                shifted(b, c, zr)
                w_done(wr)
                wk = w_get()
                b = proj(wk, 0)
                shifted(b, 4 + c, zk)
                w_done(wk)
                wv = w_get()
                b = proj(wv, 0)
                shifted(b, 8 + c, V[c])
                w_done(wv)
                lw, av, kk, t1, kmod, bbv, cs, ex = tA[0], tA[1], tA[2], tA[3], tA[4], tA[5], tA[6], tA[7]
                b = ps()
                P.mm(b[:, 0:n], lup[0:64, l, c * 128:(c + 1) * 128], twb[0:64, 0:n])
                P.act(lw[:, 0:n], b[:, 0:n], AF.Tanh, bias=hw0[:, l, c:c + 1], scale=0.5)
                P.ts('dve', lw[:, 0:n], lw[:, 0:n], LOGW_C, LOGW_C, ALU.mult, ALU.add)
                b = ps()
                P.mm(b[:, 0:n], lup[64:128, l, c * 128:(c + 1) * 128], adb[64:128, 0:n])
                P.act(av[:, 0:n], b[:, 0:n], AF.Tanh, bias=ha0[:, l, c:c + 1], scale=0.5)
                P.ts('dve', av[:, 0:n], av[:, 0:n], 0.5, 0.5, ALU.mult, ALU.add)
                if l == 0:
                    P.copy('act', VF[c][:, 0:n], V[c][:, 0:n])
                else:
                    b = ps()
                    P.mm(b[:, 0:n], vub[0:32, c * 128:(c + 1) * 128], xvd[0:32, 0:n])
                    sg = tA[6]
                    P.act(sg[:, 0:n], b[:, 0:n], AF.Tanh, bias=hvb[:, 0, c:c + 1], scale=0.5)
                    P.ts('dve', sg[:, 0:n], sg[:, 0:n], 0.5, 0.5, ALU.mult, ALU.add)
                    dd = tA[7]
                    P.tt('pool', dd[:, 0:n], VF[c][:, 0:n], V[c][:, 0:n], ALU.subtract)
                    P.tt('dve', dd[:, 0:n], dd[:, 0:n], sg[:, 0:n], ALU.mult)
                    P.tt('pool', V[c][:, 0:n], V[c][:, 0:n], dd[:, 0:n], ALU.add)
                sq = tB[0]
                P.act(sq[:, 0:n], zk[:, 0:n], AF.Square, scale=kkv[:, l, c:c + 1])
                b = ps()
                P.mm(b[:, 0:n], blkb, sq[:, 0:n])
                rn = tA[3]
                P.rsq(rn[:, 0:n], b[:, 0:n], 1e-24)
                P.stt('dve', kk[:, 0:n], zk[:, 0:n], kkv[:, l, c:c + 1], rn[:, 0:n], ALU.mult, ALU.mult)
                P.act(t1[:, 0:n], av[:, 0:n], AF.Identity, bias=omka[:, l, c:c + 1], scale=kav[:, l, c:c + 1])
                P.tt('pool', kmod[:, 0:n], zk[:, 0:n], t1[:, 0:n], ALU.mult)
                P.tt('pool', bbv[:, 0:n], kk[:, 0:n], av[:, 0:n], ALU.mult)
                rkr = tB[1]
                P.stt('dve', rkr[:, 0:n], zr[:, 0:n], rkv[:, l, c:c + 1], kmod[:, 0:n], ALU.mult, ALU.mult)
                b = ps()
                P.mm(b[:, 0:n], blkb, rkr[:, 0:n])
                P.tt('dve', BON[c][:, 0:n], b[:, 0:n], V[c][:, 0:n], ALU.mult)
                if not samp:
                    P.add('dve', lambda E, cs=cs, lw=lw: E.tensor_tensor_scan(out=cs.ap, data0=maskscan.ap, data1=lw.ap, initial=0.0,
                                                                              op0=ALU.mult, op1=ALU.add), [maskscan, lw], [cs])
                    P.tt('pool', ex, cs, lw, ALU.subtract)
                    epos, eneg, eex = tA[3], tA[1], tA[7]
                    P.act(epos, cs, AF.Exp)
                    P.act(eex, ex, AF.Exp)
                    P.act(eneg, cs, AF.Exp, scale=-1.0)
                    c3 = lambda a: a.rearrange("p (c t) -> p c t", t=C)
                    P.tt('dve', AR[c][:, :, 1, :], zr.v(c3), epos.v(c3), ALU.mult)
                    P.stt('dve', AR[c][:, :, 0, :], kk.v(c3), -1.0, eex.v(c3), ALU.mult, ALU.mult)
                    P.tt('pool', BK[c][:, :, 1, :], kmod.v(c3), eneg.v(c3), ALU.mult)
                    P.tt('pool', BK[c][:, :, 0, :], bbv.v(c3), eneg.v(c3), ALU.mult)
                    P.copy('act', WC[:, c, :], epos.v(c3)[:, :, C - 1])
                else:
                    P.copy('pool', SV[:, c, 0, :], zr[:, 0:n])
                    P.act(SV[:, c, 1, :], lw[:, 0:n], AF.Exp)
                    P.copy('pool', SV[:, c, 2, :], kmod[:, 0:n])
                    P.copy('pool', SV[:, c, 3, :], V[c][:, 0:n])
                    P.ts('pool', SV[:, c, 4, :], kk[:, 0:n], -1.0, None, ALU.mult)
                    P.copy('pool', SV[:, c, 5, :], bbv[:, 0:n])

            wv_ = {}

            def vb_blocks(tbs):
                for tb in tbs:
                    b = ps()
                    for hf, wvb in enumerate((wv_['0'], wv_['1'])):
                        for kc in range(8):
                            P.mm(b[0:rows, hf * 256:(hf + 1) * 256], xnT[kc][:, tb * 128: tb * 128 + rows], wvb[:, kc, :],
                                 start=(kc == 0), stop=(kc == 7))
                    g = gtok
                    P.act(g[0:rows, :], b[0:rows, :], AF.Gelu_apprx_tanh)
                    P.add('dve', lambda E, g=g, rows=rows, small=small: E.reduce_sum(out=small[0:rows, 0:1].ap, in_=g[0:rows, :].ap, axis=AX.X), [g], [small])
                    P.act(gtok2[0:rows, :], g[0:rows, :], AF.Square)
                    P.add('dve', lambda E, rows=rows, small=small, gtok2=gtok2: E.reduce_sum(out=small[0:rows, 1:2].ap, in_=gtok2[0:rows, :].ap, axis=AX.X), [gtok2], [small])
                    P.ts('dve', small[0:rows, 2:3], small[0:rows, 0:1], 1.0 / 512, None, ALU.mult)
                    P.tt('dve', small[0:rows, 3:4], small[0:rows, 2:3], small[0:rows, 2:3], ALU.mult)
                    P.stt('dve', small[0:rows, 4:5], small[0:rows, 1:2], 1.0 / 512, small[0:rows, 3:4], ALU.mult, ALU.subtract)
                    P.rsq(small[0:rows, 5:6], small[0:rows, 4:5], 1e-5)
                    P.ts('dve', g[0:rows, :], g[0:rows, :], small[0:rows, 2:3], small[0:rows, 5:6], ALU.subtract, ALU.mult)
                    P.tt('pool', g[0:rows, :], g[0:rows, :], lngb[0:rows, l, :], ALU.mult)
                    if samp:
                        P.tt('pool', vnf, g[0:rows, :], lnbb[0:rows, l, :], ALU.add)
                        P.dma(ocv[l], vnf, 'st')
                        b = ps()
                        for g4 in range(4):
                            P.tr(b[:, g4 * 128: g4 * 128 + NS], vnf[:, g4 * 128:(g4 + 1) * 128], ident[0:NS, 0:NS])
                        for g4 in range(4):
                            P.copy('dve', vnfm[:, g4, :], b[:, g4 * 128: g4 * 128 + NS])
                    else:
                        P.tt('pool', VN[tb], g, lnbb[:, l, :], ALU.add)

            def partB0():
                wv_['0'] = w_get()
                wv_['1'] = w_get()
                vb_blocks(range(0, (ntb + 1) // 2))

            def partB1():
                vb_blocks(range((ntb + 1) // 2, ntb))
                w_done(wv_['0'], wv_['1'])

            def partB2():
                for hf in range(2):
                    wu = w_get()
                    for cc in range(2):
                        g4 = hf * 2 + cc
                        b = proj(wu, cc * 128)
                        P.act(USG[:, g4, 0:n], b[:, 0:n], AF.Gelu_apprx_tanh)
                    w_done(wu)

            def partB3():
                wgb = None
                for g4 in range(4):
                    if g4 % 2 == 0:
                        wgb = w_get()
                    b = proj(wgb, (g4 % 2) * 128)
                    sg = tB[g4 % 2]
                    P.act(sg[:, 0:n], b[:, 0:n], AF.Silu)
                    P.tt('dve' if g4 % 2 else 'pool', USG[:, g4, 0:n], USG[:, g4, 0:n], sg[:, 0:n], ALU.mult)
                    if g4 % 2 == 1:
                        w_done(wgb)
                if samp:
                    for g4 in range(4):
                        mx = tA[0][:, 0:n]
                        P.ts('dve', mx, vnfm[:, g4, :], ws00[:, l, g4:g4 + 1], bs0[:, l, g4:g4 + 1], ALU.mult, ALU.add)
                        P.tt('dve', YBG[:, g4, 0:n], mx, USG[:, g4, 0:n], ALU.mult)
                else:
                    for tb in range(ntb):
                        b = ps()
                        for g4 in range(4):
                            o = b[:, g4 * 128:(g4 + 1) * 128]
                            P.mm(o, VN[tb][:, g4 * 128:(g4 + 1) * 128], wmT[:, l, g4, :], start=True, stop=False)
                            P.mm(o, onesb[0:1, :], bsb[0:1, (l * 4 + g4) * 128:(l * 4 + g4 + 1) * 128], start=False, stop=True)
                        tsl = slice(tb * 128, (tb + 1) * 128)
                        P.tt('dve', YBG[:, :, tsl], b.v(lambda a: a.rearrange("p (g t) -> p g t", t=128)), USG[:, :, tsl], ALU.mult)
            partsB = [partB0, partB1, partB2, partB3]

            if not samp:
                h3 = lambda a: a.rearrange("p (h x) -> p h x", x=64)
                c4 = lambda a: a.rearrange("p (c x) -> p c x", x=64)
                cq = lambda a: a.rearrange("p (c q) a b -> p c q (a b)", q=2)
                cq3 = lambda a: a.rearrange("p (c q) x -> p c q x", q=2)
                xq = lambda a: a.rearrange("p (c q x) -> p c q x", q=2, x=64)
                P.begin_side()
                for fB in partsB:
                    fB()
                sideB = P.end_side()
                nB = sum(1 for it in sideB if isinstance(it, Op))
                npull = (NCH // 2) * 60
                kB = max(1, -(-nB // npull))
                pullB = lambda: P.pull(sideB, kB)
                for jp in range(NCH // 2):
                    tsl = slice(jp * 128, (jp + 1) * 128)
                    b = ps()
                    for c in range(4):
                        P.mm(b[:, c * 128:(c + 1) * 128], V[c][:, tsl], identb)
                    P.copy('act', Vt, b)
                    pullB()
                    bb1 = ps()
                    bb2 = ps()
                    for c in range(4):
                        for jj in range(2):
                            po = 64 * jj
                            P.mm(bb1[po:po + 64, c * 128:(c + 1) * 128], BK[c][:, 2 * jp + jj, 0, :], identb, tp=(0, po))
                            P.mm(bb2[po:po + 64, c * 128:(c + 1) * 128], BK[c][:, 2 * jp + jj, 1, :], identb, tp=(0, po))
                    P.copy('dve', Bt, bb1)
                    pullB()
                    P.copy('act', Kt, bb2)
                    pullB()
                    for q in range(2):
                        qs = slice(q * 64, (q + 1) * 64)
                        b1, b2 = ps(), ps()
                        for c in range(4):
                            for jj in range(2):
                                po = 64 * jj
                                j = 2 * jp + jj
                                arv = AR[c][qs, j, :, :].v(lambda a: a.rearrange("p a b -> p (a b)"))
                                P.mm(b1[po:po + 64, c * 128:(c + 1) * 128], BK[c][qs, j, 0, :], arv, tp=(q * 64, po))
                                P.mm(b2[po:po + 64, c * 128:(c + 1) * 128], BK[c][qs, j, 1, :], arv, tp=(q * 64, po))
                        i1 = b1.v(lambda a: a.rearrange("p (h x) -> p h x", x=128))
                        i2 = b2.v(lambda a: a.rearrange("p (h x) -> p h x", x=128))
                        P.tt('dve', G1s.v(cq)[:, :, q, :], i1, mask_si, ALU.mult)
                        pullB()
                        P.tt('dve', G2s.v(cq)[:, :, q, :], i2, mask_si, ALU.mult)
                        pullB()
                    Pk, Qk = PQ[0], PQ[1]
                    for q in range(2):
                        qs = slice(q * 64, (q + 1) * 64)
                        b3 = ps()
                        for c in range(4):
                            for jj in range(2):
                                po = 64 * jj
                                j = 2 * jp + jj
                                P.mm(b3[po:po + 64, c * 64:(c + 1) * 64], AR[c][qs, j, 0, :], BK[c][qs, j, 0, :], tp=(q * 64, po))
                        P.tt('dve', Pk.v(cq3)[:, :, q, :], b3[:, 0:256].v(c4), mask_l4, ALU.mult)
                        pullB()
                    P.copy('pool', Qk, G1s[:, :, 0, :])
                    pullB()
                    P.tt('pool', TtB, G1s[:, :, 0, :], ident8, ALU.add)
                    pullB()
                    free = [PQ[2], PQ[3], PQ[4], PQ[5]]
                    for k in range(6):
                        if k >= 1:
                            b = ps()
                            for h in range(8):
                                for po in (0, 64):
                                    P.mm(b[po:po + 64, h * 64:(h + 1) * 64], Pk[po:po + 64, h, :], TtB[po:po + 64, h, :], tp=(po, po))
                            P.tt('dve', TtB, b.v(h3), TtB, ALU.add)
                            pullB()
                        Pn = Qn = None
                        if k <= 4:
                            b = ps()
                            for h in range(8):
                                for po in (0, 64):
                                    P.mm(b[po:po + 64, h * 64:(h + 1) * 64], Qk[po:po + 64, h, :], Pk[po:po + 64, h, :], tp=(po, po))
                            Pn = free.pop(0)
                            P.copy('act', Pn, b.v(h3))
                            pullB()
                        if k <= 3:
                            b = ps()
                            for h in range(8):
                                for po in (0, 64):
                                    P.mm(b[po:po + 64, h * 64:(h + 1) * 64], Pk[po:po + 64, h, :], Qk[po:po + 64, h, :], tp=(po, po))
                            Qn = free.pop(0)
                            P.copy('dve', Qn, b.v(h3))
                            pullB()
                        free.append(Pk)
                        free.append(Qk)
                        Pk, Qk = Pn, Qn
                    for jj in range(2):
                        po = 64 * jj
                        pp_ = slice(po, po + 64)
                        j = 2 * jp + jj
                        js = slice(j * C, (j + 1) * C)
                        bx2 = ps()
                        for h in range(8):
                            P.mm(bx2[pp_, h * 64:(h + 1) * 64], G2s[pp_, h, 0, :], Vt[pp_, h * 64:(h + 1) * 64], tp=(po, po))
                        P.copy('act', Xs[pp_, :], bx2[pp_, :])
                        pullB()
                        for q in range(2):
                            qs = slice(q * 64, (q + 1) * 64)
                            bxq = ps()
                            for c in range(4):
                                P.mm(bxq[pp_, c * 64:(c + 1) * 64], AR[c][qs, j, 0, :], Hb[l][qs, c, :], tp=(q * 64, po))
                            xv = Xs[pp_, :].v(xq)[:, :, q, :]
                            P.tt('dve', xv, bxq[pp_, 0:256].v(c4), xv, ALU.add)
                            pullB()
                        bu = ps()
                        for h in range(8):
                            P.mm(bu[pp_, h * 64:(h + 1) * 64], TtB[pp_, h, :], Xs[pp_, h * 64:(h + 1) * 64], tp=(po, po))
                        P.copy('dve', Us[pp_, :], bu[pp_, :])
                        pullB()
                        by2 = ps()
                        for h in range(8):
                            c, q = h // 2, h % 2
                            qs = slice(q * 64, (q + 1) * 64)
                            hs = slice(h * 64, (h + 1) * 64)
                            o = by2[qs, c * 64:(c + 1) * 64]
                            P.mm(o, Us[pp_, hs], G1s[pp_, h, 1, :], start=True, stop=False, tp=(po, q * 64))
                            P.mm(o, Vt[pp_, hs], G2s[pp_, h, 1, :], start=False, stop=True, tp=(po, q * 64))
                        P.copy('act', YA[:, :, js], by2[:, 0:256].v(c4))
                        pullB()
                        for q in range(2):
                            qs = slice(q * 64, (q + 1) * 64)
                            byq = ps()
                            for c in range(4):
                                P.mm(byq[qs, c * 64:(c + 1) * 64], Hb[l][qs, c, :], AR[c][qs, j, 1, :], tp=(q * 64, q * 64))
                            P.tt('dve', YA[qs, :, js], byq[qs, 0:256].v(c4), YA[qs, :, js], ALU.add)
                            pullB()
                        bh = ps()
                        for h in range(8):
                            c, q = h // 2, h % 2
                            qs = slice(q * 64, (q + 1) * 64)
                            hs = slice(h * 64, (h + 1) * 64)
                            o = bh[qs, c * 64:(c + 1) * 64]
                            P.mm(o, Bt[pp_, hs], Us[pp_, hs], start=True, stop=False, tp=(po, q * 64))
                            P.mm(o, Kt[pp_, hs], Vt[pp_, hs], start=False, stop=True, tp=(po, q * 64))
                        P.tt('dve', Hf[l], bh[:, 0:256].v(c4), Hf[l], ALU.add)
                        pullB()
                        P.tt('pool', Hf[l], Hf[l], bc_last(WC[:, :, j:j + 1], [128, 4, 64]), ALU.mult)
                        pullB()
                        P.copy('act', Hb[l], Hf[l])
                        pullB()
                P.pull(sideB, 10 ** 9)
                if ti == NT - 1:
                    for c in range(4):
                        b = ps()
                        P.tr(b[0:64, 0:128], Hf[l][:, c, :], ident)
                        st = tA[0]
                        P.copy('dve', st[0:64, 0:128], b[0:64, 0:128])
                        P.dma(owkv_p[l, 2 * c:2 * c + 2].v(lambda a: a.rearrange("q v k -> v q k")),
                              st[0:64, 0:128].v(lambda a: a.rearrange("p (q k) -> p q k", q=2)), 'st')
                    P.dma(oshift_p[l:l + 1, :].v(lambda a: a.rearrange("o (f p) -> p (o f)", p=128)), carry[l], 'st', slow=True)
            else:
                for fB in partsB:
                    fB()
                for x in range(6):
                    b = ps()
                    for c in range(4):
                        P.tr(b[0:NS, c * 128:(c + 1) * 128], SV[:, c, x, :], ident)
                    P.copy('act' if x % 2 else 'dve', TOKV[x % 2], b[0:NS, :])
                    P.dma(scr1[l][:, :, x, :], TOKV[x % 2].v(lambda a: a.rearrange("b (h j) -> b h j", j=64)), 'sc')
                P.dma(SVP, scr1[l].v(lambda a: a.rearrange("b h x j -> (b h) x j")), 'sc')
                bi = lambda x: SVP[:, x:x + 1, :].v(lambda a: a.to_broadcast([128, 8, 64]))
                bj = lambda t: t.v(lambda a: a.unsqueeze(2).to_broadcast([128, 8, 64]))
                sw_in = swkv[l].v(lambda a: a.rearrange("b h i j -> (b h) i j"))
                sw_out = owkv_s[l].v(lambda a: a.rearrange("b h i j -> (b h) i j"))
                for pt in range(NPART):
                    isl = slice(pt * 8, pt * 8 + 8)
                    P.dma(Sst_p, sw_in[:, isl, :], 'ld')
                    P.tt('dve', Stmp_p, Sst_p, bi(4), ALU.mult)
                    P.add('dve', lambda E, isl=isl: E.reduce_sum(out=sa[:, isl].ap, in_=Stmp_p.ap, axis=AX.X), [Stmp_p], [sa])
                    P.tt('pool', Sst_p, Sst_p, bi(1), ALU.mult)
                    P.tt('dve', Stmp_p, bj(sa[:, isl]), bi(5), ALU.mult)
                    P.tt('pool', Sst_p, Sst_p, Stmp_p, ALU.add)
                    P.tt('dve', Stmp_p, bj(SVP[:, 3, isl]), bi(2), ALU.mult)
                    P.tt('pool', Sst_p, Sst_p, Stmp_p, ALU.add)
                    P.dma(sw_out[:, isl, :], Sst_p, 'st')
                    P.tt('dve', Stmp_p, Sst_p, bi(0), ALU.mult)
                    P.add('dve', lambda E, isl=isl: E.reduce_sum(out=ysm[:, isl].ap, in_=Stmp_p.ap, axis=AX.X), [Stmp_p], [ysm])
                P.dma(scr2[l].v(lambda a: a.rearrange("b (h i) -> (b h) i", i=64)), ysm, 'sc')
                P.dma(ytok, scr2[l], 'sc')
                b = ps()
                for c in range(4):
                    P.tr(b[:, c * 128: c * 128 + NS], ytok[:, c * 128:(c + 1) * 128], ident[0:NS, 0:NS])
                for c in range(4):
                    P.copy('dve', YA[:, c, 0:n], b[:, c * 128: c * 128 + NS])

            if not samp and l == 0:
                dbg("YA", YA)
                dbg("V0", V[0], BF16)
                dbg("BON0", BON[0])
            for hf in range(2):
                wga = w_get()
                for cc in range(2):
                    c = hf * 2 + cc
                    b = proj(wga, cc * 128)
                    P.act(YAG[c][:, 0:n], b[:, 0:n], AF.Silu)
                w_done(wga)
            for c in range(4):
                y = YA[:, c, 0:n]
                ybf, ysq = tB[0], tB[1]
                P.copy('act', ybf[:, 0:n], y)
                P.act(ysq[:, 0:n], y, AF.Square)
                b1, b2 = ps(), ps()
                P.mm(b1[:, 0:n], blkb, ybf[:, 0:n])
                P.mm(b2[:, 0:n], blkb, ysq[:, 0:n])
                m, msq, var, yc = tA[0], tA[1], tA[2], tA[3]
                P.ts('dve', m[:, 0:n], b1[:, 0:n], 1.0 / 64, None, ALU.mult)
                P.tt('pool', msq[:, 0:n], m[:, 0:n], m[:, 0:n], ALU.mult)
                P.stt('dve', var[:, 0:n], b2[:, 0:n], 1.0 / 64, msq[:, 0:n], ALU.mult, ALU.subtract)
                P.rsq(var[:, 0:n], var[:, 0:n], GN_EPS)
                P.tt('pool', yc[:, 0:n], y, m[:, 0:n], ALU.subtract)
                P.tt('dve', yc[:, 0:n], yc[:, 0:n], var[:, 0:n], ALU.mult)
                P.act(yc[:, 0:n], yc[:, 0:n], AF.Identity, bias=gnb[:, l, c:c + 1], scale=gng[:, l, c:c + 1])
                P.tt('pool', yc[:, 0:n], yc[:, 0:n], BON[c][:, 0:n], ALU.add)
                P.tt('dve', YAG[c][:, 0:n], yc[:, 0:n], YAG[c][:, 0:n], ALU.mult)

            if not samp and l == 0:
                for c in range(4):
                    dbg("YAG%d" % c, YAG[c], BF16)
                dbg("YBG", YBG, BF16)
                dbg("USG", USG)
            for jg in range(4):
                wma = w_get()
                wmb = w_get()
                wba = w_get()
                wbb = w_get()
                for jj in range(2):
                    j = jg * 2 + jj
                    b = proj(wma, jj * 128)
                    sma = tA[0]
                    P.act(sma[:, 0:n], b[:, 0:n], AF.Sigmoid)
                    b = proj(wmb, jj * 128)
                    smb = tA[1]
                    P.act(smb[:, 0:n], b[:, 0:n], AF.Sigmoid)
                    ba = ps()
                    for c in range(4):
                        P.mm(ba[:, 0:n], wba[:, c, jj * 128:(jj + 1) * 128], YAG[c][:, 0:n], start=(c == 0), stop=(c == 3))
                    bb_ = ps()
                    for c in range(4):
                        P.mm(bb_[:, 0:n], wbb[:, c, jj * 128:(jj + 1) * 128], YBG[:, c, 0:n], start=(c == 0), stop=(c == 3))
                    ma, mb = tA[2], tA[3]
                    P.tt('dve', ma[:, 0:n], ba[:, 0:n], sma[:, 0:n], ALU.mult)
                    P.tt('dve', mb[:, 0:n], bb_[:, 0:n], smb[:, 0:n], ALU.mult)
                    P.tt('pool', MG[j][:, 0:n], ma[:, 0:n], mb[:, 0:n], ALU.add)
                w_done(wma, wmb, wba, wbb)
            for jg in range(4):
                wo = w_get()
                for jj in range(2):
                    j = jg * 2 + jj
                    b = ps()
                    for kc in range(8):
                        P.mm(b[:, 0:n], wo[:, kc, jj * 128:(jj + 1) * 128], MG[kc][:, 0:n], start=(kc == 0), stop=(kc == 7))
                    P.tt('dve', hT[j][:, 0:n], hT[j][:, 0:n], b[:, 0:n], ALU.add)
                w_done(wo)
            if not samp and l == 0:
                for k in range(8):
                    dbg("MG%d" % k, MG[k], BF16)
                    dbg("hmid%d" % k, hT[k])
            for tb in range(ntb):
                xt = xst
                src = psm[l] if samp else pp[l, t0 + tb * 128: t0 + (tb + 1) * 128, :]
                P.dma(xt[0:rows, 0:DPLE], src, 'ld')
                b = ps()
                for kc in range(2):
                    P.tr(b[:, kc * 128: kc * 128 + rows], xt[0:rows, kc * 128:(kc + 1) * 128], ident[0:rows, 0:rows])
                for kc in range(2):
                    P.copy('act', pT[kc][:, tb * 128: tb * 128 + rows], b[:, kc * 128: kc * 128 + rows])
            for kc in range(8):
                P.copy('act', xnT[kc][:, 0:n], hT[kc][:, 0:n])
            for jg in range(4):
                wp = w_get()
                wg = w_get()
                for jj in range(2):
                    j = jg * 2 + jj
                    bp = ps()
                    for kc in range(2):
                        P.mm(bp[:, 0:n], wp[:, kc, jj * 128:(jj + 1) * 128], pT[kc][:, 0:n], start=(kc == 0), stop=(kc == 1))
                    bg = ps()
                    for kc in range(8):
                        P.mm(bg[:, 0:n], wg[:, kc, jj * 128:(jj + 1) * 128], xnT[kc][:, 0:n], start=(kc == 0), stop=(kc == 7))
                    sg = tA[0]
                    P.act(sg[:, 0:n], bg[:, 0:n], AF.Sigmoid, bias=bgv[:, l, j:j + 1], scale=1.0)
                    tp_ = tA[1]
                    P.tt('dve', tp_[:, 0:n], bp[:, 0:n], sg[:, 0:n], ALU.mult)
                    P.tt('pool', hT[j][:, 0:n], hT[j][:, 0:n], tp_[:, 0:n], ALU.add)
                w_done(wp, wg)

        if not samp:
            for k in range(8):
                dbg("hend%d" % k, hT[k])
        yT = tA
        rmsnorm(fg32[:, 0, :], yT)
        for tb in range(ntb):
            dst = ys if samp else yp[t0 + tb * 128: t0 + (tb + 1) * 128, :]
            for half in range(2):
                b = ps()
                for kk_ in range(4):
                    kc = half * 4 + kk_
                    P.tr(b[0:rows, kk_ * 128:(kk_ + 1) * 128], yT[kc][:, tb * 128: tb * 128 + rows], ident)
                if samp:
                    P.copy('act' if half else 'dve', xst[0:rows, 0:512], b[0:rows, :])
                    P.dma(dst[:, half * 512:(half + 1) * 512], xst[0:rows, 0:512], 'st')
                else:
                    P.copy('act' if half else 'dve', xst[0:rows, half * 512:(half + 1) * 512], b[0:rows, :])
            if not samp:
                P.dma(dst, xst[0:rows, :], 'st')
        if samp:
            S_list = P.end_side()
            nS = sum(1 for it in S_list if isinstance(it, Op))
            P.auto = S_list
            P.auto_ratio = float(os.environ.get("K_RATIO", nS / (0.8 * (1800.0 + 8.5 * TT))))
            P.auto_acc = 0.0
        elif ti == 0:
            P.pull_s(10 ** 9)
            assert not P.auto
            P.auto = None

    print('sbuf bytes remaining', nc.sbuf_bytes_remaining)
    P.emit()
    return nc, P


_CACHE = {}


def kernel(**inp):
    NCORE = 8
    f = lambda a: np.ascontiguousarray(np.asarray(a, dtype=np.float32))
    if 'nc' not in _CACHE:
        _CACHE['nc'] = build()[0]
    nc = _CACHE['nc']
    wnames = ["norm_g", "w_in", "shift_mu", "w0", "w_up", "a0", "a_up", "vres_down", "vres_up", "vres_b", "k_k", "k_a",
              "gn_g", "gn_b", "ln_v_g", "ln_v_b", "w_spatial", "b_spatial", "w_br_a", "w_br_b", "w_out", "w_ple",
              "w_ple_gate", "b_ple_gate"]
    shared = {k: f(inp[k]) for k in wnames}
    shared["r_k"] = f(inp["r_k"]).reshape(L, DA)
    shared["final_g"] = f(inp["final_g"]).reshape(1, D)
    in_maps = []
    for c in range(NCORE):
        m = dict(shared)
        sl = slice(16 * c, 16 * c + 16)
        m["xp"] = f(inp["x_prompt"][c])
        m["xs"] = f(inp["x_sample"][sl, 0])
        m["swkv"] = f(inp["state_rwkv_wkv"][:, sl])
        m["sshift"] = f(inp["state_rwkv_shift"][:, sl])
        m["pp"] = f(inp["p_prompt"][:, c])
        m["psm"] = f(inp["p_sample"][:, sl, 0])
        in_maps.append(m)
    res = run_bass_kernel_spmd(nc, in_maps, core_ids=list(range(NCORE))).results
    y_prompt = np.stack([res[c]["yp"] for c in range(NCORE)], 0).astype(np.float32)
    y_sample = np.concatenate([res[c]["ys"] for c in range(NCORE)], 0)[:, None, :].astype(np.float32)
    wkv_prompt = np.stack([res[c]["owkv_p"] for c in range(NCORE)], 1).astype(np.float32)
    shift_prompt = np.stack([res[c]["oshift_p"] for c in range(NCORE)], 1).astype(np.float32)
    wkv_sample = np.concatenate([res[c]["owkv_s"] for c in range(NCORE)], 1).astype(np.float32)
    shift_sample = np.concatenate([res[c]["oshift_s"] for c in range(NCORE)], 1).astype(np.float32)
    chunk_v = np.concatenate([res[c]["ocv"] for c in range(NCORE)], 1)[:, :, None, :].astype(np.float32)
    return (y_prompt, y_sample, wkv_prompt, shift_prompt, wkv_sample, shift_sample, chunk_v)
```

```python
import sys
import os
import numpy as np
from contextlib import ExitStack
import concourse.bass as bass
import concourse.mybir as mybir
from concourse.bass_utils import run_bass_kernel_spmd

F32 = mybir.dt.float32
BF16 = mybir.dt.bfloat16
AF = mybir.ActivationFunctionType
ALU = mybir.AluOpType
AX = mybir.AxisListType

D = 1024
DA = 512
NSHIFT = 1664
DIN = 5760
DPLE = 256
L = 2
EPS = 1e-6
GN_EPS = 64e-5
C = 64
LOGW_C = -0.3032653298563167


class Buf:
    def __init__(s, name):
        s.name = name
        s.w = None
        s.r = {}


class T:
    def __init__(s, ap, buf):
        s.ap = ap
        s.buf = buf

    def __getitem__(s, k):
        return T(s.ap[k], s.buf)

    def v(s, f):
        return T(f(s.ap), s.buf)


class Op:
    pass


class Prog:
    def __init__(s, nc):
        s.nc = nc
        s.ops = []
        s.es = ExitStack()
        s.eng = {'pe': nc.tensor, 'act': nc.scalar, 'dve': nc.vector, 'pool': nc.gpsimd, 'sp': nc.sync}
        s.nbuf = 0
        s.rr = {}
        s.side = None
        s.side_kind = None
        s.auto = None
        s._in_auto = False
        s.auto_acc = 0.0
        s.auto_ratio = 0.0
        s.s_done = set()

    def sbuf(s, name, shape, dt, nsplit=None):
        t = s.es.enter_context(s.nc.sbuf_tensor(name, shape, dt))
        return T(t[:], Buf(name))

    def psum(s, name, shape, dt):
        t = s.es.enter_context(s.nc.psum_tensor(name, shape, dt))
        return T(t[:], Buf(name))

    def sub(s, t, name=None):
        s.nbuf += 1
        return T(t.ap, Buf(name or ("b%d" % s.nbuf)))

    def add(s, eng, fn, reads, writes, dma=None, force_main=False):
        op = Op()
        op.eng = eng
        op.fn = fn
        op.dma = dma
        op.sig = False
        op.cnt = 0
        op.reads = reads
        op.writes = writes
        fr = sys._getframe(1)
        wh = []
        while fr is not None and len(wh) < 4:
            wh.append(fr.f_lineno)
            fr = fr.f_back
        op.where = wh
        if s.side is not None and not force_main:
            s.side.append(op)
        else:
            s._append_main(op)
        return op

    def begin_side(s, kind='B'):
        s.side = []
        s.side_kind = kind

    def end_side(s):
        lst = s.side
        s.side = None
        s.side_kind = None
        return lst

    def marker(s, m):
        assert s.side is not None
        s.side.append(m)

    def _append_main(s, op):
        s.ops.append(op)
        if s.auto is not None and not s._in_auto:
            s._in_auto = True
            s.auto_acc += s.auto_ratio
            while s.auto_acc >= 1.0 and s.auto:
                s.auto_acc -= 1.0
                if not s.pull_s(1):
                    break
            s._in_auto = False

    def pull_s(s, k, until_done=None):
        lst = s.auto
        moved = 0
        while lst and (moved < k or (until_done and not until_done <= s.s_done)):
            it = lst[0]
            if isinstance(it, tuple):
                if it[0] == 'get':
                    if s.main_cur() <= it[1]:
                        assert not until_done, "S stream would wait for a slice the main stream has not acquired"
                        return False
                    lst.pop(0)
                else:
                    lst.pop(0)
                    s.s_done |= set(it[1])
                continue
            lst.pop(0)
            if isinstance(it, Op):
                s.ops.append(it)
                moved += 1
            else:
                it()
        return True

    def action(s, fn):
        if s.side is not None:
            s.side.append(fn)
        else:
            fn()

    def pull(s, lst, k):
        while k > 0 and lst:
            it = lst.pop(0)
            if isinstance(it, Op):
                s._append_main(it)
                k -= 1
            else:
                it()

    def analyse(s):
        last_dma = {}
        for idx, op in enumerate(s.ops):
            op.idx = idx
            deps = {}

            def dep(o):
                if o is None:
                    return
                key = o.dma if o.dma else o.eng
                if key not in deps or deps[key].idx < o.idx:
                    deps[key] = o

            def bl(ts_):
                o = []
                for t in ts_:
                    if isinstance(t.buf, (list, tuple)):
                        o.extend(t.buf)
                    else:
                        o.append(t.buf)
                return o
            rb = bl(op.reads)
            wb = bl(op.writes)
            if op.dma:
                dep(last_dma.get(op.dma))
                last_dma[op.dma] = op
            for b in rb:
                dep(b.w)
            for b in wb:
                dep(b.w)
                for o in b.r.values():
                    dep(o)
            mykey = op.dma if op.dma else op.eng
            for b in rb:
                b.r[mykey] = op
            for b in wb:
                b.w = op
                b.r = {}
            op.deps = list(deps.values())

    def emit(s):
        nc = s.nc
        import os
        if os.environ.get('K_STOP'):
            s.ops = s.ops[:int(os.environ['K_STOP'])]
        s.analyse()
        pos = {}
        for op in s.ops:
            op.pos = pos.get(op.eng, 0)
            pos[op.eng] = op.pos + 1
        for op in s.ops:
            op.waits = []
            for d in op.deps:
                if d.dma is None and d.eng == op.eng:
                    if op.eng == 'pe' and op.dma is None:
                        continue
                op.waits.append(d)
                d.sig = True
        cnt = {}
        keys = []
        for op in s.ops:
            k = op.dma if op.dma else op.eng
            if k not in cnt:
                cnt[k] = 0
                keys.append(k)
            if op.dma:
                cnt[k] += 16
                op.cnt = cnt[k]
            elif op.sig:
                cnt[k] += 1
                op.cnt = cnt[k]
        sems = {k: s.es.enter_context(nc.semaphore("s_" + k)) for k in keys}
        waited = {e: {} for e in s.eng}
        for op in s.ops:
            E = s.eng[op.eng]
            for d in op.waits:
                k = d.dma if d.dma else d.eng
                if waited[op.eng].get(k, 0) >= d.cnt:
                    continue
                E.wait_ge(sems[k], d.cnt)
                waited[op.eng][k] = d.cnt
            try:
                ins = op.fn(E)
                if os.environ.get('K_ANNOT'):
                    ins.annotate("W%s" % (op.where,))
            except BaseException:
                print("FAILED op at lines", op.where, op.eng)
                raise
            if op.dma:
                ins.then_inc(sems[op.dma], 16)
            elif op.sig:
                ins.then_inc(sems[op.eng], 1)
        for k in keys:
            if cnt[k] > 0:
                nc.sync.wait_ge(sems[k], cnt[k])

    def dma(s, out, in_, lane, eng='sp', slow=False, nlanes=1, force_main=False):
        kw = {'allow_slow_non_contiguous': True} if slow else {}
        nlanes = {'ld': 4, 'st': 4, 'wl': 2}.get(lane, nlanes)
        if nlanes > 1:
            i = s.rr.get(lane, 0)
            s.rr[lane] = i + 1
            lane = "%s%d" % (lane, i % nlanes)
        return s.add(eng, lambda E: E.dma_start(out=out.ap, in_=in_.ap, **kw), [in_], [out], dma=lane, force_main=force_main)

    def mm(s, out, lhsT, rhs, start=True, stop=True, tp=None, extra_r=()):
        kw = {}
        if tp is not None:
            kw['tile_position'] = tp
        return s.add('pe', lambda E: E.matmul(out.ap, lhsT=lhsT.ap, rhs=rhs.ap, start=start, stop=stop, **kw),
                     [lhsT, rhs] + list(extra_r), [out])

    def tr(s, out, in_, ident):
        return s.add('pe', lambda E: E.transpose(out.ap, in_.ap, ident.ap), [in_, ident], [out])

    def act(s, out, in_, func, bias=None, scale=1.0):
        reads = [in_]
        kw = {}
        if isinstance(bias, T):
            reads.append(bias)
            kw['bias'] = bias.ap
        elif bias is not None:
            kw['bias'] = bias
        if isinstance(scale, T):
            reads.append(scale)
            kw['scale'] = scale.ap
        else:
            kw['scale'] = scale
        return s.add('act', lambda E: E.activation(out=out.ap, in_=in_.ap, func=func, **kw), reads, [out])

    def tt(s, eng, out, in0, in1, op):
        return s.add(eng, lambda E: E.tensor_tensor(out=out.ap, in0=in0.ap, in1=in1.ap, op=op), [in0, in1], [out])

    def ts(s, eng, out, in0, s1, s2, op0, op1=None):
        reads = [in0]
        a1 = s1.ap if isinstance(s1, T) else s1
        a2 = s2.ap if isinstance(s2, T) else s2
        if isinstance(s1, T):
            reads.append(s1)
        if isinstance(s2, T):
            reads.append(s2)
        if op1 is None:
            return s.add(eng, lambda E: E.tensor_single_scalar(out=out.ap, in_=in0.ap, scalar=a1, op=op0), reads, [out])
        return s.add(eng, lambda E: E.tensor_scalar(out=out.ap, in0=in0.ap, scalar1=a1, scalar2=a2, op0=op0, op1=op1),
                     reads, [out])

    def stt(s, eng, out, in0, sc, in1, op0, op1):
        reads = [in0, in1]
        a = sc.ap if isinstance(sc, T) else sc
        if isinstance(sc, T):
            reads.append(sc)
        return s.add(eng, lambda E: E.scalar_tensor_tensor(out=out.ap, in0=in0.ap, scalar=a, in1=in1.ap, op0=op0, op1=op1),
                     reads, [out])

    def rsq(s, out, in_, eps):
        s.act(out, in_, AF.Ln, bias=s.epsc[eps][0:out.ap.shape[0]], scale=1.0)
        s.act(out, out, AF.Exp, scale=-0.5)

    def copy(s, eng, out, in_):
        if eng == 'act':
            return s.add('act', lambda E: E.activation(out=out.ap, in_=in_.ap, func=AF.Copy), [in_], [out])
        return s.add(eng, lambda E: E.tensor_copy(out=out.ap, in_=in_.ap), [in_], [out])

    def memset(s, eng, out, val):
        return s.add(eng, lambda E: E.memset(out.ap, val), [], [out])


def build(SEQ_T=2048, TT=512, NS=16, DEBUG=False):
    nc = bass.Bass("TRN2", target_bir_lowering=False)
    P = Prog(nc)
    dbg_seen = set()

    def dbg(name, t, dt=F32):
        if not DEBUG or name in dbg_seen:
            return
        dbg_seen.add(name)
        shape = list(t.ap.shape)
        o = T(nc.dram_tensor("dbg_" + name, shape, dt, kind="ExternalOutput").ap(), Buf("dbg_" + name))
        P.dma(o, t, 'dbg', eng='pool')
    NT = SEQ_T // TT
    NCH = TT // C
    NTB = TT // 128

    def din(name, shape):
        return T(nc.dram_tensor(name, list(shape), F32, kind="ExternalInput").ap(), Buf(name))

    def dout(name, shape):
        return T(nc.dram_tensor(name, list(shape), F32, kind="ExternalOutput").ap(), Buf(name))

    xp = din("xp", [SEQ_T, D])
    xs = din("xs", [NS, D])
    swkv = din("swkv", [L, NS, 8, 64, 64])
    sshift = din("sshift", [L, NS, NSHIFT])
    pp = din("pp", [L, SEQ_T, DPLE])
    psm = din("psm", [L, NS, DPLE])
    norm_g = din("norm_g", [L, D])
    w_in = din("w_in", [L, D, DIN])
    shift_mu = din("shift_mu", [L, NSHIFT])
    w0 = din("w0", [L, DA])
    w_up = din("w_up", [L, 64, DA])
    a0 = din("a0", [L, DA])
    a_up = din("a_up", [L, 64, DA])
    vres_down = din("vres_down", [1, D, 32])
    vres_up = din("vres_up", [1, 32, DA])
    vres_b = din("vres_b", [1, DA])
    k_k = din("k_k", [L, DA])
    k_a = din("k_a", [L, DA])
    r_k = din("r_k", [L, DA])
    gn_g = din("gn_g", [L, DA])
    gn_b = din("gn_b", [L, DA])
    ln_v_g = din("ln_v_g", [L, 512])
    ln_v_b = din("ln_v_b", [L, 512])
    w_spatial = din("w_spatial", [L, 4, 128, 128])
    b_spatial = din("b_spatial", [L, 4, 128])
    w_br_a = din("w_br_a", [L, DA, D])
    w_br_b = din("w_br_b", [L, 512, D])
    w_out = din("w_out", [L, D, D])
    w_ple = din("w_ple", [L, DPLE, D])
    w_ple_gate = din("w_ple_gate", [L, D, D])
    b_ple_gate = din("b_ple_gate", [L, D])
    final_g = din("final_g", [1, D])

    yp = dout("yp", [SEQ_T, D])
    ys = dout("ys", [NS, D])
    owkv_p = dout("owkv_p", [L, 8, 64, 64])
    oshift_p = dout("oshift_p", [L, NSHIFT])
    owkv_s = dout("owkv_s", [L, NS, 8, 64, 64])
    oshift_s = dout("oshift_s", [L, NS, NSHIFT])
    ocv = dout("ocv", [L, NS, 512])

    scr1 = T(nc.dram_tensor("scr1", [L, NS, 8, 6, 64], F32, kind="ExternalOutput").ap(), Buf("scr1"))
    scr2 = T(nc.dram_tensor("scr2", [L, NS, 512], F32, kind="ExternalOutput").ap(), Buf("scr2"))

    ident = P.sbuf("ident", [128, 128], F32)
    identb = P.sbuf("identb", [128, 128], BF16)
    onesb = P.sbuf("onesb", [128, 128], BF16)
    blkb = P.sbuf("blkb", [128, 128], BF16)
    maskscan = P.sbuf("maskscan", [128, TT], F32)
    maskc = P.sbuf("maskc", [128, 128], F32)

    epst = P.sbuf("epst", [128, 4], F32)
    P.epsc = {}
    for i, e in enumerate([D * EPS, 1e-24, GN_EPS, 1e-5]):
        P.memset('pool', epst[:, i:i + 1], e)
        P.epsc[e] = epst[:, i:i + 1]
    P.memset('pool', ident, 0.0)
    P.add('pool', lambda E: E.affine_select(out=ident.ap, in_=ident.ap, pattern=[[-1, 128]], compare_op=ALU.not_equal,
                                            fill=1.0, base=0, channel_multiplier=1), [ident], [ident])
    P.copy('pool', identb, ident)
    P.memset('pool', onesb, 1.0)
    P.memset('pool', blkb, 0.0)
    P.memset('pool', blkb[0:64, 0:64], 1.0)
    P.memset('pool', blkb[64:128, 64:128], 1.0)
    P.memset('pool', maskscan, 1.0)
    P.memset('pool', maskscan.v(lambda a: a.rearrange("p (c t) -> p c t", t=C)[:, :, 0:1]), 0.0)
    P.memset('pool', maskc, 1.0)
    P.add('pool', lambda E: E.affine_select(out=maskc.ap, in_=maskc.ap, pattern=[[1, 128]], compare_op=ALU.is_ge,
                                            fill=0.0, base=0, channel_multiplier=-1), [maskc], [maskc])
    def build_mask128(name, shape, pat, cm, base):
        a = P.sbuf(name, [128] + shape, F32)
        tmp = P.sbuf(name + "_t", [128] + shape, F32)
        P.memset('pool', a, 1.0)
        P.memset('pool', tmp, 1.0)
        P.add('pool', lambda E: E.affine_select(out=a.ap, in_=a.ap, pattern=pat, compare_op=ALU.is_ge, fill=0.0,
                                                base=base, channel_multiplier=cm), [a], [a])
        P.add('pool', lambda E: E.affine_select(out=tmp.ap, in_=tmp.ap, pattern=pat, compare_op=ALU.is_ge, fill=0.0,
                                                base=base - 64 * cm, channel_multiplier=cm), [tmp], [tmp])
        P.copy('pool', a[64:128], tmp[64:128])
        return a
    m_strict = build_mask128("m_strict", [64], [[1, 64]], -1, -1)
    m_incl = build_mask128("m_incl", [64], [[1, 64]], -1, 0)
    mask_si1 = P.sbuf("mask_si", [128, 2, 64], F32)
    P.copy('pool', mask_si1[:, 0, :], m_strict)
    P.copy('pool', mask_si1[:, 1, :], m_incl)
    mask_l1 = build_mask128("mask_l", [64], [[-1, 64]], 1, -1)
    id2 = P.sbuf("id2", [128, 64], F32)
    P.tt('pool', id2, ident[:, 0:64], ident[:, 64:128], ALU.add)
    mask_si = mask_si1.v(lambda a: a.rearrange("p a b -> p (a b)").unsqueeze(1).to_broadcast([128, 4, 128]))
    mask_l4 = mask_l1.v(lambda a: a.unsqueeze(1).to_broadcast([128, 4, 64]))
    ident8 = id2.v(lambda a: a.unsqueeze(1).to_broadcast([128, 8, 64]))

    def load_vec(name, src, nl, ncol):
        t = P.sbuf(name, [128, nl, ncol], F32)
        P.dma(t, src.v(lambda a: a.rearrange("l (c p) -> p l c", p=128)), 'ld', slow=True)
        return t

    g32 = load_vec("g32", norm_g, L, 8)
    fg32 = load_vec("fg32", final_g, 1, 8)
    mu = load_vec("mu", shift_mu, L, 13)
    hw0 = load_vec("hw0", w0, L, 4)
    a0v = load_vec("a0v", a0, L, 4)
    vbv = load_vec("vbv", vres_b, 1, 4)
    kkv = load_vec("kkv", k_k, L, 4)
    kav = load_vec("kav", k_a, L, 4)
    rkv = load_vec("rkv", r_k, L, 4)
    gng = load_vec("gng", gn_g, L, 4)
    gnb = load_vec("gnb", gn_b, L, 4)
    bgv = load_vec("bgv", b_ple_gate, L, 8)
    omka = P.sbuf("omka", [128, L, 4], F32)
    P.ts('pool', g32, g32, 32.0, None, ALU.mult)
    P.ts('pool', fg32, fg32, 32.0, None, ALU.mult)
    P.ts('pool', hw0, hw0, 0.5, None, ALU.mult)
    ha0 = P.sbuf("ha0", [128, L, 4], F32)
    hvb = P.sbuf("hvb", [128, 1, 4], F32)
    P.ts('pool', ha0, a0v, 0.5, None, ALU.mult)
    P.ts('pool', hvb, vbv, 0.5, None, ALU.mult)
    P.ts('pool', omka, kav, -1.0, 1.0, ALU.mult, ALU.add)

    lngb = P.sbuf("lngb", [128, L, 512], F32)
    lnbb = P.sbuf("lnbb", [128, L, 512], F32)
    for l in range(L):
        P.dma(lngb[:, l, :], ln_v_g[l:l + 1, :].v(lambda a: a.partition_broadcast(128)), 'ld')
        P.dma(lnbb[:, l, :], ln_v_b[l:l + 1, :].v(lambda a: a.partition_broadcast(128)), 'ld')
    ws00 = P.sbuf("ws00", [128, L, 4], F32)
    bs0 = P.sbuf("bs0", [128, L, 4], F32)
    P.dma(ws00, w_spatial.v(lambda a: a[:, :, 0, 0:1].rearrange("l g o -> o l g").partition_broadcast(128)), 'ld', slow=True)
    P.dma(bs0, b_spatial.v(lambda a: a[:, :, 0:1].rearrange("l g o -> o l g").partition_broadcast(128)), 'ld', slow=True)
    bsb = P.sbuf("bsb", [1, L * 4 * 128], BF16)
    P.dma(bsb, b_spatial.v(lambda a: a.rearrange("(o l) g t -> o (l g t)", o=1)), 'wl', eng='pool')
    lup = P.sbuf("lup", [128, L, 512], BF16)
    for l in range(L):
        P.dma(lup[0:64, l, :], w_up[l], 'wl', eng='pool')
        P.dma(lup[64:128, l, :], a_up[l], 'wl', eng='pool')
    vdb = P.sbuf("vdb", [128, 8, 32], BF16)
    P.dma(vdb, vres_down[0].v(lambda a: a.rearrange("(kc p) c -> p kc c", p=128)), 'wl', eng='pool')
    vub = P.sbuf("vub", [32, 512], BF16)
    P.dma(vub, vres_up[0], 'wl', eng='pool')

    banks = [P.psum("bank%d" % i, [128, 512], F32) for i in range(8)]
    bank_i = [0]

    sbank_i = [0]
    ssbank_i = [0]

    def ps():
        if P.side_kind == 'S':
            b = banks[6 + ssbank_i[0] % 2]
            ssbank_i[0] += 1
            return b
        if P.side_kind == 'B':
            b = banks[4 + sbank_i[0] % 2]
            sbank_i[0] += 1
            return b
        b = banks[bank_i[0] % 4]
        bank_i[0] += 1
        return b

    wmT = P.sbuf("wmT", [128, L, 4, 128], BF16)
    gtok = P.sbuf("gtok", [128, 512], F32)
    wstage = gtok.v(lambda a: a.rearrange("p (g s) -> p g s", s=128))
    for l in range(L):
        P.dma(wstage, w_spatial[l].v(lambda a: a.rearrange("g t s -> t g s")), 'ld')
        b = ps()
        for g in range(4):
            P.tr(b[:, g * 128:(g + 1) * 128], wstage[:, g, :], ident)
        for g in range(4):
            P.tt('dve', wmT[:, l, g, :], b[:, g * 128:(g + 1) * 128], maskc, ALU.mult)

    NSLOT = 9
    wslots = [P.sbuf("wslot%d" % i, [128, 8 * 256], BF16) for i in range(NSLOT)]
    wlist = []

    def wsrc(src2d, kcs, c0, ncols):
        return src2d.v(lambda a: a.rearrange("(kc p) c -> p kc c", p=128)[:, :, c0:c0 + ncols]), kcs, ncols

    tiles = [('s', 0)] + [('p', i) for i in range(NT)]
    for (kind, ti) in tiles[1:]:
        for l in range(L):
            wlist.append(wsrc(w_in[l], 8, 1536, 128))
            for c in range(4):
                for x in range(3):
                    wlist.append(wsrc(w_in[l], 8, x * 512 + c * 128, 128))
            for hf in range(2):
                wlist.append(wsrc(w_in[l], 8, 2688 + hf * 256, 256))
            for hf in range(2):
                wlist.append(wsrc(w_in[l], 8, 2176 + hf * 256, 256))
            for hf in range(2):
                wlist.append(wsrc(w_in[l], 8, 3200 + hf * 256, 256))
            for hf in range(2):
                wlist.append(wsrc(w_in[l], 8, 1664 + hf * 256, 256))
            for jg in range(4):
                wlist.append(wsrc(w_in[l], 8, 3712 + jg * 256, 256))
                wlist.append(wsrc(w_in[l], 8, 4736 + jg * 256, 256))
                wlist.append(wsrc(w_br_a[l], 4, jg * 256, 256))
                wlist.append(wsrc(w_br_b[l], 4, jg * 256, 256))
            for jg in range(4):
                wlist.append(wsrc(w_out[l], 8, jg * 256, 256))
            for jg in range(4):
                wlist.append(wsrc(w_ple[l], 2, jg * 256, 256))
                wlist.append(wsrc(w_ple_gate[l], 8, jg * 256, 256))
    wcur = [0]
    wissued = [0]
    wdone = set()

    def w_issue_ready():
        while wissued[0] < len(wlist):
            i = wissued[0]
            if i >= NSLOT and (i - NSLOT) not in wdone:
                return
            src, kcs, ncols = wlist[i]
            slot = wslots[i % NSLOT]
            dst = slot[:, 0:kcs * ncols].v(lambda a: a.rearrange("p (kc c) -> p kc c", c=ncols))
            P.dma(dst, src, 'ws%d' % (i % NSLOT), eng='pool', force_main=True)
            wissued[0] += 1

    class WS:
        pass

    s_cur = [0]
    P.main_cur = lambda: wcur[0]

    def w_get():
        if P.side_kind == 'S':
            i = s_cur[0]
            s_cur[0] += 1
            src, kcs, ncols = wlist[i]
            slot = wslots[i % NSLOT]
            t = slot[:, 0:kcs * ncols].v(lambda a: a.rearrange("p (kc c) -> p kc c", c=ncols))
            t.widx = i
            P.marker(('get', i))
            return t
        i = wcur[0]
        w_issue_ready()
        assert wissued[0] > i, "weight slot pool too small"
        src, kcs, ncols = wlist[i]
        slot = wslots[i % NSLOT]
        wcur[0] += 1
        t = slot[:, 0:kcs * ncols].v(lambda a: a.rearrange("p (kc c) -> p kc c", c=ncols))
        t.widx = i
        return t

    per_tile_slices = len(wlist) // NT

    def w_done(*ts_):
        if P.side_kind == 'S':
            P.marker(('done', [t.widx for t in ts_]))
            return

        def go():
            idxs = set(t.widx for t in ts_)
            if P.auto is not None and max(idxs) < per_tile_slices:
                P.pull_s(0, until_done=idxs)
            for t in ts_:
                wdone.add(t.widx)
            w_issue_ready()
        P.action(go)

    hT = [P.sbuf("hT%d" % k, [128, TT], F32) for k in range(8)]
    xnT = [P.sbuf("xnT%d" % k, [128, TT], BF16) for k in range(8)]
    sqb = [P.sbuf("sqb0", [128, TT], BF16)] * 2
    rstd = P.sbuf("rstd", [128, TT], F32)
    pT = [P.sbuf("pT%d" % k, [128, TT], BF16) for k in range(2)]
    xtok = [P.sbuf("xtok%d" % i, [128, D], F32) for i in range(2)]
    zraw = [P.sbuf("zraw0", [128, TT + 1], F32)] * 2
    zraw_i = [0]
    carry = [P.sbuf("carry%d" % l, [128, 13], F32) for l in range(L)]
    zpT = P.sbuf("zpT", [128, 13, NS], F32)
    twb = P.sbuf("twb", [64, TT], BF16)
    adb = P.sbuf("adb", [128, TT], BF16)
    zr = P.sbuf("zr", [128, TT], F32)
    zk = P.sbuf("zk", [128, TT], F32)
    V = [P.sbuf("V%d" % c, [128, TT], BF16) for c in range(4)]
    VF = [P.sbuf("VF%d" % c, [128, TT], BF16) for c in range(4)]
    xvd = P.sbuf("xvd", [32, TT], BF16)
    tA = [P.sbuf("tA%d" % i, [128, TT], F32) for i in range(8)]
    tB = [P.sbuf("tB%d" % i, [128, TT], BF16) for i in range(2)]
    zsl = tA[6]
    ar1 = P.es.enter_context(nc.sbuf_tensor("arena1", [128, 4096], F32))
    bufAR, bufBK = Buf("AR"), Buf("BK")
    ARh = ar1[:, 0:2048].bitcast(BF16)
    BKh = ar1[:, 2048:4096].bitcast(BF16)
    c4v = lambda a: a.rearrange("p (j a b) -> p j a b", a=2, b=C)
    AR = [T(c4v(ARh[:, c * 2 * TT:(c + 1) * 2 * TT]), bufAR) for c in range(4)]
    BK = [T(c4v(BKh[:, c * 2 * TT:(c + 1) * 2 * TT]), bufBK) for c in range(4)]
    MG = [T(ARh[:, k * TT:(k + 1) * TT], bufAR) for k in range(8)]
    YAG = [T(BKh[:, c * TT:(c + 1) * TT], bufBK) for c in range(4)]
    YBG = P.sbuf("YBG", [128, 4, TT], BF16)
    WC = P.sbuf("WC", [128, 4, NCH], F32)
    BON = [P.sbuf("BON%d" % c, [128, TT], BF16) for c in range(4)]
    YA = P.sbuf("YA", [128, 4, TT], F32)
    USG = P.sbuf("USG", [128, 4, TT], BF16)
    VN = [P.sbuf("VN%d" % i, [128, 512], BF16) for i in range(max(NTB, 1))]
    Hf = [P.sbuf("Hf%d" % l, [128, 4, 64], F32) for l in range(L)]
    Hb = [P.sbuf("Hb%d" % l, [128, 4, 64], BF16) for l in range(L)]
    ar2 = P.es.enter_context(nc.sbuf_tensor("arena2", [128, 4096], F32))
    a2off = [0]
    a2bufs = []

    def a2(name, words, dt, shape_fn=None):
        o = a2off[0]
        a2off[0] += words
        assert a2off[0] <= 4096
        ap = ar2[:, o:o + words]
        if dt == BF16:
            ap = ap.bitcast(BF16)
        if shape_fn is not None:
            ap = shape_fn(ap)
        b_ = Buf(name)
        a2bufs.append(b_)
        return T(ap, b_)
    h8 = lambda a: a.rearrange("p (h x) -> p h x", h=8)
    g8 = lambda a: a.rearrange("p (h a b) -> p h a b", h=8, a=2)
    Vt = a2("Vt", 256, BF16)
    Bt = a2("Bt", 256, BF16)
    Kt = a2("Kt", 256, BF16)
    G1s = a2("G1s", 512, BF16, g8)
    G2s = a2("G2s", 512, BF16, g8)
    PQ = [a2("PQ%d" % i, 256, BF16, h8) for i in range(6)]
    TtB = a2("TtB", 256, BF16, h8)
    Xs = a2("Xs", 256, BF16)
    Us = a2("Us", 256, BF16)
    SV = P.sbuf("SV", [128, 4, 6, NS], F32)
    TOKV = [P.sbuf("TOKV0", [NS, 512], F32)] * 2
    SVP = P.sbuf("SVP", [128, 6, 64], F32)
    sa = P.sbuf("sa", [128, 64], F32)
    ysm = P.sbuf("ysm", [128, 64], F32)
    ytok = TOKV[0]
    outtok = [xtok[0]]
    small = P.sbuf("small", [128, 8], F32)
    gtok2 = P.sbuf("gtok2", [128, 512], F32)
    vnf = gtok2[0:NS, :]
    vnfm = P.sbuf("vnfm", [128, 4, NS], F32)

    prm = dict(hT=hT, xnT=xnT, sqb=sqb, rstd=rstd, pT=pT, zraw=zraw, zsl=zsl, twb=twb, adb=adb, zr=zr, zk=zk, V=V, VF=VF,
               xvd=xvd, tA=tA, tB=tB, BON=BON, YA=YA, USG=USG, YBG=YBG, YAG=YAG, MG=MG, small=small, gtok=gtok, gtok2=gtok2)
    S_hT = [P.sbuf("S_hT%d" % k, [128, NS], F32) for k in range(8)]
    S_xnT = [P.sbuf("S_xnT%d" % k, [128, NS], BF16) for k in range(8)]
    S_tA = [P.sbuf("S_tA%d" % i, [128, NS], F32) for i in range(8)]
    S_tB = [P.sbuf("S_tB%d" % i, [128, NS], BF16) for i in range(2)]
    smp = dict(hT=S_hT, xnT=S_xnT, sqb=[P.sbuf("S_sqb", [128, NS], BF16)] * 2, rstd=P.sbuf("S_rstd", [128, NS], F32),
               pT=[P.sbuf("S_pT%d" % k, [128, NS], BF16) for k in range(2)],
               zraw=[P.sbuf("S_zraw", [128, NS + 1], F32)] * 2, zsl=S_tA[6],
               twb=P.sbuf("S_twb", [64, NS], BF16), adb=P.sbuf("S_adb", [128, NS], BF16),
               zr=P.sbuf("S_zr", [128, NS], F32), zk=P.sbuf("S_zk", [128, NS], F32),
               V=[P.sbuf("S_V%d" % c, [128, NS], BF16) for c in range(4)],
               VF=[P.sbuf("S_VF%d" % c, [128, NS], BF16) for c in range(4)],
               xvd=P.sbuf("S_xvd", [32, NS], BF16), tA=S_tA, tB=S_tB,
               BON=[P.sbuf("S_BON%d" % c, [128, NS], BF16) for c in range(4)],
               YA=P.sbuf("S_YA", [128, 4, NS], F32), USG=P.sbuf("S_USG", [128, 4, NS], BF16),
               YBG=P.sbuf("S_YBG", [128, 4, NS], BF16),
               YAG=[P.sbuf("S_YAG%d" % c, [128, NS], BF16) for c in range(4)],
               MG=[P.sbuf("S_MG%d" % k, [128, NS], BF16) for k in range(8)],
               small=P.sbuf("S_small", [128, 8], F32), gtok=TOKV[0], gtok2=P.sbuf("S_gtok2", [NS, 512], F32))
    P.memset('pool', smp['small'], 0.0)
    NPART = 8
    xt1 = xtok[1].ap
    Sst_p = T(xt1[:, 0:512].rearrange("p (i j) -> p i j", j=64), Buf("Sst_p"))
    Stmp_p = T(xt1[:, 512:1024].rearrange("p (i j) -> p i j", j=64), Buf("Stmp_p"))
    for l in range(L):
        P.memset('pool', Hf[l], 0.0)
        P.memset('pool', Hb[l], 0.0)
        P.memset('pool', carry[l], 0.0)

    def bc_last(t, shape):
        return t.v(lambda a: a.to_broadcast(shape))

    for (kind, ti) in tiles:
        samp = (kind == 's')
        bs_ = smp if samp else prm
        hT, xnT, sqb, rstd, pT, zraw, zsl, twb, adb, zr, zk, V, VF = (bs_[k_] for k_ in (
            'hT', 'xnT', 'sqb', 'rstd', 'pT', 'zraw', 'zsl', 'twb', 'adb', 'zr', 'zk', 'V', 'VF'))
        xvd, tA, tB, BON, YA, USG, YBG, YAG, MG, small, gtok, gtok2 = (bs_[k_] for k_ in (
            'xvd', 'tA', 'tB', 'BON', 'YA', 'USG', 'YBG', 'YAG', 'MG', 'small', 'gtok', 'gtok2'))
        vnf = gtok2[0:NS, :]
        SW = 512 if samp else 1024
        xst = TOKV[0] if samp else xtok[0]
        if samp:
            P.begin_side('S')
        n = NS if samp else TT
        t0 = ti * TT
        ntb = 1 if samp else NTB
        rows = NS if samp else 128

        for tb in range(ntb):
            for pc in range(D // SW):
                src = xs if samp else xp[t0 + tb * 128: t0 + (tb + 1) * 128, :]
                P.dma(xst[0:rows, 0:SW], src[:, pc * SW:(pc + 1) * SW], 'ld')
                for half in range(SW // 512):
                    b = ps()
                    for kk_ in range(4):
                        P.tr(b[:, kk_ * 128: kk_ * 128 + rows], xst[0:rows, (half * 4 + kk_) * 128:(half * 4 + kk_ + 1) * 128],
                             ident[0:rows, 0:rows])
                    for kk_ in range(4):
                        kc = pc * (SW // 128) + half * 4 + kk_
                        P.copy('dve', hT[kc][:, tb * 128: tb * 128 + rows], b[:, kk_ * 128: kk_ * 128 + rows])
        if samp:
            pass

        def rmsnorm(gv, outs, f32out=False):
            b = ps()
            for kc in range(8):
                sq = sqb[kc % 2]
                P.act(sq[:, 0:n], hT[kc][:, 0:n], AF.Square)
                P.mm(b[:, 0:n], onesb, sq[:, 0:n], start=(kc == 0), stop=(kc == 7))
            P.rsq(rstd[:, 0:n], b[:, 0:n], D * EPS)
            for kc in range(8):
                P.stt('dve', outs[kc][:, 0:n], hT[kc][:, 0:n], gv[:, kc:kc + 1], rstd[:, 0:n], ALU.mult, ALU.mult)

        def proj(wt, col0, ncol_chunks=1, M=128):
            b = ps()
            for kc in range(8):
                P.mm(b[0:M, 0:n], wt[:, kc, col0:col0 + M], xnT[kc][:, 0:n], start=(kc == 0), stop=(kc == 7))
            return b

        for l in range(L):
            rmsnorm(g32[:, l, :], xnT)

            def shifted(b, f, dst):
                zz = zraw[zraw_i[0] % 2]
                zraw_i[0] += 1
                P.copy('act', zz[:, 1:n + 1], b[:, 0:n])
                if samp:
                    prev = zpT[:, f, :]
                    P.dma(oshift_s[l].v(lambda a: a[:, f * 128:(f + 1) * 128].rearrange("b p -> p b")), zz[:, 1:n + 1], 'st', slow=True)
                else:
                    P.copy('pool', zz[:, 0:1], carry[l][:, f:f + 1])
                    prev = zz[:, 0:n]
                    P.copy('pool', carry[l][:, f:f + 1], zz[:, n:n + 1])
                d = tA[7]
                P.tt('pool', d[:, 0:n], prev, zz[:, 1:n + 1], ALU.subtract)
                P.stt('dve', dst[:, 0:n], d[:, 0:n], mu[:, l, f:f + 1], zz[:, 1:n + 1], ALU.mult, ALU.add)

            if samp:
                for f0 in range(0, 13, 4):
                    fs = list(range(f0, min(13, f0 + 4)))
                    P.dma(xst[0:NS, 0:len(fs) * 128], sshift[l][:, f0 * 128:(f0 + len(fs)) * 128], 'ld')
                    b = ps()
                    for i, f in enumerate(fs):
                        P.tr(b[:, i * 128: i * 128 + NS], xst[0:NS, i * 128:(i + 1) * 128], ident[0:NS, 0:NS])
                    for i, f in enumerate(fs):
                        P.copy('dve', zpT[:, f, :], b[:, i * 128: i * 128 + NS])

            wt = w_get()
            b = proj(wt, 0)
            shifted(b, 12, zsl)
            w_done(wt)
            P.act(twb[0:64, 0:n], zsl[0:64, 0:n], AF.Tanh)
            P.copy('act', adb[64:128, 0:n], zsl[64:128, 0:n])
            if l == 1:
                b = ps()
                for kc in range(8):
                    P.mm(b[0:32, 0:n], vdb[:, kc, :], xnT[kc][:, 0:n], start=(kc == 0), stop=(kc == 7))
                P.copy('act', xvd[0:32, 0:n], b[0:32, 0:n])

            for c in range(4):
                wr = w_get()
                b = proj(wr, 0)
                shifted(b, c, zr)
                w_done(wr)
                wk = w_get()
                b = proj(wk, 0)
                shifted(b, 4 + c, zk)
                w_done(wk)
                wv = w_get()
                b = proj(wv, 0)
                shifted(b, 8 + c, V[c])
                w_done(wv)
                lw, av, kk, t1, kmod, bbv, cs, ex = tA[0], tA[1], tA[2], tA[3], tA[4], tA[5], tA[6], tA[7]
                b = ps()
                P.mm(b[:, 0:n], lup[0:64, l, c * 128:(c + 1) * 128], twb[0:64, 0:n])
                P.act(lw[:, 0:n], b[:, 0:n], AF.Tanh, bias=hw0[:, l, c:c + 1], scale=0.5)
                P.ts('dve', lw[:, 0:n], lw[:, 0:n], LOGW_C, LOGW_C, ALU.mult, ALU.add)
                b = ps()
                P.mm(b[:, 0:n], lup[64:128, l, c * 128:(c + 1) * 128], adb[64:128, 0:n])
                P.act(av[:, 0:n], b[:, 0:n], AF.Tanh, bias=ha0[:, l, c:c + 1], scale=0.5)
                P.ts('dve', av[:, 0:n], av[:, 0:n], 0.5, 0.5, ALU.mult, ALU.add)
                if l == 0:
                    P.copy('act', VF[c][:, 0:n], V[c][:, 0:n])
                else:
                    b = ps()
                    P.mm(b[:, 0:n], vub[0:32, c * 128:(c + 1) * 128], xvd[0:32, 0:n])
                    sg = tA[6]
                    P.act(sg[:, 0:n], b[:, 0:n], AF.Tanh, bias=hvb[:, 0, c:c + 1], scale=0.5)
                    P.ts('dve', sg[:, 0:n], sg[:, 0:n], 0.5, 0.5, ALU.mult, ALU.add)
                    dd = tA[7]
                    P.tt('pool', dd[:, 0:n], VF[c][:, 0:n], V[c][:, 0:n], ALU.subtract)
                    P.tt('dve', dd[:, 0:n], dd[:, 0:n], sg[:, 0:n], ALU.mult)
                    P.tt('pool', V[c][:, 0:n], V[c][:, 0:n], dd[:, 0:n], ALU.add)
                sq = tB[0]
                P.act(sq[:, 0:n], zk[:, 0:n], AF.Square, scale=kkv[:, l, c:c + 1])
                b = ps()
                P.mm(b[:, 0:n], blkb, sq[:, 0:n])
                rn = tA[3]
                P.rsq(rn[:, 0:n], b[:, 0:n], 1e-24)
                P.stt('dve', kk[:, 0:n], zk[:, 0:n], kkv[:, l, c:c + 1], rn[:, 0:n], ALU.mult, ALU.mult)
                P.act(t1[:, 0:n], av[:, 0:n], AF.Identity, bias=omka[:, l, c:c + 1], scale=kav[:, l, c:c + 1])
                P.tt('pool', kmod[:, 0:n], zk[:, 0:n], t1[:, 0:n], ALU.mult)
                P.tt('pool', bbv[:, 0:n], kk[:, 0:n], av[:, 0:n], ALU.mult)
                rkr = tB[1]
                P.stt('dve', rkr[:, 0:n], zr[:, 0:n], rkv[:, l, c:c + 1], kmod[:, 0:n], ALU.mult, ALU.mult)
                b = ps()
                P.mm(b[:, 0:n], blkb, rkr[:, 0:n])
                P.tt('dve', BON[c][:, 0:n], b[:, 0:n], V[c][:, 0:n], ALU.mult)
                if not samp:
                    P.add('dve', lambda E, cs=cs, lw=lw: E.tensor_tensor_scan(out=cs.ap, data0=maskscan.ap, data1=lw.ap, initial=0.0,
                                                                              op0=ALU.mult, op1=ALU.add), [maskscan, lw], [cs])
                    P.tt('dve', ex, cs, lw, ALU.subtract)
                    epos, eneg, eex = tA[3], tA[1], tA[7]
                    P.act(epos, cs, AF.Exp)
                    P.act(eex, ex, AF.Exp)
                    P.act(eneg, cs, AF.Exp, scale=-1.0)
                    c3 = lambda a: a.rearrange("p (c t) -> p c t", t=C)
                    P.tt('dve', AR[c][:, :, 1, :], zr.v(c3), epos.v(c3), ALU.mult)
                    P.stt('dve', AR[c][:, :, 0, :], kk.v(c3), -1.0, eex.v(c3), ALU.mult, ALU.mult)
                    P.tt('dve', BK[c][:, :, 1, :], kmod.v(c3), eneg.v(c3), ALU.mult)
                    P.tt('pool', BK[c][:, :, 0, :], bbv.v(c3), eneg.v(c3), ALU.mult)
                    P.copy('act', WC[:, c, :], epos.v(c3)[:, :, C - 1])
                else:
                    P.copy('pool', SV[:, c, 0, :], zr[:, 0:n])
                    P.act(SV[:, c, 1, :], lw[:, 0:n], AF.Exp)
                    P.copy('pool', SV[:, c, 2, :], kmod[:, 0:n])
                    P.copy('pool', SV[:, c, 3, :], V[c][:, 0:n])
                    P.ts('pool', SV[:, c, 4, :], kk[:, 0:n], -1.0, None, ALU.mult)
                    P.copy('pool', SV[:, c, 5, :], bbv[:, 0:n])

            wv_ = {}

            def vb_blocks(tbs):
                for tb in tbs:
                    b = ps()
                    for hf, wvb in enumerate((wv_['0'], wv_['1'])):
                        for kc in range(8):
                            P.mm(b[0:rows, hf * 256:(hf + 1) * 256], xnT[kc][:, tb * 128: tb * 128 + rows], wvb[:, kc, :],
                                 start=(kc == 0), stop=(kc == 7))
                    g = gtok
                    P.act(g[0:rows, :], b[0:rows, :], AF.Gelu_apprx_tanh)
                    P.add('dve', lambda E, g=g, rows=rows, small=small: E.reduce_sum(out=small[0:rows, 0:1].ap, in_=g[0:rows, :].ap, axis=AX.X), [g], [small])
                    P.act(gtok2[0:rows, :], g[0:rows, :], AF.Square)
                    P.add('dve', lambda E, rows=rows, small=small, gtok2=gtok2: E.reduce_sum(out=small[0:rows, 1:2].ap, in_=gtok2[0:rows, :].ap, axis=AX.X), [gtok2], [small])
                    P.ts('dve', small[0:rows, 2:3], small[0:rows, 0:1], 1.0 / 512, None, ALU.mult)
                    P.tt('dve', small[0:rows, 3:4], small[0:rows, 2:3], small[0:rows, 2:3], ALU.mult)
                    P.stt('dve', small[0:rows, 4:5], small[0:rows, 1:2], 1.0 / 512, small[0:rows, 3:4], ALU.mult, ALU.subtract)
                    P.rsq(small[0:rows, 5:6], small[0:rows, 4:5], 1e-5)
                    P.ts('dve', g[0:rows, :], g[0:rows, :], small[0:rows, 2:3], small[0:rows, 5:6], ALU.subtract, ALU.mult)
                    P.tt('pool', g[0:rows, :], g[0:rows, :], lngb[0:rows, l, :], ALU.mult)
                    if samp:
                        P.tt('pool', vnf, g[0:rows, :], lnbb[0:rows, l, :], ALU.add)
                        P.dma(ocv[l], vnf, 'st')
                        b = ps()
                        for g4 in range(4):
                            P.tr(b[:, g4 * 128: g4 * 128 + NS], vnf[:, g4 * 128:(g4 + 1) * 128], ident[0:NS, 0:NS])
                        for g4 in range(4):
                            P.copy('dve', vnfm[:, g4, :], b[:, g4 * 128: g4 * 128 + NS])
                    else:
                        P.tt('pool', VN[tb], g, lnbb[:, l, :], ALU.add)

            def partB0():
                wv_['0'] = w_get()
                wv_['1'] = w_get()
                vb_blocks(range(0, (ntb + 1) // 2))

            def partB1():
                vb_blocks(range((ntb + 1) // 2, ntb))
                w_done(wv_['0'], wv_['1'])

            def partB2():
                for hf in range(2):
                    wu = w_get()
                    for cc in range(2):
                        g4 = hf * 2 + cc
                        b = proj(wu, cc * 128)
                        P.act(USG[:, g4, 0:n], b[:, 0:n], AF.Gelu_apprx_tanh)
                    w_done(wu)

            def partB3():
                wgb = None
                for g4 in range(4):
                    if g4 % 2 == 0:
                        wgb = w_get()
                    b = proj(wgb, (g4 % 2) * 128)
                    sg = tB[g4 % 2]
                    P.act(sg[:, 0:n], b[:, 0:n], AF.Silu)
                    P.tt('dve' if g4 % 2 else 'pool', USG[:, g4, 0:n], USG[:, g4, 0:n], sg[:, 0:n], ALU.mult)
                    if g4 % 2 == 1:
                        w_done(wgb)
                if samp:
                    for g4 in range(4):
                        mx = tA[0][:, 0:n]
                        P.ts('dve', mx, vnfm[:, g4, :], ws00[:, l, g4:g4 + 1], bs0[:, l, g4:g4 + 1], ALU.mult, ALU.add)
                        P.tt('dve', YBG[:, g4, 0:n], mx, USG[:, g4, 0:n], ALU.mult)
                else:
                    for tb in range(ntb):
                        b = ps()
                        for g4 in range(4):
                            o = b[:, g4 * 128:(g4 + 1) * 128]
                            P.mm(o, VN[tb][:, g4 * 128:(g4 + 1) * 128], wmT[:, l, g4, :], start=True, stop=False)
                            P.mm(o, onesb[0:1, :], bsb[0:1, (l * 4 + g4) * 128:(l * 4 + g4 + 1) * 128], start=False, stop=True)
                        tsl = slice(tb * 128, (tb + 1) * 128)
                        P.tt('dve', YBG[:, :, tsl], b.v(lambda a: a.rearrange("p (g t) -> p g t", t=128)), USG[:, :, tsl], ALU.mult)
            partsB = [partB0, partB1, partB2, partB3]

            if not samp:
                h3 = lambda a: a.rearrange("p (h x) -> p h x", x=64)
                c4 = lambda a: a.rearrange("p (c x) -> p c x", x=64)
                cq = lambda a: a.rearrange("p (c q) a b -> p c q (a b)", q=2)
                cq3 = lambda a: a.rearrange("p (c q) x -> p c q x", q=2)
                xq = lambda a: a.rearrange("p (c q x) -> p c q x", q=2, x=64)
                P.begin_side()
                for fB in partsB:
                    fB()
                sideB = P.end_side()
                nB = sum(1 for it in sideB if isinstance(it, Op))
                npull = (NCH // 2) * 60
                kB = max(1, -(-nB // npull))
                pullB = lambda: P.pull(sideB, kB)
                for jp in range(NCH // 2):
                    tsl = slice(jp * 128, (jp + 1) * 128)
                    b = ps()
                    for c in range(4):
                        P.mm(b[:, c * 128:(c + 1) * 128], V[c][:, tsl], identb)
                    P.copy('act', Vt, b)
                    pullB()
                    bb1 = ps()
                    bb2 = ps()
                    for c in range(4):
                        for jj in range(2):
                            po = 64 * jj
                            P.mm(bb1[po:po + 64, c * 128:(c + 1) * 128], BK[c][:, 2 * jp + jj, 0, :], identb, tp=(0, po))
                            P.mm(bb2[po:po + 64, c * 128:(c + 1) * 128], BK[c][:, 2 * jp + jj, 1, :], identb, tp=(0, po))
                    P.copy('dve', Bt, bb1)
                    pullB()
                    P.copy('act', Kt, bb2)
                    pullB()
                    for q in range(2):
                        qs = slice(q * 64, (q + 1) * 64)
                        b1, b2 = ps(), ps()
                        for c in range(4):
                            for jj in range(2):
                                po = 64 * jj
                                j = 2 * jp + jj
                                arv = AR[c][qs, j, :, :].v(lambda a: a.rearrange("p a b -> p (a b)"))
                                P.mm(b1[po:po + 64, c * 128:(c + 1) * 128], BK[c][qs, j, 0, :], arv, tp=(q * 64, po))
                                P.mm(b2[po:po + 64, c * 128:(c + 1) * 128], BK[c][qs, j, 1, :], arv, tp=(q * 64, po))
                        i1 = b1.v(lambda a: a.rearrange("p (h x) -> p h x", x=128))
                        i2 = b2.v(lambda a: a.rearrange("p (h x) -> p h x", x=128))
                        P.tt('dve', G1s.v(cq)[:, :, q, :], i1, mask_si, ALU.mult)
                        pullB()
                        P.tt('dve', G2s.v(cq)[:, :, q, :], i2, mask_si, ALU.mult)
                        pullB()
                    Pk, Qk = PQ[0], PQ[1]
                    for q in range(2):
                        qs = slice(q * 64, (q + 1) * 64)
                        b3 = ps()
                        for c in range(4):
                            for jj in range(2):
                                po = 64 * jj
                                j = 2 * jp + jj
                                P.mm(b3[po:po + 64, c * 64:(c + 1) * 64], AR[c][qs, j, 0, :], BK[c][qs, j, 0, :], tp=(q * 64, po))
                        P.tt('dve', Pk.v(cq3)[:, :, q, :], b3[:, 0:256].v(c4), mask_l4, ALU.mult)
                        pullB()
                    P.copy('pool', Qk, G1s[:, :, 0, :])
                    pullB()
                    P.tt('pool', TtB, G1s[:, :, 0, :], ident8, ALU.add)
                    pullB()
                    free = [PQ[2], PQ[3], PQ[4], PQ[5]]
                    for k in range(6):
                        if k >= 1:
                            b = ps()
                            for h in range(8):
                                for po in (0, 64):
                                    P.mm(b[po:po + 64, h * 64:(h + 1) * 64], Pk[po:po + 64, h, :], TtB[po:po + 64, h, :], tp=(po, po))
                            P.tt('dve', TtB, b.v(h3), TtB, ALU.add)
                            pullB()
                        Pn = Qn = None
                        if k <= 4:
                            b = ps()
                            for h in range(8):
                                for po in (0, 64):
                                    P.mm(b[po:po + 64, h * 64:(h + 1) * 64], Qk[po:po + 64, h, :], Pk[po:po + 64, h, :], tp=(po, po))
                            Pn = free.pop(0)
                            P.copy('act', Pn, b.v(h3))
                            pullB()
                        if k <= 3:
                            b = ps()
                            for h in range(8):
                                for po in (0, 64):
                                    P.mm(b[po:po + 64, h * 64:(h + 1) * 64], Pk[po:po + 64, h, :], Qk[po:po + 64, h, :], tp=(po, po))
                            Qn = free.pop(0)
                            P.copy('dve', Qn, b.v(h3))
                            pullB()
                        free.append(Pk)
                        free.append(Qk)
                        Pk, Qk = Pn, Qn
                    for jj in range(2):
                        po = 64 * jj
                        pp_ = slice(po, po + 64)
                        j = 2 * jp + jj
                        js = slice(j * C, (j + 1) * C)
                        bx2 = ps()
                        for h in range(8):
                            P.mm(bx2[pp_, h * 64:(h + 1) * 64], G2s[pp_, h, 0, :], Vt[pp_, h * 64:(h + 1) * 64], tp=(po, po))
                        P.copy('act', Xs[pp_, :], bx2[pp_, :])
                        pullB()
                        for q in range(2):
                            qs = slice(q * 64, (q + 1) * 64)
                            bxq = ps()
                            for c in range(4):
                                P.mm(bxq[pp_, c * 64:(c + 1) * 64], AR[c][qs, j, 0, :], Hb[l][qs, c, :], tp=(q * 64, po))
                            xv = Xs[pp_, :].v(xq)[:, :, q, :]
                            P.tt('dve', xv, bxq[pp_, 0:256].v(c4), xv, ALU.add)
                            pullB()
                        bu = ps()
                        for h in range(8):
                            P.mm(bu[pp_, h * 64:(h + 1) * 64], TtB[pp_, h, :], Xs[pp_, h * 64:(h + 1) * 64], tp=(po, po))
                        P.copy('dve', Us[pp_, :], bu[pp_, :])
                        pullB()
                        by2 = ps()
                        for h in range(8):
                            c, q = h // 2, h % 2
                            qs = slice(q * 64, (q + 1) * 64)
                            hs = slice(h * 64, (h + 1) * 64)
                            o = by2[qs, c * 64:(c + 1) * 64]
                            P.mm(o, Us[pp_, hs], G1s[pp_, h, 1, :], start=True, stop=False, tp=(po, q * 64))
                            P.mm(o, Vt[pp_, hs], G2s[pp_, h, 1, :], start=False, stop=True, tp=(po, q * 64))
                        P.copy('act', YA[:, :, js], by2[:, 0:256].v(c4))
                        pullB()
                        for q in range(2):
                            qs = slice(q * 64, (q + 1) * 64)
                            byq = ps()
                            for c in range(4):
                                P.mm(byq[qs, c * 64:(c + 1) * 64], Hb[l][qs, c, :], AR[c][qs, j, 1, :], tp=(q * 64, q * 64))
                            P.tt('dve', YA[qs, :, js], byq[qs, 0:256].v(c4), YA[qs, :, js], ALU.add)
                            pullB()
                        bh = ps()
                        for h in range(8):
                            c, q = h // 2, h % 2
                            qs = slice(q * 64, (q + 1) * 64)
                            hs = slice(h * 64, (h + 1) * 64)
                            o = bh[qs, c * 64:(c + 1) * 64]
                            P.mm(o, Bt[pp_, hs], Us[pp_, hs], start=True, stop=False, tp=(po, q * 64))
                            P.mm(o, Kt[pp_, hs], Vt[pp_, hs], start=False, stop=True, tp=(po, q * 64))
                        P.tt('dve', Hf[l], bh[:, 0:256].v(c4), Hf[l], ALU.add)
                        P.tt('dve', Hb[l], Hf[l], bc_last(WC[:, :, j:j + 1], [128, 4, 64]), ALU.mult)
                        pullB()
                        P.tt('pool', Hf[l], Hf[l], bc_last(WC[:, :, j:j + 1], [128, 4, 64]), ALU.mult)
                        pullB()
                        pullB()
                P.pull(sideB, 10 ** 9)
                if ti == NT - 1:
                    for c in range(4):
                        b = ps()
                        P.tr(b[0:64, 0:128], Hf[l][:, c, :], ident)
                        st = tA[0]
                        P.copy('dve', st[0:64, 0:128], b[0:64, 0:128])
                        P.dma(owkv_p[l, 2 * c:2 * c + 2].v(lambda a: a.rearrange("q v k -> v q k")),
                              st[0:64, 0:128].v(lambda a: a.rearrange("p (q k) -> p q k", q=2)), 'st')
                    P.dma(oshift_p[l:l + 1, :].v(lambda a: a.rearrange("o (f p) -> p (o f)", p=128)), carry[l], 'st', slow=True)
            else:
                for fB in partsB:
                    fB()
                for x in range(6):
                    b = ps()
                    for c in range(4):
                        P.tr(b[0:NS, c * 128:(c + 1) * 128], SV[:, c, x, :], ident)
                    P.copy('act' if x % 2 else 'dve', TOKV[x % 2], b[0:NS, :])
                    P.dma(scr1[l][:, :, x, :], TOKV[x % 2].v(lambda a: a.rearrange("b (h j) -> b h j", j=64)), 'sc')
                P.dma(SVP, scr1[l].v(lambda a: a.rearrange("b h x j -> (b h) x j")), 'sc')
                bi = lambda x: SVP[:, x:x + 1, :].v(lambda a: a.to_broadcast([128, 8, 64]))
                bj = lambda t: t.v(lambda a: a.unsqueeze(2).to_broadcast([128, 8, 64]))
                sw_in = swkv[l].v(lambda a: a.rearrange("b h i j -> (b h) i j"))
                sw_out = owkv_s[l].v(lambda a: a.rearrange("b h i j -> (b h) i j"))
                for pt in range(NPART):
                    isl = slice(pt * 8, pt * 8 + 8)
                    P.dma(Sst_p, sw_in[:, isl, :], 'ld')
                    P.tt('dve', Stmp_p, Sst_p, bi(4), ALU.mult)
                    P.add('dve', lambda E, isl=isl: E.reduce_sum(out=sa[:, isl].ap, in_=Stmp_p.ap, axis=AX.X), [Stmp_p], [sa])
                    P.tt('pool', Sst_p, Sst_p, bi(1), ALU.mult)
                    P.tt('dve', Stmp_p, bj(sa[:, isl]), bi(5), ALU.mult)
                    P.tt('pool', Sst_p, Sst_p, Stmp_p, ALU.add)
                    P.tt('dve', Stmp_p, bj(SVP[:, 3, isl]), bi(2), ALU.mult)
                    P.tt('pool', Sst_p, Sst_p, Stmp_p, ALU.add)
                    P.dma(sw_out[:, isl, :], Sst_p, 'st')
                    P.tt('dve', Stmp_p, Sst_p, bi(0), ALU.mult)
                    P.add('dve', lambda E, isl=isl: E.reduce_sum(out=ysm[:, isl].ap, in_=Stmp_p.ap, axis=AX.X), [Stmp_p], [ysm])
                P.dma(scr2[l].v(lambda a: a.rearrange("b (h i) -> (b h) i", i=64)), ysm, 'sc')
                P.dma(ytok, scr2[l], 'sc')
                b = ps()
                for c in range(4):
                    P.tr(b[:, c * 128: c * 128 + NS], ytok[:, c * 128:(c + 1) * 128], ident[0:NS, 0:NS])
                for c in range(4):
                    P.copy('dve', YA[:, c, 0:n], b[:, c * 128: c * 128 + NS])

            if not samp and l == 0:
                dbg("YA", YA)
                dbg("V0", V[0], BF16)
                dbg("BON0", BON[0])
            for hf in range(2):
                wga = w_get()
                for cc in range(2):
                    c = hf * 2 + cc
                    b = proj(wga, cc * 128)
                    P.act(YAG[c][:, 0:n], b[:, 0:n], AF.Silu)
                w_done(wga)
            for c in range(4):
                y = YA[:, c, 0:n]
                ybf, ysq = tB[0], tB[1]
                P.copy('act', ybf[:, 0:n], y)
                P.act(ysq[:, 0:n], y, AF.Square)
                b1, b2 = ps(), ps()
                P.mm(b1[:, 0:n], blkb, ybf[:, 0:n])
                P.mm(b2[:, 0:n], blkb, ysq[:, 0:n])
                m, msq, var, yc = tA[0], tA[1], tA[2], tA[3]
                P.ts('dve', m[:, 0:n], b1[:, 0:n], 1.0 / 64, None, ALU.mult)
                P.tt('pool', msq[:, 0:n], m[:, 0:n], m[:, 0:n], ALU.mult)
                P.stt('dve', var[:, 0:n], b2[:, 0:n], 1.0 / 64, msq[:, 0:n], ALU.mult, ALU.subtract)
                P.rsq(var[:, 0:n], var[:, 0:n], GN_EPS)
                P.tt('pool', yc[:, 0:n], y, m[:, 0:n], ALU.subtract)
                P.tt('dve', yc[:, 0:n], yc[:, 0:n], var[:, 0:n], ALU.mult)
                P.act(yc[:, 0:n], yc[:, 0:n], AF.Identity, bias=gnb[:, l, c:c + 1], scale=gng[:, l, c:c + 1])
                P.tt('pool', yc[:, 0:n], yc[:, 0:n], BON[c][:, 0:n], ALU.add)
                P.tt('dve', YAG[c][:, 0:n], yc[:, 0:n], YAG[c][:, 0:n], ALU.mult)

            if not samp and l == 0:
                for c in range(4):
                    dbg("YAG%d" % c, YAG[c], BF16)
                dbg("YBG", YBG, BF16)
                dbg("USG", USG)
            for jg in range(4):
                wma = w_get()
                wmb = w_get()
                wba = w_get()
                wbb = w_get()
                for jj in range(2):
                    j = jg * 2 + jj
                    b = proj(wma, jj * 128)
                    sma = tA[0]
                    P.act(sma[:, 0:n], b[:, 0:n], AF.Sigmoid)
                    b = proj(wmb, jj * 128)
                    smb = tA[1]
                    P.act(smb[:, 0:n], b[:, 0:n], AF.Sigmoid)
                    ba = ps()
                    for c in range(4):
                        P.mm(ba[:, 0:n], wba[:, c, jj * 128:(jj + 1) * 128], YAG[c][:, 0:n], start=(c == 0), stop=(c == 3))
                    bb_ = ps()
                    for c in range(4):
                        P.mm(bb_[:, 0:n], wbb[:, c, jj * 128:(jj + 1) * 128], YBG[:, c, 0:n], start=(c == 0), stop=(c == 3))
                    ma, mb = tA[2], tA[3]
                    P.tt('dve', ma[:, 0:n], ba[:, 0:n], sma[:, 0:n], ALU.mult)
                    P.tt('dve', mb[:, 0:n], bb_[:, 0:n], smb[:, 0:n], ALU.mult)
                    P.tt('pool', MG[j][:, 0:n], ma[:, 0:n], mb[:, 0:n], ALU.add)
                w_done(wma, wmb, wba, wbb)
            for jg in range(4):
                wo = w_get()
                for jj in range(2):
                    j = jg * 2 + jj
                    b = ps()
                    for kc in range(8):
                        P.mm(b[:, 0:n], wo[:, kc, jj * 128:(jj + 1) * 128], MG[kc][:, 0:n], start=(kc == 0), stop=(kc == 7))
                    P.tt('dve', hT[j][:, 0:n], hT[j][:, 0:n], b[:, 0:n], ALU.add)
                w_done(wo)
            if not samp and l == 0:
                for k in range(8):
                    dbg("MG%d" % k, MG[k], BF16)
                    dbg("hmid%d" % k, hT[k])
            for tb in range(ntb):
                xt = xst
                src = psm[l] if samp else pp[l, t0 + tb * 128: t0 + (tb + 1) * 128, :]
                P.dma(xt[0:rows, 0:DPLE], src, 'ld')
                b = ps()
                for kc in range(2):
                    P.tr(b[:, kc * 128: kc * 128 + rows], xt[0:rows, kc * 128:(kc + 1) * 128], ident[0:rows, 0:rows])
                for kc in range(2):
                    P.copy('act', pT[kc][:, tb * 128: tb * 128 + rows], b[:, kc * 128: kc * 128 + rows])
            for kc in range(8):
                P.copy('act', xnT[kc][:, 0:n], hT[kc][:, 0:n])
            for jg in range(4):
                wp = w_get()
                wg = w_get()
                for jj in range(2):
                    j = jg * 2 + jj
                    bp = ps()
                    for kc in range(2):
                        P.mm(bp[:, 0:n], wp[:, kc, jj * 128:(jj + 1) * 128], pT[kc][:, 0:n], start=(kc == 0), stop=(kc == 1))
                    bg = ps()
                    for kc in range(8):
                        P.mm(bg[:, 0:n], wg[:, kc, jj * 128:(jj + 1) * 128], xnT[kc][:, 0:n], start=(kc == 0), stop=(kc == 7))
                    sg = tA[0]
                    P.act(sg[:, 0:n], bg[:, 0:n], AF.Sigmoid, bias=bgv[:, l, j:j + 1], scale=1.0)
                    tp_ = tA[1]
                    P.tt('dve', tp_[:, 0:n], bp[:, 0:n], sg[:, 0:n], ALU.mult)
                    P.tt('pool', hT[j][:, 0:n], hT[j][:, 0:n], tp_[:, 0:n], ALU.add)
                w_done(wp, wg)

        if not samp:
            for k in range(8):
                dbg("hend%d" % k, hT[k])
        yT = tA
        rmsnorm(fg32[:, 0, :], yT)
        for tb in range(ntb):
            dst = ys if samp else yp[t0 + tb * 128: t0 + (tb + 1) * 128, :]
            for half in range(2):
                b = ps()
                for kk_ in range(4):
                    kc = half * 4 + kk_
                    P.tr(b[0:rows, kk_ * 128:(kk_ + 1) * 128], yT[kc][:, tb * 128: tb * 128 + rows], ident)
                if samp:
                    P.copy('act' if half else 'dve', xst[0:rows, 0:512], b[0:rows, :])
                    P.dma(dst[:, half * 512:(half + 1) * 512], xst[0:rows, 0:512], 'st')
                else:
                    P.copy('act' if half else 'dve', xst[0:rows, half * 512:(half + 1) * 512], b[0:rows, :])
            if not samp:
                P.dma(dst, xst[0:rows, :], 'st')
        if samp:
            S_list = P.end_side()
            nS = sum(1 for it in S_list if isinstance(it, Op))
            P.auto = S_list
            P.auto_ratio = float(os.environ.get("K_RATIO", nS / (0.8 * (1800.0 + 8.5 * TT))))
            P.auto_acc = 0.0
        elif ti == 0:
            P.pull_s(10 ** 9)
            assert not P.auto
            P.auto = None

    print('sbuf bytes remaining', nc.sbuf_bytes_remaining)
    P.emit()
    return nc, P


_CACHE = {}


def kernel(**inp):
    NCORE = 8
    f = lambda a: np.ascontiguousarray(np.asarray(a, dtype=np.float32))
    if 'nc' not in _CACHE:
        _CACHE['nc'] = build()[0]
    nc = _CACHE['nc']
    wnames = ["norm_g", "w_in", "shift_mu", "w0", "w_up", "a0", "a_up", "vres_down", "vres_up", "vres_b", "k_k", "k_a",
              "gn_g", "gn_b", "ln_v_g", "ln_v_b", "w_spatial", "b_spatial", "w_br_a", "w_br_b", "w_out", "w_ple",
              "w_ple_gate", "b_ple_gate"]
    shared = {k: f(inp[k]) for k in wnames}
    shared["r_k"] = f(inp["r_k"]).reshape(L, DA)
    shared["final_g"] = f(inp["final_g"]).reshape(1, D)
    in_maps = []
    for c in range(NCORE):
        m = dict(shared)
        sl = slice(16 * c, 16 * c + 16)
        m["xp"] = f(inp["x_prompt"][c])
        m["xs"] = f(inp["x_sample"][sl, 0])
        m["swkv"] = f(inp["state_rwkv_wkv"][:, sl])
        m["sshift"] = f(inp["state_rwkv_shift"][:, sl])
        m["pp"] = f(inp["p_prompt"][:, c])
        m["psm"] = f(inp["p_sample"][:, sl, 0])
        in_maps.append(m)
    res = run_bass_kernel_spmd(nc, in_maps, core_ids=list(range(NCORE))).results
    y_prompt = np.stack([res[c]["yp"] for c in range(NCORE)], 0).astype(np.float32)
    y_sample = np.concatenate([res[c]["ys"] for c in range(NCORE)], 0)[:, None, :].astype(np.float32)
    wkv_prompt = np.stack([res[c]["owkv_p"] for c in range(NCORE)], 1).astype(np.float32)
    shift_prompt = np.stack([res[c]["oshift_p"] for c in range(NCORE)], 1).astype(np.float32)
    wkv_sample = np.concatenate([res[c]["owkv_s"] for c in range(NCORE)], 1).astype(np.float32)
    shift_sample = np.concatenate([res[c]["oshift_s"] for c in range(NCORE)], 1).astype(np.float32)
    chunk_v = np.concatenate([res[c]["ocv"] for c in range(NCORE)], 1)[:, :, None, :].astype(np.float32)
    return (y_prompt, y_sample, wkv_prompt, shift_prompt, wkv_sample, shift_sample, chunk_v)
```
